# Optimizing a Trainium2 kernel written in Bass

```python
import jax, jax.numpy as jnp
from jax import lax
import numpy as np

D_MODEL = 1024
BATCH = 16
SEQ = 2048
DEPTH = 1
DEC_BATCH = 32
DEC_SEQ = 4
PAST_LEN = 16384
PAGE_SIZE = 128

MIX_WIDTH = D_MODEL
A_WIDTH = MIX_WIDTH // 2
B_WIDTH = MIX_WIDTH - A_WIDTH
HGRN_HEAD_DIM = 128
H_A = A_WIDTH // HGRN_HEAD_DIM
HGRN_CHUNK = 64
HD_B = 64
H_B = B_WIDTH // HD_B
ROT_DIM = HD_B // 4
ROPE_THETA = 500000.0
DILATIONS = ((128, 1), (512, 4), (2048, 16))
WIN_MAX = 2048
ATTN_BLOCK = 128
D_FF = 2816
CONV_W = 3
PLE_DIM = 256
EPS = 1e-6
NEG_INF = -1e30
IN_SPLITS = (A_WIDTH, A_WIDTH, A_WIDTH, A_WIDTH, B_WIDTH, B_WIDTH, B_WIDTH)
D_IN = A_WIDTH * 4 + B_WIDTH * 3

kernel_name = "hymba_hgrn2_dilated_swa_convffn_step"


def rmsnorm(x, g):
    x32 = x.astype(jnp.float32)
    y = x32 * lax.rsqrt(jnp.mean(x32 * x32, axis=-1, keepdims=True) + EPS)
    return (y * g.astype(jnp.float32)).astype(x.dtype)


def partial_rotary(x, pos):
    half = ROT_DIM // 2
    inv_freq = jnp.power(ROPE_THETA, -jnp.arange(half, dtype=jnp.float32) * (2.0 / ROT_DIM))
    ang = pos[:, None] * inv_freq[None, :]
    cos = jnp.cos(ang)[None, :, None, :]
    sin = jnp.sin(ang)[None, :, None, :]
    x1 = x[..., :half]
    x2 = x[..., half:ROT_DIM]
    return jnp.concatenate([x1 * cos - x2 * sin, x2 * cos + x1 * sin, x[..., ROT_DIM:]], axis=-1)


def gla_chunked(q, k, v, logf, s0, chunk):
    bsz, t_len, nh, dk = q.shape
    dv = v.shape[-1]
    nc = t_len // chunk

    def to_chunks(t):
        return t.reshape(bsz, nc, chunk, nh, t.shape[-1]).transpose(1, 0, 3, 2, 4)

    causal = jnp.tril(jnp.ones((chunk, chunk), dtype=bool))

    def step(s, xs):
        qc, kc, vc, gc = xs
        b = jnp.cumsum(gc, axis=2)
        o_inter = jnp.einsum('bhtk,bhkv->bhtv', qc * jnp.exp(b), s)
        diff = b[:, :, :, None, :] - b[:, :, None, :, :]
        decay = jnp.exp(jnp.where(causal[:, :, None], diff, -jnp.inf))
        scores = jnp.einsum('bhtk,bhsk,bhtsk->bhts', qc, kc, decay)
        o = o_inter + jnp.einsum('bhts,bhsv->bhtv', scores, vc)
        b_last = b[:, :, -1]
        s = jnp.exp(b_last)[..., None] * s + jnp.einsum(
            'bhsk,bhsv->bhkv', kc * jnp.exp(b_last[:, :, None] - b), vc)
        return s, o

    s_fin, o = lax.scan(step, s0, (to_chunks(q), to_chunks(k), to_chunks(v), to_chunks(logf)))
    o = o.transpose(1, 0, 3, 2, 4).reshape(bsz, t_len, nh, dv)
    return s_fin, o


def combine_dilations(parts):
    ms = jnp.stack([p[0] for p in parts])
    ls = jnp.stack([p[1] for p in parts])
    accs = jnp.stack([p[2] for p in parts])
    w = jnp.exp(ms - ms.max(axis=0, keepdims=True))
    return (w[..., None] * accs).sum(0) / (w * ls).sum(0)[..., None]


def dilated_attn_prompt(q, k, v):
    bsz, s_len, nh, hd = q.shape
    scale = hd ** -0.5
    qi = jnp.arange(ATTN_BLOCK)[:, None]
    kj = jnp.arange(2 * ATTN_BLOCK)[None, :]
    dist = qi - kj + ATTN_BLOCK
    parts = []
    for window, dil in DILATIONS:
        n_back = window // dil
        sub_len = s_len // dil
        nb = -(-sub_len // ATTN_BLOCK)
        sub_pad = nb * ATTN_BLOCK

        def to_sub(t):
            return t.reshape(bsz, sub_len, dil, nh, hd).transpose(0, 2, 1, 3, 4)

        qs = jnp.pad(to_sub(q * scale), ((0, 0), (0, 0), (0, sub_pad - sub_len), (0, 0), (0, 0)))
        qs = qs.reshape(bsz, dil, nb, ATTN_BLOCK, nh, hd)

        def key_blocks(t):
            t = jnp.pad(to_sub(t), ((0, 0), (0, 0), (ATTN_BLOCK, sub_pad - sub_len), (0, 0), (0, 0)))
            t = t.reshape(bsz, dil, nb + 1, ATTN_BLOCK, nh, hd)
            return jnp.concatenate([t[:, :, :-1], t[:, :, 1:]], axis=3)

        kb = key_blocks(k)
        vb = key_blocks(v)
        blk = jnp.arange(nb)[:, None, None]
        valid = (dist >= 0) & (dist <= n_back) & (blk * ATTN_BLOCK - ATTN_BLOCK + kj >= 0)
        s = jnp.einsum('brnqhc,brnkhc->brnhqk', qs, kb)
        s = jnp.where(valid[None, None, :, None], s, NEG_INF)
        m = s.max(axis=-1)
        p = jnp.exp(s - m[..., None])
        l = p.sum(axis=-1)
        acc = jnp.einsum('brnhqk,brnkhc->brnqhc', p, vb)

        def from_sub(t):
            t = t.reshape((bsz, dil, sub_pad) + t.shape[4:])[:, :, :sub_len]
            t = jnp.moveaxis(t, 1, 2)
            return t.reshape((bsz, s_len) + t.shape[3:])

        parts.append((from_sub(jnp.swapaxes(m, 3, 4)), from_sub(jnp.swapaxes(l, 3, 4)), from_sub(acc)))
    return combine_dilations(parts)


def dilated_attn_sample(q, k_all, v_all):
    t_len = q.shape[1]
    wb = k_all.shape[1] - t_len
    scale = q.shape[-1] ** -0.5
    parts = []
    for window, dil in DILATIONS:
        n_back = window // dil
        idx = wb + jnp.arange(t_len)[:, None] - dil * jnp.arange(n_back + 1)[None, :]
        valid = idx >= 0
        idxc = jnp.maximum(idx, 0)
        kg = k_all[:, idxc]
        vg = v_all[:, idxc]
        s = jnp.einsum('bthc,btjhc->bthj', q * scale, kg)
        s = jnp.where(valid[None, :, None, :], s, NEG_INF)
        m = s.max(axis=-1)
        p = jnp.exp(s - m[..., None])
        l = p.sum(axis=-1)
        acc = jnp.einsum('bthj,btjhc->bthc', p, vg)
        parts.append((m, l, acc))
    return combine_dilations(parts)


def layer_step(x, p_emb, pos0, hgrn_s0, win_k, win_v, conv_buf, lb,
               norm_attn_g, w_in, hgrn_onorm_g, w_o, norm_ffn_g, w_gate, w_up,
               conv_w, conv_b, w_down, norm_ple_g, w_ple_gate, w_ple_proj, is_prompt):
    bsz, t_len, _ = x.shape
    dt = x.dtype
    f32 = jnp.float32
    a = rmsnorm(x, norm_attn_g)
    proj = a @ w_in
    pts = [sum(IN_SPLITS[:j]) for j in range(1, len(IN_SPLITS))]
    qa, fa, ia, ga, qb, kb, vb = jnp.split(proj, pts, axis=-1)

    lb_h = lb.reshape(H_A, HGRN_HEAD_DIM)
    q_a = jax.nn.silu(qa.astype(f32).reshape(bsz, t_len, H_A, HGRN_HEAD_DIM)) * HGRN_HEAD_DIM ** -0.5
    forget = lb_h + (1.0 - lb_h) * jax.nn.sigmoid(fa.astype(f32).reshape(bsz, t_len, H_A, HGRN_HEAD_DIM))
    k_a = 1.0 - forget
    v_a = ia.astype(f32).reshape(bsz, t_len, H_A, HGRN_HEAD_DIM)
    chunk = min(HGRN_CHUNK, t_len) if is_prompt else t_len
    s_new, o_a = gla_chunked(q_a, k_a, v_a, jnp.log(forget), hgrn_s0, chunk)
    o_a = rmsnorm(o_a, hgrn_onorm_g.reshape(H_A, HGRN_HEAD_DIM)).reshape(bsz, t_len, A_WIDTH)
    o_a = o_a * jax.nn.silu(ga.astype(f32))

    pos = pos0 + jnp.arange(t_len, dtype=f32)
    q_b = partial_rotary(qb.astype(f32).reshape(bsz, t_len, H_B, HD_B), pos)
    k_b = partial_rotary(kb.astype(f32).reshape(bsz, t_len, H_B, HD_B), pos)
    v_b = vb.astype(f32).reshape(bsz, t_len, H_B, HD_B)
    if is_prompt:
        o_b = dilated_attn_prompt(q_b, k_b, v_b)
        keep = min(WIN_MAX, t_len)
        new_k = k_b[:, t_len - keep:]
        new_v = v_b[:, t_len - keep:]
    else:
        k_all = jnp.concatenate([win_k.astype(f32), k_b], axis=1)
        v_all = jnp.concatenate([win_v.astype(f32), v_b], axis=1)
        o_b = dilated_attn_sample(q_b, k_all, v_all)
        keep = win_k.shape[1]
        new_k = k_all[:, -keep:]
        new_v = v_all[:, -keep:]

    mix = jnp.concatenate([o_a.astype(dt), o_b.reshape(bsz, t_len, B_WIDTH).astype(dt)], axis=-1)
    h = x + mix @ w_o

    hn = rmsnorm(h, norm_ffn_g)
    u = hn @ w_gate
    full = jnp.concatenate([conv_buf.astype(dt), u], axis=1)
    c = conv_b
    for j in range(CONV_W):
        c = c + conv_w[j] * full[:, j:j + t_len]
    h = h + (jax.nn.silu(c) * (hn @ w_up)) @ w_down
    new_conv = full[:, t_len:]

    gate = jax.nn.sigmoid(rmsnorm(h, norm_ple_g) @ w_ple_gate)
    h = h + gate * (p_emb.astype(dt) @ w_ple_proj)
    return h, s_new, new_k, new_v, new_conv


def setup_inputs(seed: int = 0) -> dict:
    key = jax.random.key(seed)
    ks = jax.random.split(key, 24)
    f32 = jnp.float32
    wb = min(WIN_MAX, PAST_LEN)

    def nrm(k, shape, s=1.0):
        return jax.random.normal(k, shape, f32) * s

    def gain(k, shape):
        return 1.0 + 0.02 * jax.random.normal(k, shape, f32)

    return {
        "x_prompt": nrm(ks[0], (BATCH, SEQ, D_MODEL)),
        "x_sample": nrm(ks[1], (DEC_BATCH, DEC_SEQ, D_MODEL)),
        "state_hgrn": nrm(ks[2], (DEPTH, DEC_BATCH, H_A, HGRN_HEAD_DIM, HGRN_HEAD_DIM), 0.5),
        "cache_win_k": nrm(ks[3], (DEPTH, DEC_BATCH, wb, H_B, HD_B)),
        "cache_win_v": nrm(ks[4], (DEPTH, DEC_BATCH, wb, H_B, HD_B)),
        "state_ffn_conv": nrm(ks[5], (DEPTH, DEC_BATCH, CONV_W - 1, D_FF)),
        "p_prompt": nrm(ks[6], (DEPTH, BATCH, SEQ, PLE_DIM)),
        "p_sample": nrm(ks[7], (DEPTH, DEC_BATCH, DEC_SEQ, PLE_DIM)),
        "norm_attn_g": gain(ks[8], (DEPTH, D_MODEL)),
        "w_in": nrm(ks[9], (DEPTH, D_MODEL, D_IN), D_MODEL ** -0.5),
        "hgrn_lb_logits": nrm(ks[10], (DEPTH + 1, A_WIDTH), 0.1),
        "hgrn_onorm_g": gain(ks[11], (DEPTH, A_WIDTH)),
        "w_o": nrm(ks[12], (DEPTH, MIX_WIDTH, D_MODEL), MIX_WIDTH ** -0.5),
        "norm_ffn_g": gain(ks[13], (DEPTH, D_MODEL)),
        "w_gate": nrm(ks[14], (DEPTH, D_MODEL, D_FF), D_MODEL ** -0.5),
        "w_up": nrm(ks[15], (DEPTH, D_MODEL, D_FF), D_MODEL ** -0.5),
        "conv_w": nrm(ks[16], (DEPTH, CONV_W, D_FF), CONV_W ** -0.5),
        "conv_b": nrm(ks[17], (DEPTH, D_FF), 0.01),
        "w_down": nrm(ks[18], (DEPTH, D_FF, D_MODEL), D_FF ** -0.5),
        "norm_ple_g": gain(ks[19], (DEPTH, D_MODEL)),
        "w_ple_gate": nrm(ks[20], (DEPTH, D_MODEL, D_MODEL), D_MODEL ** -0.5),
        "w_ple_proj": nrm(ks[21], (DEPTH, PLE_DIM, D_MODEL), PLE_DIM ** -0.5),
        "norm_final_g": gain(ks[22], (D_MODEL,)),
    }


def reference(x_prompt, x_sample, state_hgrn, cache_win_k, cache_win_v, state_ffn_conv,
              p_prompt, p_sample, norm_attn_g, w_in, hgrn_lb_logits, hgrn_onorm_g, w_o,
              norm_ffn_g, w_gate, w_up, conv_w, conv_b, w_down, norm_ple_g, w_ple_gate,
              w_ple_proj, norm_final_g):
    f32 = jnp.float32
    lb_all = jnp.cumsum(jax.nn.softmax(hgrn_lb_logits.astype(f32), axis=0), axis=0)
    hp = x_prompt
    hs = x_sample
    p_s, p_k, p_v, p_c = [], [], [], []
    s_s, s_k, s_v, s_c = [], [], [], []
    for i in range(DEPTH):
        weights = (norm_attn_g[i], w_in[i], hgrn_onorm_g[i], w_o[i], norm_ffn_g[i], w_gate[i],
                   w_up[i], conv_w[i], conv_b[i], w_down[i], norm_ple_g[i], w_ple_gate[i],
                   w_ple_proj[i])
        s0_p = jnp.zeros((hp.shape[0], H_A, HGRN_HEAD_DIM, HGRN_HEAD_DIM), f32)
        conv0_p = jnp.zeros((hp.shape[0], CONV_W - 1, D_FF), hp.dtype)
        hp, a1, a2, a3, a4 = layer_step(hp, p_prompt[i], 0.0, s0_p, None, None, conv0_p,
                                        lb_all[i], *weights, True)
        hs, b1, b2, b3, b4 = layer_step(hs, p_sample[i], float(PAST_LEN), state_hgrn[i].astype(f32),
                                        cache_win_k[i], cache_win_v[i], state_ffn_conv[i],
                                        lb_all[i], *weights, False)
        p_s.append(a1); p_k.append(a2); p_v.append(a3); p_c.append(a4)
        s_s.append(b1); s_k.append(b2); s_v.append(b3); s_c.append(b4)
    y_prompt = rmsnorm(hp, norm_final_g)
    y_sample = rmsnorm(hs, norm_final_g)
    prompt_state_hgrn = jnp.stack(p_s)
    prompt_cache_win_k = jnp.stack(p_k)
    prompt_cache_win_v = jnp.stack(p_v)
    prompt_state_ffn_conv = jnp.stack(p_c)
    sample_state_hgrn = jnp.stack(s_s)
    sample_cache_win_k = jnp.stack(s_k)
    sample_cache_win_v = jnp.stack(s_v)
    sample_state_ffn_conv = jnp.stack(s_c)
    return (y_prompt, y_sample, prompt_state_hgrn, prompt_cache_win_k, prompt_cache_win_v,
            prompt_state_ffn_conv, sample_state_hgrn, sample_cache_win_k, sample_cache_win_v,
            sample_state_ffn_conv)
```

```python
import sys
import numpy as np
import concourse.bass as bass
import concourse.mybir as mybir
from concourse.bass_utils import run_bass_kernel_spmd

F32 = mybir.dt.float32
BF16 = mybir.dt.bfloat16
AF = mybir.ActivationFunctionType
ALU = mybir.AluOpType

NCORES = 8
D = 1024
DIN = 3584
DFF = 2816
NFF = 22
PLE = 256
SEQ = 2048
T = 512
NT = 4
NST = SEQ // T
NSEQ_P = 2
NSEQ_S = 4
TS = 4
EPS = 1e-6
WB = 2048
PAST = 16384
NSLAB = 4


class Buf:
    __slots__ = ("name", "w", "r", "excl")

    def __init__(self, name, excl=False):
        self.name = name
        self.w = None
        self.r = {}
        self.excl = excl


class Prog:
    def __init__(self, nc):
        self.nc = nc
        self.eng = {"pe": nc.tensor, "act": nc.scalar, "dve": nc.vector, "pool": nc.gpsimd, "sp": nc.sync}
        self.sem = {}
        self.cnt = {}
        for e in ("pe", "act", "dve", "pool"):
            self.sem[e] = nc.alloc_semaphore(name="sem_" + e)
            self.cnt[e] = 0
        self.known = {e: {} for e in self.eng}
        self.ring = {}
        self.ring_n = {}
        for q, k in (("sp", 12), ("pool", 6)):
            self.ring[q] = [[nc.alloc_semaphore(name="dsem_%s%d" % (q, i)), 0] for i in range(k)]
            self.ring_n[q] = 0
        self.n_ins = 0
        self.marks = []
        self.desc = {}

    def _deps(self, reads, writes, e=None):
        toks = []
        for b in reads:
            if b.w is not None:
                toks.append(b.w)
            if b.excl:
                toks.extend(t for t in b.r.values() if t[3] != e)
        for b in writes:
            if b.w is not None:
                toks.append(b.w)
            toks.extend(b.r.values())
        return toks

    def _wait(self, e, toks):
        best = {}
        for t in toks:
            key, sem, val, src = t
            if src == "pe" and e == "pe":
                continue
            if key not in best or best[key][2] < val:
                best[key] = t
        for key, (k2, sem, val, src) in best.items():
            if self.known[e].get(key, 0) >= val:
                continue
            self.eng[e].wait_ge(sem, val)
            self.known[e][key] = val
            self.n_ins += 1

    def _mark(self, tok, reads, writes):
        for b in reads:
            b.r[tok[0]] = tok
        for b in writes:
            b.w = tok
            b.r = {}

    def op(self, e, fn, reads=(), writes=()):
        self._wait(e, self._deps(reads, writes, e))
        ins = fn(self.eng[e])
        self.cnt[e] += 1
        ins.then_inc(self.sem[e], 1)
        tok = (e, self.sem[e], self.cnt[e], e)
        self._mark(tok, reads, writes)
        self.n_ins += 1
        fr = sys._getframe(1)
        self.desc[(e, self.cnt[e])] = "%s:%d" % (fr.f_code.co_name, fr.f_lineno)
        return ins

    def pe_group(self, fns, reads=(), writes=()):
        self._wait("pe", self._deps(reads, writes))
        ins = None
        for fn in fns:
            ins = fn(self.eng["pe"])
            self.n_ins += 1
        self.cnt["pe"] += 1
        ins.then_inc(self.sem["pe"], 1)
        tok = ("pe", self.sem["pe"], self.cnt["pe"], "pe")
        self._mark(tok, reads, writes)
        fr = sys._getframe(1)
        self.desc[("pe", self.cnt["pe"])] = "%s:%d" % (fr.f_code.co_name, fr.f_lineno)

    def dma(self, q, out, in_, reads=(), writes=(), **kw):
        ring = self.ring[q]
        slot = ring[self.ring_n[q] % len(ring)]
        self.ring_n[q] += 1
        toks = self._deps(reads, writes)
        key = "d_%s_%d" % (q, id(slot))
        if slot[1] > 0:
            toks.append((key, slot[0], slot[1], "dma"))
        self._wait(q, toks)
        if q == "pool":
            kw.setdefault("max_dma_last_dim", 4096)
        ins = self.eng[q].dma_start(out=out, in_=in_, **kw)
        slot[1] += 16
        ins.then_inc(slot[0], 16)
        tok = (key, slot[0], slot[1], "dma")
        self._mark(tok, reads, writes)
        self.n_ins += 1

    def mark(self, name):
        self.marks.append((name, dict(self.cnt)))

    def handoff(self, old, new):
        toks = {}
        for b in old:
            if b.w is not None:
                t = b.w
                if t[0] not in toks or toks[t[0]][2] < t[2]:
                    toks[t[0]] = t
            for t in b.r.values():
                if t[0] not in toks or toks[t[0]][2] < t[2]:
                    toks[t[0]] = t
        for b in new:
            for k, t in toks.items():
                if k not in b.r or b.r[k][2] < t[2]:
                    b.r[k] = t

    def final_wait(self):
        for q in self.ring:
            for slot in self.ring[q]:
                if slot[1] > 0:
                    self.eng["sp"].wait_ge(slot[0], slot[1])


def build_program():
    nc = bass.Bass("TRN2", target_bir_lowering=False)
    P = Prog(nc)

    def din(name, shape, dt=F32):
        return nc.dram_tensor(name, list(shape), dt, kind="ExternalInput").ap()

    def dout(name, shape):
        return nc.dram_tensor(name, list(shape), F32, kind="ExternalOutput").ap()

    x_p = din("x_p", [NSEQ_P * SEQ, D])
    p_p = din("p_p", [NSEQ_P * SEQ, PLE])
    x_s = din("x_s", [NSEQ_S * TS, D])
    p_s = din("p_s", [NSEQ_S * TS, PLE])
    st_in = din("st_in", [NSEQ_S, 4, 128, 128])
    ck_in = din("ck_in", [NSEQ_S, WB, 512])
    cv_in = din("cv_in", [NSEQ_S, WB, 512])
    conv_in = din("conv_in", [128, NFF, NSEQ_S, 2])
    w_in = din("w_in", [D, DIN])
    w_o = din("w_o", [D, D])
    w_gate = din("w_gate", [D, DFF])
    w_up = din("w_up", [D, DFF])
    w_down = din("w_down", [DFF, D])
    w_pg = din("w_pg", [D, D])
    w_pp = din("w_pp", [PLE, D])
    gcols_d = din("gcols", [128, 3, 8])
    gfin_d = din("gfin", [128, D])
    lbl_d = din("lbl", [128, 2, 4])
    gon_d = din("gon", [128, 4])
    convw_d = din("convw", [128, NFF, 3])
    convb_d = din("convb", [128, NFF])
    maskr_d = din("maskr", [128, 20, 128])
    causal_d = din("causal", [128, 64])
    csp_d = din("csp", [128, SEQ // 128, 16])
    css_d = din("css", [TS, 16])

    y_p = dout("y_p", [NSEQ_P * SEQ, D])
    y_s = dout("y_s", [NSEQ_S * TS, D])
    st_p = dout("st_p", [NSEQ_P, 4, 128, 128])
    ck_p = dout("ck_p", [NSEQ_P, SEQ, 512])
    cv_p = dout("cv_p", [NSEQ_P, SEQ, 512])
    conv_p = dout("conv_p", [NSEQ_P, 128, NFF, 2])
    st_s = dout("st_s", [NSEQ_S, 4, 128, 128])
    ck_s = dout("ck_s", [NSEQ_S, WB, 512])
    cv_s = dout("cv_s", [NSEQ_S, WB, 512])
    conv_s = dout("conv_s", [NSEQ_S, 128, NFF, 2])
    out_buf = Buf("outputs")

    def sb(name, shape, dt=F32):
        return nc.alloc_sbuf_tensor("sb_" + name, list(shape), dt)

    psum_all = nc.alloc_psum_tensor("psum_all", [128, 8 * 512], F32)
    banks = [psum_all[:, i * 512:(i + 1) * 512] for i in range(8)]
    bank_b = [Buf("bank%d" % i, excl=True) for i in range(8)]

    ident = sb("ident", [128, 128], BF16)
    ones = sb("ones", [128, 128], BF16)
    causal = sb("causal", [128, 64], BF16)
    maskr = sb("maskr", [128, 20, 128], BF16)
    csp = sb("csp", [128, SEQ // 128, 16]); css = sb("css", [TS, 16])
    gcols = sb("gcols", [128, 3, 8]); gfin = sb("gfin", [128, D])
    lbl = sb("lbl", [128, 2, 4]); gon = sb("gon", [128, 4])
    c0 = sb("c0", [128, 4]); c1 = sb("c1", [128, 4]); nc1 = sb("nc1", [128, 4]); lbt = sb("lbt", [128, 4])
    convw = sb("convw", [128, NFF, 3]); convb = sb("convb", [128, NFF])
    rmask_p = sb("rmask_p", [128, T], BF16); rmask_s = sb("rmask_s", [128, 16], BF16)
    neghalf = sb("neghalf", [128, 8])
    const_b = Buf("const")

    KT = sb("KT", [128, 4, SEQ], BF16); KT_b = Buf("KT")
    VH = sb("VH", [128, 8, 512], BF16); VH_b = Buf("VH")
    Vp = sb("Vp", [128, 4, 4, 512], BF16); Vp_b = [Buf("Vp%d" % i) for i in range(4)]
    S = sb("S", [128, 4, 128]); S_bf = sb("S_bf", [128, 4, 128], BF16)
    S_b = [Buf("S%d" % i) for i in range(4)]; Sbf_b = [Buf("Sbf%d" % i) for i in range(4)]
    convhist = sb("convhist", [128, NFF, NSEQ_S, 2]); convhist_b = [Buf("ch%d" % i) for i in range(NFF)]

    h = sb("h", [128, NT, D]); h_b = [Buf("h%d" % i) for i in range(NT)]
    ss = sb("ss", [128, 16]); ms = sb("ms", [128, 16]); rstd = sb("rstd", [128, 16])
    ss_b = [Buf("ss%d" % i) for i in range(16)]
    aT = sb("aT", [128, 8, T], BF16); aT_b = Buf("aT")

    regM = sb("regM", [128, 22 * 1024 // 4])
    regM_bf = regM[:].bitcast(BF16)
    mixT = regM_bf[:, 0:4096].rearrange("p (c t) -> p c t", t=T)
    QT = regM_bf[:, 4096:6144].rearrange("p (c t) -> p c t", t=T)
    ia_tok = regM_bf[:, 6144:8192].rearrange("p (c t) -> p c t", t=512)
    stage = [regM[:, 4096 + i * 512: 4096 + (i + 1) * 512] for i in range(3)]
    gT = regM_bf.rearrange("p (c t) -> p c t", t=T)
    mixT_b = Buf("mixT"); QT_b = Buf("QT"); ia_b = Buf("ia"); stage_b = [Buf("stg%d" % i) for i in range(3)]
    gT_b = Buf("gT")
    yst = regM[:, 0:4096].rearrange("p (k d) -> p k d", d=D)
    yst_b = [Buf("yst%d" % i) for i in range(NT)]
    regM_A = [mixT_b, QT_b, ia_b] + stage_b
    regM_B = [gT_b] + yst_b
    qkb = [sb("qkb%d" % i, [128, 512], BF16) for i in range(4)]; qkb_b = [Buf("qkb%d" % i) for i in range(4)]
    stv = [sb("stv0", [128, 512]), stage[2]]; stv_b = [Buf("stv0"), stage_b[2]]

    HKB = 52
    regH = sb("regH", [128, HKB * 256])
    regH_bf = regH[:].bitcast(BF16)
    identf = regH[:, 0:128]
    o = [0]

    def carve(nelem_f32):
        a = o[0]; o[0] += nelem_f32
        assert o[0] <= HKB * 256
        return a

    sq = []; th = []
    for i in range(4):
        a = carve(512); sq.append(regH[:, a:a + 512])
    for i in range(4):
        a = carve(512); th.append(regH[:, a:a + 512])
    gs = []
    for i in range(4):
        a = carve(256); gs.append(regH_bf[:, 2 * a:2 * a + 512])
    a = carve(512); lf = regH[:, a:a + 512]
    oa_sb = regH[:, a:a + 2048].rearrange("p (h t) -> p h t", t=512)
    a = carve(512); bb = regH[:, a:a + 512]
    a = carve(512); eb = regH[:, a:a + 512]
    a = carve(512); enb = regH[:, a:a + 512]
    qdT = []; kdT = []; kdtok = []
    for i in range(4):
        a = carve(256); qdT.append(regH_bf[:, 2 * a:2 * a + 512])
    for i in range(4):
        a = carve(256); kdT.append(regH_bf[:, 2 * a:2 * a + 512])
    for i in range(4):
        a = carve(256); kdtok.append(regH_bf[:, 2 * a:2 * a + 512].rearrange("p (i k) -> p i k", k=128))
    a = carve(32); ebl = regH[:, a:a + 32].rearrange("p (h j) -> p h j", j=8)
    a = carve(128); AmA = regH_bf[:, 2 * a:2 * a + 256]
    a = carve(256); osq = regH_bf[:, 2 * a:2 * a + 512]

    a = carve(512); lnv = regH[:, a:a + 512]
    a = carve(512); rsv = regH[:, a:a + 512]
    a = carve(512); t1 = regH[:, a:a + 512]
    a = carve(512); StA = regH[:, a:a + 512]
    a = carve(256); rotw = regH[:, a:a + 256].rearrange("p (a h c) -> p a h c", a=4, h=8)
    hg_end = o[0]
    rot_b = Buf("rot"); rot2_b = Buf("rot2")
    sq_b = [Buf("sq%d" % i) for i in range(4)]; th_b = [Buf("th%d" % i) for i in range(4)]
    gs_b = [Buf("gs%d" % i) for i in range(4)]
    lf_b = Buf("lf"); bb_b = Buf("bb"); eb_b = Buf("eb"); enb_b = Buf("enb"); oa_b = Buf("oa")
    qdT_b = [Buf("qdT%d" % i) for i in range(4)]; kdT_b = [Buf("kdT%d" % i) for i in range(4)]
    kdtok_b = [Buf("kdtok%d" % i) for i in range(4)]; ebl_b = [Buf("ebl%d" % i) for i in range(4)]
    AmA_b = Buf("AmA"); osq_b = Buf("osq"); lnv_b = Buf("lnv"); rsv_b = Buf("rsv")
    t1_b = Buf("t1"); StA_b = Buf("StA")
    regH_A = sq_b + th_b + gs_b + [lf_b, bb_b, eb_b, enb_b] + qdT_b + kdT_b + kdtok_b + ebl_b + [AmA_b] + \
        [osq_b, lnv_b, rsv_b, t1_b, StA_b, oa_b, rot_b, rot2_b]
    o[0] = 0
    uext = []; ccv = []; scv = []
    for i in range(2):
        a = carve(520); uext.append(regH[:, a:a + 520])
    for i in range(2):
        a = carve(512); ccv.append(regH[:, a:a + 512])
    for i in range(2):
        a = carve(512); scv.append(regH[:, a:a + 512])
    pst = []
    for i in range(2):
        a = carve(256); pst.append(regH[:, a:a + 256])
    a = carve(512); pb = regH_bf[:, 2 * a:2 * a + 1024].rearrange("p (k c) -> p k c", c=256)
    a = carve(512); pT = regH_bf[:, 2 * a:2 * a + 1024].rearrange("p (c t) -> p c t", t=T)
    tg = []; t2 = []
    for i in range(2):
        a = carve(512); tg.append(regH[:, a:a + 512])
    for i in range(2):
        a = carve(512); t2.append(regH[:, a:a + 512])
    a = carve(1024); wpp_sb = regH_bf[:, 2 * a:2 * a + 2048].rearrange("p (c n) -> p c n", n=512)
    wpp_b = Buf("wpp")
    uext_b = [Buf("uext%d" % i) for i in range(2)]; ccv_b = [Buf("cc%d" % i) for i in range(2)]
    scv_b = [Buf("sc%d" % i) for i in range(2)]; pst_b = [Buf("pst%d" % i) for i in range(2)]
    pb_b = Buf("pb"); pT_b = Buf("pT"); tg_b = [Buf("tg%d" % i) for i in range(2)]; t2_b = [Buf("t2%d" % i) for i in range(2)]
    regH_B = uext_b + ccv_b + scv_b + pst_b + [pb_b, pT_b, wpp_b] + tg_b + t2_b

    Pexp = [sb("Pexp%d" % i, [128, 2, 512], BF16) for i in range(3)]; Pexp_b = [Buf("Pexp%d" % i) for i in range(3)]
    Pm = [sb("Pm%d" % i, [128, 2, 512], BF16) for i in range(3)]; Pm_b = [Buf("Pm%d" % i) for i in range(3)]
    junk = Pexp[0][:, :, :].rearrange("p e n -> p (e n)"); junk_b = Pexp_b[0]
    xs = [Pexp[1 + i][:, :, :].rearrange("p e n -> p (e n)") for i in range(2)]; xs_b = [Pexp_b[1 + i] for i in range(2)]
    lnL = lnv; rL = rsv; lnL_b = lnv_b; rL_b = rsv_b
    KTn = sb("KTn", [128, 4, 4], BF16); KTn_b = Buf("KTn")
    Vn = sb("Vn", [4, 512], BF16); Vn_b = Buf("Vn")
    sacc = sb("sacc", [128, 32]); sacc_b = Buf("sacc")
    cbf = [sb("cbf%d" % i, [128, 512], BF16) for i in range(2)]; cbf_b = [Buf("cbf%d" % i) for i in range(2)]

    slab = [sb("slab%d" % i, [128, 8, 512], BF16) for i in range(NSLAB)]
    slab_b = [Buf("slab%d" % i) for i in range(NSLAB)]
    slab_i = [0]

    def bank_bf(i):
        return banks[i].bitcast(BF16)

    def group_sched():
        L = [("win", b) for b in (1, 0, 3, 2, 4, 5, 6)]
        L += [("wo", 0), ("wo", 1)]
        for g0 in range(0, NFF, 4):
            L += [("gate", g0), ("up", g0)]
        for half in range(2):
            for g0 in range(0, NFF, 8):
                L.append(("down", half, g0))
        L += [("pg", 0), ("pg", 1)]
        return L

    sched = []
    for _ in range(NSEQ_P * NST + 1):
        sched += group_sched()
    ws = {"issued": 0, "consumed": 0}

    def w_parts(desc, sl):
        k = desc[0]
        col = lambda w, c0_, nc_: w[:, c0_:c0_ + nc_].rearrange("(kc p) n -> p kc n", p=128)
        if k == "win":
            return [(sl[:, 0:8, 0:512], col(w_in, desc[1] * 512, 512))]
        if k == "wo":
            return [(sl[:, 0:8, 0:512], col(w_o, desc[1] * 512, 512))]
        if k in ("gate", "up"):
            ncc = min(4, NFF - desc[1])
            return [(sl[:, 0:8, 0:ncc * 128], col(w_gate if k == "gate" else w_up, desc[1] * 128, ncc * 128))]
        if k == "down":
            half, g0 = desc[1], desc[2]
            ncc = min(8, NFF - g0)
            return [(sl[:, 0:ncc, :], w_down[g0 * 128:(g0 + ncc) * 128, half * 512:(half + 1) * 512].rearrange("(cc p) n -> p cc n", p=128))]
        if k == "pp":
            return [(sl[:, 0:2, :], col(w_pp, 0, 512)), (sl[:, 2:4, :], col(w_pp, 512, 512))]
        if k == "pg":
            return [(sl[:, 0:8, 0:512], col(w_pg, desc[1] * 512, 512))]
        raise ValueError(desc)

    def get_w(desc, oldest=None):
        j = ws["consumed"]
        assert sched[j] == desc, (sched[j], desc)
        if oldest is None:
            oldest = j
        while ws["issued"] < min(len(sched), oldest + NSLAB):
            i = ws["issued"]
            for dst, src in w_parts(sched[i], slab[i % NSLAB]):
                P.dma("pool", dst, src, writes=[slab_b[i % NSLAB]])
            ws["issued"] += 1
        assert ws["issued"] > j
        ws["consumed"] += 1
        ws["last"] = j
        return slab[j % NSLAB], slab_b[j % NSLAB]

    def w_prefetch():
        j = ws["consumed"]
        while ws["issued"] < min(len(sched), j + NSLAB):
            i = ws["issued"]
            for dst, src in w_parts(sched[i], slab[i % NSLAB]):
                P.dma("pool", dst, src, writes=[slab_b[i % NSLAB]])
            ws["issued"] += 1

    def mm(out, lhsT, rhs, start, stop):
        return lambda e: e.matmul(out, lhsT=lhsT, rhs=rhs, start=start, stop=stop)

    for dst, src in ((csp, csp_d), (gcols, gcols_d), (gfin, gfin_d), (lbl, lbl_d), (gon, gon_d),
                     (convw, convw_d), (convb, convb_d)):
        P.dma("sp", dst[:], src, writes=[const_b])
    P.dma("sp", css[:], css_d, writes=[const_b])
    P.dma("pool", maskr[:], maskr_d, writes=[const_b])
    P.dma("pool", causal[:], causal_d, writes=[const_b])
    P.op("pool", lambda e: e.memset(identf[:], 1.0), writes=[const_b])
    P.op("pool", lambda e: e.affine_select(out=identf[:], in_=identf[:], pattern=[[-1, 128]],
                                           compare_op=ALU.is_equal, fill=0.0, base=0, channel_multiplier=1),
         writes=[const_b])
    P.op("dve", lambda e: e.tensor_copy(out=ident[:], in_=identf[:]), reads=[const_b], writes=[const_b])
    P.op("pool", lambda e: e.memset(ones[:], 1.0), writes=[const_b])
    P.op("pool", lambda e: e.memset(neghalf[:], -0.5), writes=[const_b])
    P.op("pool", lambda e: e.memset(rmask_p[:], 1.0), writes=[const_b])
    P.op("pool", lambda e: e.memset(rmask_p[:].rearrange("p (c t) -> p c t", t=64)[:, :, 0:1], 0.0), writes=[const_b])
    P.op("pool", lambda e: e.memset(rmask_s[:], 1.0), writes=[const_b])
    P.op("pool", lambda e: e.memset(rmask_s[:].rearrange("p (c t) -> p c t", t=4)[:, :, 0:1], 0.0), writes=[const_b])
    P.op("dve", lambda e: e.tensor_tensor(out=lbt[:], in0=lbl[:, 1, :], in1=lbl[:, 0, :], op=ALU.subtract),
         reads=[const_b], writes=[const_b])
    P.op("act", lambda e: e.activation(out=lbt[:], in_=lbt[:], func=AF.Exp), reads=[const_b], writes=[const_b])
    P.op("dve", lambda e: e.tensor_scalar(out=lbt[:], in0=lbt[:], scalar1=1.0, scalar2=None, op0=ALU.add),
         reads=[const_b], writes=[const_b])
    P.op("dve", lambda e: e.reciprocal(out=lbt[:], in_=lbt[:]), reads=[const_b], writes=[const_b])
    P.op("dve", lambda e: e.tensor_scalar(out=c1[:], in0=lbt[:], scalar1=-0.5, scalar2=0.5, op0=ALU.mult, op1=ALU.add),
         reads=[const_b], writes=[const_b])
    P.op("dve", lambda e: e.tensor_scalar(out=c0[:], in0=lbt[:], scalar1=0.5, scalar2=0.5, op0=ALU.mult, op1=ALU.add),
         reads=[const_b], writes=[const_b])
    P.op("dve", lambda e: e.tensor_scalar(out=nc1[:], in0=c1[:], scalar1=-1.0, scalar2=None, op0=ALU.mult),
         reads=[const_b], writes=[const_b])

    P.handoff([const_b], regH_A + regH_B)
    ss_n = [0]

    def norm_A(k, ntok):
        si = ss_n[0] % 16
        ss_n[0] += 1
        P.op("act", lambda e: e.activation(out=junk[:ntok, :], in_=h[:ntok, k, :], func=AF.Square,
                                           accum_out=ss[:ntok, si:si + 1]),
             reads=[h_b[k]], writes=[junk_b, ss_b[si]])
        P.op("dve", lambda e: e.tensor_scalar(out=ms[:ntok, si:si + 1], in0=ss[:ntok, si:si + 1], scalar1=1.0 / D,
                                              scalar2=EPS, op0=ALU.mult, op1=ALU.add),
             reads=[ss_b[si]], writes=[ss_b[si]])
        P.op("pool", lambda e: e.tensor_tensor(out=rstd[:ntok, si:si + 1], in0=ms[:ntok, si:si + 1],
                                               in1=neghalf[:ntok, 0:1], op=ALU.pow),
             reads=[ss_b[si], const_b], writes=[ss_b[si]])
        return si

    def norm_B(k, ntok, si):
        xi = k % 2
        P.op("dve", lambda e: e.tensor_scalar(out=xs[xi][:ntok, :], in0=h[:ntok, k, :], scalar1=rstd[:ntok, si:si + 1],
                                              scalar2=None, op0=ALU.mult),
             reads=[h_b[k], ss_b[si]], writes=[xs_b[xi]])

    def norm_C(k, ntok):
        xi = k % 2
        bk = 6 + (k % 2)
        bv = bank_bf(bk)
        P.pe_group([(lambda e, c=c: e.transpose(out=bv[:, c * 128:c * 128 + ntok], in_=xs[xi][:ntok, c * 128:(c + 1) * 128],
                                                identity=ident[:ntok, :ntok])) for c in range(8)],
                   reads=[xs_b[xi], const_b], writes=[bank_b[bk]])

    def norm_D(k, col0, ntok, gi):
        bk = 6 + (k % 2)
        bv = bank_bf(bk)
        P.op("dve", lambda e: e.tensor_tensor(
            out=aT[:, :, col0:col0 + ntok],
            in0=bv.rearrange("p (c t) -> p c t", t=128)[:, :, 0:ntok],
            in1=gcols[:, gi, :].unsqueeze(2).to_broadcast([128, 8, ntok]), op=ALU.mult),
            reads=[bank_b[bk], const_b], writes=[aT_b])

    def norm_T(G, gi):
        dt_ = G["dt"]
        n_ = len(dt_)
        sis = {}
        order = []
        for k in range(min(2, n_)):
            order += [("A", k), ("B", k)]
        for k in range(n_):
            order.append(("C", k))
            if k + 2 < n_:
                order.append(("A", k + 2))
            order.append(("D", k))
            if k + 2 < n_:
                order.append(("B", k + 2))
        for what, k in order:
            col0, ntok = dt_[k]
            if what == "A":
                sis[k] = norm_A(k, ntok)
            elif what == "B":
                norm_B(k, ntok, sis[k])
            elif what == "C":
                norm_C(k, ntok)
            else:
                norm_D(k, col0, ntok, gi)

    def phase_wo_norm1(G):
        sl0, slb0 = get_w(("wo", 0))
        sl1, slb1 = get_w(("wo", 1), oldest=ws["last"])
        sls = [(sl0, slb0), (sl1, slb1)]
        dt_ = G["dt"]
        sis = []
        for k, (col0, ntok) in enumerate(dt_):
            for half in range(2):
                sl, slb = sls[half]
                bk = (2 * k + half) % 4
                P.pe_group([mm(banks[bk][:ntok, :], mixT[:, kc, col0:col0 + ntok], sl[:, kc, :], kc == 0, kc == 7)
                            for kc in range(8)], reads=[slb, mixT_b], writes=[bank_b[bk]])
            if k == len(dt_) - 1:
                w_prefetch()
            if k >= 1:
                norm_B(k - 1, dt_[k - 1][1], sis[k - 1])
                norm_C(k - 1, dt_[k - 1][1])
            for half in range(2):
                bk = (2 * k + half) % 4
                hv = h[:ntok, k, half * 512:(half + 1) * 512]
                P.op("dve", lambda e: e.tensor_tensor(out=hv, in0=banks[bk][:ntok, :], in1=hv, op=ALU.add),
                     reads=[bank_b[bk], h_b[k]], writes=[h_b[k]])
            sis.append(norm_A(k, ntok))
            if k >= 1:
                norm_D(k - 1, dt_[k - 1][0], dt_[k - 1][1], 1)
        k = len(dt_) - 1
        norm_B(k, dt_[k][1], sis[k])
        norm_C(k, dt_[k][1])
        norm_D(k, dt_[k][0], dt_[k][1], 1)

    def rotary(G, st_ap, cs_ap, ntok, bufs, dst_ap=None, dst_bufs=None):
        if dst_ap is None:
            dst_ap = st_ap; dst_bufs = bufs
        v = st_ap.rearrange("p (h c) -> p h c", c=64)
        dv = dst_ap.rearrange("p (h c) -> p h c", c=64)
        u = v[:ntok, :, 0:16].rearrange("p h (a c) -> p h a c", c=8)
        cosb = cs_ap[:ntok, 0:8].unsqueeze(1).unsqueeze(1).to_broadcast([ntok, 8, 2, 8])
        sinb = cs_ap[:ntok, 8:16].unsqueeze(1).unsqueeze(1).to_broadcast([ntok, 8, 2, 8])
        U1 = rotw[:ntok, 0:2, :, :].rearrange("p a h c -> p h a c")
        W = rotw[:ntok, 2:4, :, :].rearrange("p a h c -> p h a c")
        P.op("dve", lambda e: e.tensor_tensor(out=U1, in0=u, in1=cosb, op=ALU.mult), reads=bufs + [const_b], writes=[rot_b])
        P.op("dve", lambda e: e.tensor_tensor(out=W, in0=u, in1=sinb, op=ALU.mult), reads=bufs + [const_b], writes=[rot2_b])
        P.op("dve", lambda e: e.tensor_tensor(out=dv[:ntok, :, 0:8], in0=rotw[:ntok, 0, :, :], in1=rotw[:ntok, 3, :, :], op=ALU.subtract),
             reads=[rot_b, rot2_b], writes=dst_bufs)
        P.op("dve", lambda e: e.tensor_tensor(out=dv[:ntok, :, 8:16], in0=rotw[:ntok, 1, :, :], in1=rotw[:ntok, 2, :, :], op=ALU.add),
             reads=[rot_b, rot2_b], writes=dst_bufs)


    def phase_win(G, sample, seq, st):
        n = G["n"]
        for blk, kind in ((1, "f"), (0, "q"), (3, "g")):
            sl, slb = get_w(("win", blk))
            for hh in range(4):
                bk = hh
                P.pe_group([mm(banks[bk][:, 0:n], sl[:, kc, hh * 128:(hh + 1) * 128], aT[:, kc, 0:n], kc == 0, kc == 7)
                            for kc in range(8)], reads=[slb, aT_b], writes=[bank_b[bk]])
                if kind == "f":
                    P.op("act", lambda e: e.activation(out=th[hh][:, 0:n], in_=banks[bk][:, 0:n], func=AF.Tanh, scale=0.5),
                         reads=[bank_b[bk]], writes=[th_b[hh]])
                elif kind == "q":
                    P.op("act", lambda e: e.activation(out=sq[hh][:, 0:n], in_=banks[bk][:, 0:n], func=AF.Silu),
                         reads=[bank_b[bk]], writes=[sq_b[hh]])
                else:
                    P.op("act", lambda e: e.activation(out=gs[hh][:, 0:n], in_=banks[bk][:, 0:n], func=AF.Silu),
                         reads=[bank_b[bk]], writes=[gs_b[hh]])
        for blk, kind in ((2, "i"), (4, "q"), (5, "k"), (6, "v")):
            sl, slb = get_w(("win", blk))
            posts = []
            for k, (col0, ntok) in enumerate(G["stl"]):
                bk = k % 4
                gt = st * NT + k
                P.pe_group([mm(banks[bk][:ntok, :], aT[:, kc, col0:col0 + ntok], sl[:, kc, :], kc == 0, kc == 7)
                            for kc in range(8)], reads=[slb, aT_b], writes=[bank_b[bk]])
                if len(posts) >= 2:
                    posts.pop(0)()
                if kind == "i":
                    P.op("act", lambda e: e.activation(out=ia_tok[:ntok, k, :], in_=banks[bk][:ntok, :], func=AF.Copy),
                         reads=[bank_b[bk]], writes=[ia_b])
                    continue
                d2 = k % 2
                if kind == "v":
                    P.op("act", lambda e: e.activation(out=VH[:, gt % 8, :], in_=banks[bk][:, :], func=AF.Copy),
                         reads=[bank_b[bk]], writes=[VH_b])
                    P.op("dve", lambda e: e.tensor_copy(out=stv[d2][:, :], in_=banks[bk][:, :]), reads=[bank_b[bk]], writes=[stv_b[d2]])
                    P.dma("sp", cv_p[seq, gt * 128:(gt + 1) * 128, :], stv[d2][:, :], reads=[stv_b[d2]])
                    continue
                cs_ap = csp[:, gt, :]
                if kind == "q":
                    qb_ = qkb[d2]; qbb = qkb_b[d2]
                    P.op("act", lambda e: e.activation(out=qb_[:, :], in_=banks[bk][:, :], func=AF.Copy), reads=[bank_b[bk]], writes=[qbb])
                    rotary(G, banks[bk][:, :], cs_ap, 128, [bank_b[bk]], dst_ap=qb_[:, :], dst_bufs=[qbb])
                else:
                    qb_ = qkb[2 + d2]; qbb = qkb_b[2 + d2]
                    sk = stage[d2]; skb = stage_b[d2]
                    P.op("act", lambda e: e.activation(out=sk[:, :], in_=banks[bk][:, :], func=AF.Copy), reads=[bank_b[bk]], writes=[skb])
                    rotary(G, banks[bk][:, :], cs_ap, 128, [bank_b[bk]], dst_ap=sk[:, :], dst_bufs=[skb])
                    P.op("dve", lambda e: e.tensor_copy(out=qb_[:, :], in_=sk[:, :]), reads=[skb], writes=[qbb])
                    P.dma("sp", ck_p[seq, gt * 128:(gt + 1) * 128, :], sk[:, :], reads=[skb])

                def post(kind=kind, k=k, gt=gt, col0=col0, qb_=qb_, qbb=qbb):
                    tb_ = 6 + (k % 2)
                    bv = bank_bf(tb_)
                    P.pe_group([(lambda e, c=c: e.transpose(out=bv[:, c * 128:(c + 1) * 128], in_=qb_[:, c * 128:(c + 1) * 128],
                                                            identity=ident[:, :])) for c in range(4)],
                               reads=[qbb, const_b], writes=[bank_b[tb_]])
                    src = bv[:, 0:512].rearrange("p (c t) -> p c t", t=128)
                    if kind == "q":
                        P.op("act", lambda e: e.activation(out=QT[:, :, col0:col0 + 128], in_=src, func=AF.Copy, scale=0.125),
                             reads=[bank_b[tb_]], writes=[QT_b])
                    else:
                        P.op("act", lambda e: e.activation(out=KT[:, :, gt * 128:(gt + 1) * 128], in_=src, func=AF.Copy),
                             reads=[bank_b[tb_]], writes=[KT_b])
                posts.append(post)
            for p_ in posts:
                p_()
            if kind == "v":
                for rho in range(4):
                    for t_ in range(4):
                        P.dma("sp", Vp[32 * t_:32 * (t_ + 1), st % 4, rho, :], VH[rho:128:4, (4 * st + t_) % 8, :],
                              reads=[VH_b], writes=[Vp_b[st % 4]])
            hgrn_g1(G, {"i": 0, "q": 1, "k": 2, "v": 3}[kind])

    def hgrn_gates(G, sample):
        n = G["n"]
        rm = rmask_s if sample else rmask_p
        C = 4 if sample else 64
        nch = n // C
        for hh in range(4):
            P.op("act", lambda e: e.activation(out=lf[:, 0:n], in_=th[hh][:, 0:n], func=AF.Ln,
                                               scale=c1[:, hh:hh + 1], bias=c0[:, hh:hh + 1]),
                 reads=[th_b[hh], const_b], writes=[lf_b])
            P.op("dve", lambda e: e.tensor_tensor_scan(out=bb[:, 0:n], data0=rm[:, 0:n], data1=lf[:, 0:n], initial=0.0,
                                                       op0=ALU.mult, op1=ALU.add),
                 reads=[lf_b, const_b], writes=[bb_b])
            P.op("act", lambda e: e.activation(out=eb[:, 0:n], in_=bb[:, 0:n], func=AF.Exp), reads=[bb_b], writes=[eb_b])
            P.op("act", lambda e: e.activation(out=enb[:, 0:n], in_=bb[:, 0:n], func=AF.Exp, scale=-1.0),
                 reads=[bb_b], writes=[enb_b])
            P.op("dve", lambda e: e.scalar_tensor_tensor(out=qdT[hh][:, 0:n], in0=sq[hh][:, 0:n], scalar=128.0 ** -0.5,
                                                         in1=eb[:, 0:n], op0=ALU.mult, op1=ALU.mult),
                 reads=[sq_b[hh], eb_b], writes=[qdT_b[hh]])
            P.op("dve", lambda e: e.tensor_copy(out=ebl[:, hh, 0:nch],
                                                in_=eb[:, 0:n].rearrange("p (c t) -> p c t", t=C)[:, :, C - 1]),
                 reads=[eb_b], writes=[ebl_b[hh]])
            P.op("dve", lambda e: e.tensor_scalar(out=th[hh][:, 0:n], in0=th[hh][:, 0:n], scalar1=nc1[:, hh:hh + 1],
                                                  scalar2=c1[:, hh:hh + 1], op0=ALU.mult, op1=ALU.add),
                 reads=[th_b[hh], const_b], writes=[th_b[hh]])
            P.op("dve", lambda e: e.tensor_tensor(out=kdT[hh][:, 0:n], in0=th[hh][:, 0:n], in1=enb[:, 0:n], op=ALU.mult),
                 reads=[th_b[hh], enb_b], writes=[kdT_b[hh]])
            tb_ = 6 + (hh % 2)
            bv = bank_bf(tb_)
            fns = []
            for k, (col0, ntok) in enumerate(G["stl"]):
                fns.append(lambda e, k=k, col0=col0, ntok=ntok: e.transpose(
                    out=bv[:ntok, k * 128:(k + 1) * 128], in_=kdT[hh][:, col0:col0 + ntok], identity=ident[:, :]))
            P.pe_group(fns, reads=[kdT_b[hh], const_b], writes=[bank_b[tb_]])
            nst = len(G["stl"])
            ntok0 = G["stl"][0][1]
            P.op("act", lambda e: e.activation(out=kdtok[hh][:ntok0, 0:nst, :],
                                               in_=bv[:ntok0, 0:nst * 128].rearrange("p (i k) -> p i k", k=128), func=AF.Copy),
                 reads=[bank_b[tb_]], writes=[kdtok_b[hh]])

    def hgrn_chunks(G, sample, seq_of_chunk=None):
        n = G["n"]
        C = 4 if sample else 64
        nch = n // C
        S_all = S[:, :, :].rearrange("p h v -> p (h v)")
        Sbf_all = S_bf[:, :, :].rearrange("p h v -> p (h v)")

        def base_of(j):
            return 0 if sample else (j % 2) * 64

        def emit_A(j):
            cols = slice(j * C, (j + 1) * C)
            base = base_of(j)
            P.pe_group([mm(banks[4][base:base + C, hh * 64:hh * 64 + C], kdT[hh][:, cols], qdT[hh][:, cols], True, True)
                        for hh in range(4)], reads=kdT_b + qdT_b, writes=[bank_b[4]])
            P.op("dve", lambda e: e.tensor_tensor(
                out=AmA[base:base + C, :].rearrange("p (h t) -> p h t", t=64)[:, :, 0:C],
                in0=banks[4][base:base + C, 0:256].rearrange("p (h t) -> p h t", t=64)[:, :, 0:C],
                in1=causal[base:base + C, 0:C].unsqueeze(1).to_broadcast([C, 4, C]), op=ALU.mult),
                reads=[bank_b[4], const_b], writes=[AmA_b])

        if not sample:
            emit_A(0)
        for j in range(nch):
            cols = slice(j * C, (j + 1) * C)
            base = base_of(j)
            if sample:
                k = j
                P.dma("sp", S[:, :, :], st_in[j].rearrange("h k v -> k h v"), writes=S_b)
                P.op("act", lambda e: e.activation(out=Sbf_all, in_=S_all, func=AF.Copy), reads=S_b, writes=Sbf_b)
                emit_A(j)
            else:
                k = j // 2
            db = 6 + (j % 2)
            fns = []
            for hh in range(4):
                vtok = ia_tok[base:base + C, k, hh * 128:(hh + 1) * 128]
                fns.append(mm(banks[hh][:, cols], S_bf[:, hh, :], qdT[hh][:, cols], True, False))
                fns.append(mm(banks[hh][:, cols], vtok, AmA[base:base + C, hh * 64:hh * 64 + C], False, True))
                fns.append(mm(banks[db][:, hh * 128:(hh + 1) * 128], kdtok[hh][base:base + C, k, :], vtok, True, True))
            P.pe_group(fns, reads=Sbf_b + qdT_b + kdtok_b + [ia_b, AmA_b], writes=[bank_b[0], bank_b[1], bank_b[2], bank_b[3], bank_b[db]])
            if (not sample) and j + 1 < nch:
                emit_A(j + 1)
            P.op("dve", lambda e: e.tensor_tensor(out=StA[:, :], in0=banks[db][:, :], in1=S_all, op=ALU.add),
                 reads=[bank_b[db]] + S_b, writes=[StA_b])
            P.op("dve", lambda e: e.tensor_tensor(out=S[:, :, :], in0=StA[:, :].rearrange("p (h v) -> p h v", v=128),
                                                  in1=ebl[:, :, j:j + 1].to_broadcast([128, 4, 128]), op=ALU.mult),
                 reads=[StA_b] + ebl_b, writes=S_b)
            P.op("act", lambda e: e.activation(out=Sbf_all, in_=S_all, func=AF.Copy), reads=S_b, writes=Sbf_b)
            if sample:
                P.dma("sp", st_s[j].rearrange("h k v -> k h v"), S[:, :, :], reads=S_b)

    def hgrn_out(G):
        n = G["n"]
        for hh in range(4):
            P.op("act", lambda e: e.activation(out=osq[:, 0:n], in_=banks[hh][:, 0:n], func=AF.Square),
                 reads=[bank_b[hh]], writes=[osq_b])
            nb = 4 + (hh % 2)
            P.pe_group([mm(banks[nb][:, 0:n], ones[:, :], osq[:, 0:n], True, True)], reads=[osq_b, const_b], writes=[bank_b[nb]])
            P.op("act", lambda e: e.activation(out=lnv[:, 0:n], in_=banks[nb][:, 0:n], func=AF.Ln, scale=1.0 / 128, bias=EPS),
                 reads=[bank_b[nb]], writes=[lnv_b])
            P.op("act", lambda e: e.activation(out=rsv[:, 0:n], in_=lnv[:, 0:n], func=AF.Exp, scale=-0.5),
                 reads=[lnv_b], writes=[rsv_b])
            P.op("dve", lambda e: e.tensor_tensor(out=t1[:, 0:n], in0=banks[hh][:, 0:n], in1=rsv[:, 0:n], op=ALU.mult),
                 reads=[bank_b[hh], rsv_b], writes=[t1_b])
            P.op("dve", lambda e: e.scalar_tensor_tensor(out=mixT[:, hh, 0:n], in0=t1[:, 0:n], scalar=gon[:, hh:hh + 1],
                                                         in1=gs[hh][:, 0:n], op0=ALU.mult, op1=ALU.mult),
                 reads=[t1_b, gs_b[hh], const_b], writes=[mixT_b])

    att_i = [0]

    def attn_items(items, sbase=4, side=None, nslot=2):
        st_ = []

        def qk(it):
            i2 = att_i[0] % nslot
            att_i[0] += 1
            b0 = sbase + 2 * i2
            sc2 = psum_all[:, b0 * 512:(b0 + 2) * 512].rearrange("p (e n) -> p e n", n=512)
            nk, N = it["nk"], it["N"]
            P.pe_group([mm(sc2[:nk, e2, 0:N], it["kt_ap"][64 * e2:64 * e2 + 64, :], it["q_ap"][64 * e2:64 * e2 + 64, :], True, True)
                        for e2 in range(2)], reads=it["rd"] + [QT_b], writes=[bank_b[b0], bank_b[b0 + 1]])
            P.op("act", lambda e: e.activation(out=Pexp[i2][:nk, :, 0:N], in_=sc2[:nk, :, 0:N], func=AF.Exp),
                 reads=[bank_b[b0], bank_b[b0 + 1]], writes=[Pexp_b[i2]])
            nq = it["nq"]
            if it["m3"]:
                nj = N // nq
                P.op("dve", lambda e: e.tensor_tensor(out=Pm[i2][:nk, :, 0:N].rearrange("p e (b q) -> p e b q", q=nq),
                                                      in0=Pexp[i2][:nk, :, 0:N].rearrange("p e (b q) -> p e b q", q=nq),
                                                      in1=it["m_ap"].unsqueeze(1).to_broadcast([nk, 2, nj, nq]), op=ALU.mult),
                     reads=[Pexp_b[i2], const_b], writes=[Pm_b[i2]])
            else:
                P.op("dve", lambda e: e.tensor_tensor(out=Pm[i2][:nk, :, 0:N], in0=Pexp[i2][:nk, :, 0:N],
                                                      in1=it["m_ap"].unsqueeze(1).to_broadcast([nk, 2, N]), op=ALU.mult),
                     reads=[Pexp_b[i2], const_b], writes=[Pm_b[i2]])
            return i2

        def pv(it, i2):
            nk, N = it["nk"], it["N"]
            ob, lb_ = it["obank"], it["lbank"]
            oc = it["ocols"]
            f_, l_ = it["first"], it["last"]
            P.pe_group([mm(banks[ob][0:64, oc], it["v_aps"][0], Pm[i2][:nk, 0, 0:N], f_, l_),
                        mm(banks[ob][64:128, oc], it["v_aps"][1], Pm[i2][:nk, 1, 0:N], f_, l_),
                        mm(banks[lb_][0:64, oc], ones[:nk, 0:64], Pm[i2][:nk, 0, 0:N], f_, l_),
                        mm(banks[lb_][64:128, oc], ones[:nk, 0:64], Pm[i2][:nk, 1, 0:N], f_, l_)],
                       reads=it["rd"] + [Pm_b[i2], const_b], writes=[bank_b[ob], bank_b[lb_]])
            if it["fin"] is not None:
                it["fin"]()

        side = list(side or [])
        every = max(1, len(items) // (len(side) + 1)) if side else 0
        npv = [0]

        def pv_side(a, b):
            pv(a, b)
            npv[0] += 1
            if side and npv[0] % every == 0:
                side.pop(0)(sbase + 2 * b)

        for idx, it in enumerate(items):
            st_.append((it, qk(it)))
            if len(st_) > nslot - 1:
                a, b = st_.pop(0)
                pv_side(a, b)
        while st_:
            a, b = st_.pop(0)
            pv_side(a, b)
        while side:
            side.pop(0)(sbase)

    def attn_finish(pr, cols, ocols_mix, obank, lbank):
        n_ = cols.stop - cols.start
        P.op("act", lambda e: e.activation(out=lnL[:, 0:n_], in_=banks[lbank][:, cols], func=AF.Ln), reads=[bank_b[lbank]], writes=[lnL_b])
        P.op("act", lambda e: e.activation(out=rL[:, 0:n_], in_=lnL[:, 0:n_], func=AF.Exp, scale=-1.0), reads=[lnL_b], writes=[rL_b])
        P.op("dve", lambda e: e.tensor_tensor(out=mixT[:, 4 + pr, ocols_mix], in0=banks[obank][:, cols], in1=rL[:, 0:n_], op=ALU.mult),
             reads=[bank_b[obank], rL_b], writes=[mixT_b])

    def phase_attn_prompt(G, st, side):
        gt0 = st * NT
        items = []
        for pr in range(4):
            obank = 0; lbank = 1
            its = []
            kts = [gt0] + [kt for kt in range(max(0, gt0 - 4), gt0 + NT) if kt != gt0]
            for kt in kts:
                j_lo = max(0, kt - gt0); j_hi = NT - 1
                nj = j_hi - j_lo + 1
                off_lo = gt0 + j_lo - kt
                cols = slice(j_lo * 128, (j_hi + 1) * 128)
                its.append(dict(kt_ap=KT[:, pr, kt * 128:(kt + 1) * 128], q_ap=QT[:, pr, cols],
                                v_aps=[VH[:, kt % 8, (2 * pr) * 64:(2 * pr + 1) * 64], VH[:, kt % 8, (2 * pr + 1) * 64:(2 * pr + 2) * 64]],
                                m_ap=maskr[:, off_lo:off_lo + nj, :], m3=True, nq=128, nk=128, N=nj * 128,
                                obank=obank, lbank=lbank, ocols=cols, rd=[KT_b, VH_b], fin=None))
            for G_ in (st - 2, st - 3, st - 4):
                if G_ < 0:
                    continue
                for rho in range(4):
                    its.append(dict(kt_ap=KT[:, pr, 512 * G_:512 * (G_ + 1)].rearrange("p (i f) -> p f i", f=4)[:, rho, :],
                                    q_ap=QT[:, pr, 0:T].rearrange("p (i f) -> p f i", f=4)[:, rho, :],
                                    v_aps=[Vp[:, G_ % 4, rho, (2 * pr) * 64:(2 * pr + 1) * 64],
                                           Vp[:, G_ % 4, rho, (2 * pr + 1) * 64:(2 * pr + 2) * 64]],
                                    m_ap=maskr[:, 19 if G_ == st - 4 else 18, :], m3=False, nq=128, nk=128, N=128,
                                    obank=obank, lbank=lbank, ocols=slice(rho, T, 4), rd=[KT_b, Vp_b[G_ % 4]], fin=None))
            for ii, it in enumerate(its):
                it["first"] = (ii == 0)
                it["last"] = (ii == len(its) - 1)
            its[-1]["fin"] = (lambda pr=pr, obank=obank, lbank=lbank: attn_finish(pr, slice(0, T), slice(0, T), obank, lbank))
            items += its
        attn_items(items, sbase=2, side=side, nslot=3)

    def hgrn_g1(G, hh):
        n = G["n"]
        rm = rmask_p
        C = 64
        nch = n // C
        P.op("act", lambda e: e.activation(out=lf[:, 0:n], in_=th[hh][:, 0:n], func=AF.Ln,
                                           scale=c1[:, hh:hh + 1], bias=c0[:, hh:hh + 1]),
             reads=[th_b[hh], const_b], writes=[lf_b])
        P.op("dve", lambda e: e.tensor_tensor_scan(out=bb[:, 0:n], data0=rm[:, 0:n], data1=lf[:, 0:n], initial=0.0,
                                                   op0=ALU.mult, op1=ALU.add),
             reads=[lf_b, const_b], writes=[bb_b])
        P.op("act", lambda e: e.activation(out=eb[:, 0:n], in_=bb[:, 0:n], func=AF.Exp), reads=[bb_b], writes=[eb_b])
        P.op("act", lambda e: e.activation(out=enb[:, 0:n], in_=bb[:, 0:n], func=AF.Exp, scale=-1.0),
             reads=[bb_b], writes=[enb_b])
        P.op("dve", lambda e: e.scalar_tensor_tensor(out=qdT[hh][:, 0:n], in0=sq[hh][:, 0:n], scalar=128.0 ** -0.5,
                                                     in1=eb[:, 0:n], op0=ALU.mult, op1=ALU.mult),
             reads=[sq_b[hh], eb_b], writes=[qdT_b[hh]])
        P.op("dve", lambda e: e.tensor_copy(out=ebl[:, hh, 0:nch],
                                            in_=eb[:, 0:n].rearrange("p (c t) -> p c t", t=C)[:, :, C - 1]),
             reads=[eb_b], writes=[ebl_b[hh]])
        P.op("dve", lambda e: e.tensor_scalar(out=th[hh][:, 0:n], in0=th[hh][:, 0:n], scalar1=nc1[:, hh:hh + 1],
                                              scalar2=c1[:, hh:hh + 1], op0=ALU.mult, op1=ALU.add),
             reads=[th_b[hh], const_b], writes=[th_b[hh]])
        P.op("dve", lambda e: e.tensor_tensor(out=kdT[hh][:, 0:n], in0=th[hh][:, 0:n], in1=enb[:, 0:n], op=ALU.mult),
             reads=[th_b[hh], enb_b], writes=[kdT_b[hh]])

    def hgrn_g2(G):
        for hh in range(4):
            tb_ = 6 + (hh % 2)
            bv = bank_bf(tb_)
            P.pe_group([(lambda e, k=k, col0=col0: e.transpose(out=bv[:, k * 128:(k + 1) * 128], in_=kdT[hh][:, col0:col0 + 128],
                                                               identity=ident[:, :])) for k, (col0, ntok) in enumerate(G["stl"])],
                       reads=[kdT_b[hh], const_b], writes=[bank_b[tb_]])
            P.op("act", lambda e: e.activation(out=kdtok[hh][:, 0:NT, :],
                                               in_=bv[:, 0:NT * 128].rearrange("p (i k) -> p i k", k=128), func=AF.Copy),
                 reads=[bank_b[tb_]], writes=[kdtok_b[hh]])

    def hgrn_side_steps(G):
        n = G["n"]
        C = 64
        nch = n // C
        S_all = S[:, :, :].rearrange("p h v -> p (h v)")
        Sbf_all = S_bf[:, :, :].rearrange("p h v -> p (h v)")

        def emit_A(j, ba):
            cols = slice(j * C, (j + 1) * C)
            base = (j % 2) * 64
            P.pe_group([mm(banks[ba][base:base + C, 256 + hh * 64:256 + hh * 64 + C], kdT[hh][:, cols], qdT[hh][:, cols], True, True)
                        for hh in range(4)], reads=kdT_b + qdT_b, writes=[bank_b[ba]])
            P.op("dve", lambda e: e.tensor_tensor(
                out=AmA[base:base + C, :].rearrange("p (h t) -> p h t", t=64),
                in0=banks[ba][base:base + C, 256:512].rearrange("p (h t) -> p h t", t=64),
                in1=causal[base:base + C, 0:C].unsqueeze(1).to_broadcast([C, 4, C]), op=ALU.mult),
                reads=[bank_b[ba], const_b], writes=[AmA_b])

        def chunk(j, ba):
            if j == 0:
                emit_A(0, ba)
            cols = slice(j * C, (j + 1) * C)
            k = j // 2
            base = (j % 2) * 64
            oc = 0
            fns = []
            for hh in range(4):
                vtok = ia_tok[base:base + C, k, hh * 128:(hh + 1) * 128]
                fns.append(mm(banks[ba][:, oc + hh * 64:oc + hh * 64 + C], S_bf[:, hh, :], qdT[hh][:, cols], True, False))
                fns.append(mm(banks[ba][:, oc + hh * 64:oc + hh * 64 + C], vtok, AmA[base:base + C, hh * 64:hh * 64 + C], False, True))
                fns.append(mm(banks[ba + 1][:, hh * 128:(hh + 1) * 128], kdtok[hh][base:base + C, k, :], vtok, True, True))
            P.pe_group(fns, reads=Sbf_b + qdT_b + kdtok_b + [ia_b, AmA_b], writes=[bank_b[ba], bank_b[ba + 1]])
            if j + 1 < nch:
                emit_A(j + 1, ba)
            P.op("act", lambda e: e.activation(out=oa_sb[:, :, cols],
                                               in_=banks[ba][:, oc:oc + 256].rearrange("p (h t) -> p h t", t=64), func=AF.Copy),
                 reads=[bank_b[ba]], writes=[oa_b])
            P.op("dve", lambda e: e.tensor_tensor(out=StA[:, :], in0=banks[ba + 1][:, :], in1=S_all, op=ALU.add),
                 reads=[bank_b[ba + 1]] + S_b, writes=[StA_b])
            P.op("dve", lambda e: e.tensor_tensor(out=S[:, :, :], in0=StA[:, :].rearrange("p (h v) -> p h v", v=128),
                                                  in1=ebl[:, :, j:j + 1].to_broadcast([128, 4, 128]), op=ALU.mult),
                 reads=[StA_b] + ebl_b, writes=S_b)
            P.op("dve", lambda e: e.tensor_tensor(out=S_bf[:, :, :], in0=StA[:, :].rearrange("p (h v) -> p h v", v=128),
                                                  in1=ebl[:, :, j:j + 1].to_broadcast([128, 4, 128]), op=ALU.mult),
                 reads=[StA_b] + ebl_b, writes=Sbf_b)

        def hsq(hh, ba):
            P.op("dve", lambda e: e.tensor_tensor(out=qdT[hh][:, 0:n], in0=oa_sb[:, hh, 0:n], in1=oa_sb[:, hh, 0:n], op=ALU.mult),
                 reads=[oa_b], writes=[qdT_b[hh]])

        def hout(hh, ba):
            osq = qdT[hh]
            osq_b = qdT_b[hh]
            P.pe_group([mm(banks[ba][:, 0:n], ones[:, :], osq[:, 0:n], True, True)], reads=[osq_b, const_b], writes=[bank_b[ba]])
            P.op("act", lambda e: e.activation(out=lnv[:, 0:n], in_=banks[ba][:, 0:n], func=AF.Ln, scale=1.0 / 128, bias=EPS),
                 reads=[bank_b[ba]], writes=[lnv_b])
            P.op("act", lambda e: e.activation(out=rsv[:, 0:n], in_=lnv[:, 0:n], func=AF.Exp, scale=-0.5),
                 reads=[lnv_b], writes=[rsv_b])
            P.op("dve", lambda e: e.tensor_tensor(out=t1[:, 0:n], in0=oa_sb[:, hh, 0:n], in1=rsv[:, 0:n], op=ALU.mult),
                 reads=[oa_b, rsv_b], writes=[t1_b])
            P.op("dve", lambda e: e.scalar_tensor_tensor(out=mixT[:, hh, 0:n], in0=t1[:, 0:n], scalar=gon[:, hh:hh + 1],
                                                         in1=gs[hh][:, 0:n], op0=ALU.mult, op1=ALU.mult),
                 reads=[t1_b, gs_b[hh], const_b], writes=[mixT_b])

        steps = [(lambda ba, j=j: chunk(j, ba)) for j in range(nch)] + [(lambda ba, hh=hh: hsq(hh, ba)) for hh in range(4)] + \
            [(lambda ba, hh=hh: hout(hh, ba)) for hh in range(4)]
        return steps

    def phase_wo(G):
        for half in range(2):
            sl, slb = get_w(("wo", half))
            for k, (col0, ntok) in enumerate(G["dt"]):
                bk = (half * len(G["dt"]) + k) % 4
                P.pe_group([mm(banks[bk][:ntok, :], mixT[:, kc, col0:col0 + ntok], sl[:, kc, :], kc == 0, kc == 7)
                            for kc in range(8)], reads=[slb, mixT_b], writes=[bank_b[bk]])
                hv = h[:ntok, k, half * 512:(half + 1) * 512]
                P.op("dve", lambda e: e.tensor_tensor(out=hv, in0=banks[bk][:ntok, :], in1=hv, op=ALU.add),
                     reads=[bank_b[bk], h_b[k]], writes=[h_b[k]])

    def phase_ffn(G, sample, seq, last_st, ptok0=None):
        n = G["n"]
        nsq = NSEQ_S if sample else 1
        L = n // nsq
        ui = 0
        for g0 in range(0, NFF, 4):
            ncc = min(4, NFF - g0)
            if ptok0 is not None and g0 == 0:
                p_load(G, False, ptok0)
            if ptok0 is not None and g0 == 12:
                p_transposes(G, False, ptok0)
            slg, slgb = get_w(("gate", g0))
            slu, slub = get_w(("up", g0), oldest=ws["last"])
            for cc in range(ncc):
                c = g0 + cc
                bg = (2 * c) % 8; bu = bg + 1
                P.pe_group([mm(banks[bg][:, 0:n], slg[:, kc, cc * 128:(cc + 1) * 128], aT[:, kc, 0:n], kc == 0, kc == 7)
                            for kc in range(8)], reads=[slgb, aT_b], writes=[bank_b[bg]])
                P.pe_group([mm(banks[bu][:, 0:n], slu[:, kc, cc * 128:(cc + 1) * 128], aT[:, kc, 0:n], kc == 0, kc == 7)
                            for kc in range(8)], reads=[slub, aT_b], writes=[bank_b[bu]])
                u = uext[ui % 2]; ub = uext_b[ui % 2]
                cv_ = ccv[ui % 2]; cb_ = ccv_b[ui % 2]
                sv = scv[ui % 2]; svb = scv_b[ui % 2]
                ui += 1
                u3 = u[:, 0:nsq * (L + 2)].rearrange("p (s l) -> p s l", l=L + 2)
                c3 = cv_[:, 0:n].rearrange("p (s l) -> p s l", l=L)
                ps3 = banks[bg][:, 0:n].rearrange("p (s l) -> p s l", l=L)
                P.op("act", lambda e: e.activation(out=u3[:, :, 2:L + 2], in_=ps3, func=AF.Copy), reads=[bank_b[bg]], writes=[ub])
                P.op("act", lambda e: e.activation(out=cv_[:, 0:n], in_=banks[bg][:, 0:n], func=AF.Identity,
                                                   scale=convw[:, c, 2:3], bias=convb[:, c:c + 1]),
                     reads=[bank_b[bg], const_b], writes=[cb_])
                P.op("dve", lambda e: e.tensor_copy(out=u3[:, :, 0:2], in_=convhist[:, c, 0:nsq, :]),
                     reads=[convhist_b[c]], writes=[ub])
                P.op("dve", lambda e: e.tensor_copy(out=convhist[:, c, 0:nsq, :], in_=u3[:, :, L:L + 2]),
                     reads=[ub], writes=[convhist_b[c]])
                P.op("dve", lambda e: e.scalar_tensor_tensor(out=c3, in0=u3[:, :, 1:L + 1], scalar=convw[:, c, 1:2], in1=c3,
                                                             op0=ALU.mult, op1=ALU.add),
                     reads=[ub, cb_, const_b], writes=[cb_])
                P.op("dve", lambda e: e.scalar_tensor_tensor(out=c3, in0=u3[:, :, 0:L], scalar=convw[:, c, 0:1], in1=c3,
                                                             op0=ALU.mult, op1=ALU.add),
                     reads=[ub, cb_, const_b], writes=[cb_])
                P.op("act", lambda e: e.activation(out=sv[:, 0:n], in_=cv_[:, 0:n], func=AF.Silu), reads=[cb_], writes=[svb])
                P.op("dve", lambda e: e.tensor_tensor(out=gT[:, c, 0:n], in0=sv[:, 0:n], in1=banks[bu][:, 0:n], op=ALU.mult),
                     reads=[svb, bank_b[bu]], writes=[gT_b])
                if sample:
                    for s_ in range(NSEQ_S):
                        P.dma("sp", conv_s[s_, :, c, :], convhist[:, c, s_, :], reads=[convhist_b[c]])
                elif last_st:
                    P.dma("sp", conv_p[seq, :, c, :], convhist[:, c, 0, :], reads=[convhist_b[c]])
        ndt = len(G["dt"])
        for half in range(2):
            accb = [half * 4 + k for k in range(ndt)]
            for g0 in range(0, NFF, 8):
                ncc = min(8, NFF - g0)
                sld, sldb = get_w(("down", half, g0))
                fns = []
                for cc in range(ncc):
                    c = g0 + cc
                    for k, (col0, ntok) in enumerate(G["dt"]):
                        fns.append(mm(banks[accb[k]][:ntok, :], gT[:, c, col0:col0 + ntok], sld[:, cc, :], c == 0, c == NFF - 1))
                P.pe_group(fns, reads=[sldb, gT_b], writes=[bank_b[b] for b in accb])
            for k, (col0, ntok) in enumerate(G["dt"]):
                hv = h[:ntok, k, half * 512:(half + 1) * 512]
                P.op("dve", lambda e: e.tensor_tensor(out=hv, in0=banks[accb[k]][:ntok, :], in1=hv, op=ALU.add),
                     reads=[bank_b[accb[k]], h_b[k]], writes=[h_b[k]])

    def p_load(G, sample, tok0):
        P.dma("pool", wpp_sb[:, 0:2, :], w_pp[:, 0:512].rearrange("(kc p) n -> p kc n", p=128), writes=[wpp_b])
        P.dma("pool", wpp_sb[:, 2:4, :], w_pp[:, 512:1024].rearrange("(kc p) n -> p kc n", p=128), writes=[wpp_b])
        for k, (col0, ntok) in enumerate(G["dt"]):
            src = (p_s if sample else p_p)[tok0 + col0: tok0 + col0 + ntok, :]
            P.dma("pool", pb[:ntok, k, :], src, writes=[pb_b])

    def p_transposes(G, sample, tok0):
        for k, (col0, ntok) in enumerate(G["dt"]):
            tb_ = 6 + (k % 2)
            bv = bank_bf(tb_)
            P.pe_group([(lambda e, c=c: e.transpose(out=bv[:, c * 128:c * 128 + ntok], in_=pb[:ntok, k, c * 128:(c + 1) * 128],
                                                    identity=ident[:ntok, :ntok])) for c in range(2)],
                       reads=[pb_b, const_b], writes=[bank_b[tb_]])
            P.op("act", lambda e: e.activation(out=pT[:, :, col0:col0 + ntok],
                                               in_=bv[:, 0:256].rearrange("p (c t) -> p c t", t=128)[:, :, 0:ntok], func=AF.Copy),
                 reads=[bank_b[tb_]], writes=[pT_b])

    def phase_ple_final_prompt(G, tok0, has_next):
        slp, slpb = wpp_sb, wpp_b
        ti = 0
        u_ = 0
        dt_ = G["dt"]
        nsis = {}
        P.handoff([gT_b], yst_b)
        for half in range(2):
            sl, slb = get_w(("pg", half))
            for k, (col0, ntok) in enumerate(dt_):
                bg = (2 * u_) % 6; bp = bg + 1
                u_ += 1
                P.pe_group([mm(banks[bg][:ntok, :], aT[:, kc, col0:col0 + ntok], sl[:, kc, :], kc == 0, kc == 7)
                            for kc in range(8)], reads=[slb, aT_b], writes=[bank_b[bg]])
                P.pe_group([mm(banks[bp][:ntok, :], pT[:, kc, col0:col0 + ntok], slp[:, 2 * half + kc, :], kc == 0, kc == 1)
                            for kc in range(2)], reads=[slpb, pT_b], writes=[bank_b[bp]])
                tgv = tg[ti % 2]; tgb = tg_b[ti % 2]; t2v = t2[ti % 2]; t2b = t2_b[ti % 2]
                ti += 1
                P.op("act", lambda e: e.activation(out=tgv[:ntok, :], in_=banks[bg][:ntok, :], func=AF.Tanh, scale=0.5),
                     reads=[bank_b[bg]], writes=[tgb])
                P.op("dve", lambda e: e.scalar_tensor_tensor(out=t2v[:ntok, :], in0=tgv[:ntok, :], scalar=1.0, in1=banks[bp][:ntok, :],
                                                             op0=ALU.add, op1=ALU.mult),
                     reads=[tgb, bank_b[bp]], writes=[t2b])
                cs_ = slice(half * 512, (half + 1) * 512)
                P.op("dve", lambda e: e.scalar_tensor_tensor(out=yst[:ntok, k, cs_], in0=t2v[:ntok, :], scalar=0.5, in1=h[:ntok, k, cs_],
                                                             op0=ALU.mult, op1=ALU.add),
                     reads=[t2b, h_b[k]], writes=[yst_b[k]])
                if half == 1 and has_next:
                    nt0 = tok0 + T
                    P.dma("sp", h[:, k, :], x_p[nt0 + k * 128: nt0 + (k + 1) * 128, :], writes=[h_b[k]])
                    if k >= 2:
                        kk = k - 2
                        nsis[kk] = norm_A(kk, 128)
                        norm_B(kk, 128, nsis[kk])
        if has_next:
            norm_C(0, 128)
            norm_C(1, 128)
            for kk in (2, 3):
                nsis[kk] = norm_A(kk, 128)
            norm_D(0, 0, 128, 0)
            norm_B(2, 128, nsis[2])
            norm_C(2, 128)
            norm_D(1, 128, 128, 0)
            norm_B(3, 128, nsis[3])
            norm_C(3, 128)
            norm_D(2, 256, 128, 0)
            norm_D(3, 384, 128, 0)
        for k, (col0, ntok) in enumerate(dt_):
            si = ss_n[0] % 16
            ss_n[0] += 1
            P.op("act", lambda e: e.activation(out=junk[:ntok, :], in_=yst[:ntok, k, :], func=AF.Square,
                                               accum_out=ss[:ntok, si:si + 1]),
                 reads=[yst_b[k]], writes=[junk_b, ss_b[si]])
            P.op("dve", lambda e: e.tensor_scalar(out=ms[:ntok, si:si + 1], in0=ss[:ntok, si:si + 1], scalar1=1.0 / D,
                                                  scalar2=EPS, op0=ALU.mult, op1=ALU.add),
                 reads=[ss_b[si]], writes=[ss_b[si]])
            P.op("pool", lambda e: e.tensor_tensor(out=rstd[:ntok, si:si + 1], in0=ms[:ntok, si:si + 1],
                                                   in1=neghalf[:ntok, 0:1], op=ALU.pow),
                 reads=[ss_b[si], const_b], writes=[ss_b[si]])
            P.op("dve", lambda e: e.scalar_tensor_tensor(out=yst[:ntok, k, :], in0=yst[:ntok, k, :], scalar=rstd[:ntok, si:si + 1],
                                                         in1=gfin[:ntok, :], op0=ALU.mult, op1=ALU.mult),
                 reads=[yst_b[k], ss_b[si], const_b], writes=[yst_b[k]])
            P.dma("sp", y_p[tok0 + col0: tok0 + col0 + ntok, :], yst[:ntok, k, :], reads=[yst_b[k]])

    def phase_ple(G, sample, tok0):
        p_load(G, sample, tok0)
        p_transposes(G, sample, tok0)
        slp, slpb = wpp_sb, wpp_b
        ti = 0
        for half in range(2):
            sl, slb = get_w(("pg", half))
            for k, (col0, ntok) in enumerate(G["dt"]):
                bg = (2 * (half * len(G["dt"]) + k)) % 8; bp = bg + 1
                P.pe_group([mm(banks[bg][:ntok, :], aT[:, kc, col0:col0 + ntok], sl[:, kc, :], kc == 0, kc == 7)
                            for kc in range(8)], reads=[slb, aT_b], writes=[bank_b[bg]])
                P.pe_group([mm(banks[bp][:ntok, :], pT[:, kc, col0:col0 + ntok], slp[:, 2 * half + kc, :], kc == 0, kc == 1)
                            for kc in range(2)], reads=[slpb, pT_b], writes=[bank_b[bp]])
                tgv = tg[ti % 2]; tgb = tg_b[ti % 2]; t2v = t2[ti % 2]; t2b = t2_b[ti % 2]
                ti += 1
                P.op("act", lambda e: e.activation(out=tgv[:ntok, :], in_=banks[bg][:ntok, :], func=AF.Tanh, scale=0.5),
                     reads=[bank_b[bg]], writes=[tgb])
                P.op("dve", lambda e: e.scalar_tensor_tensor(out=t2v[:ntok, :], in0=tgv[:ntok, :], scalar=1.0, in1=banks[bp][:ntok, :],
                                                             op0=ALU.add, op1=ALU.mult),
                     reads=[tgb, bank_b[bp]], writes=[t2b])
                hv = h[:ntok, k, half * 512:(half + 1) * 512]
                P.op("dve", lambda e: e.scalar_tensor_tensor(out=hv, in0=t2v[:ntok, :], scalar=0.5, in1=hv, op0=ALU.mult, op1=ALU.add),
                     reads=[t2b, h_b[k]], writes=[h_b[k]])

    def phase_final(G, sample, tok0):
        for k, (col0, ntok) in enumerate(G["dt"]):
            si = ss_n[0] % 16
            ss_n[0] += 1
            P.op("act", lambda e: e.activation(out=junk[:ntok, :], in_=h[:ntok, k, :], func=AF.Square, accum_out=ss[:ntok, si:si + 1]),
                 reads=[h_b[k]], writes=[junk_b, ss_b[si]])
            P.op("dve", lambda e: e.tensor_scalar(out=ms[:ntok, si:si + 1], in0=ss[:ntok, si:si + 1], scalar1=1.0 / D, scalar2=EPS,
                                                  op0=ALU.mult, op1=ALU.add), reads=[ss_b[si]], writes=[ss_b[si]])
            P.op("pool", lambda e: e.tensor_tensor(out=rstd[:ntok, si:si + 1], in0=ms[:ntok, si:si + 1], in1=neghalf[:ntok, 0:1],
                                                   op=ALU.pow), reads=[ss_b[si], const_b], writes=[ss_b[si]])
            P.op("dve", lambda e: e.scalar_tensor_tensor(out=h[:ntok, k, :], in0=h[:ntok, k, :], scalar=rstd[:ntok, si:si + 1],
                                                         in1=gfin[:ntok, :], op0=ALU.mult, op1=ALU.mult),
                 reads=[h_b[k], ss_b[si], const_b], writes=[h_b[k]])
            dst = (y_s if sample else y_p)[tok0 + col0: tok0 + col0 + ntok, :]
            P.dma("sp", dst, h[:ntok, k, :], reads=[h_b[k]])

    Gp = {"n": T, "dt": [(i * 128, 128) for i in range(NT)], "stl": [(i * 128, 128) for i in range(NT)]}
    for seq in range(NSEQ_P):
        for hh in range(4):
            P.op("dve", lambda e, hh=hh: e.memset(S[:, hh, :], 0.0), writes=[S_b[hh]])
            P.op("dve", lambda e, hh=hh: e.memset(S_bf[:, hh, :], 0.0), writes=[Sbf_b[hh]])
        for c in range(NFF):
            P.op("pool", lambda e, c=c: e.memset(convhist[:, c, :, :], 0.0), writes=[convhist_b[c]])
        for st in range(NST):
            tok0 = seq * SEQ + st * T
            first = (seq == 0 and st == 0)
            last = (seq == NSEQ_P - 1 and st == NST - 1)
            P.mark('norm0')
            if first:
                for k in range(NT):
                    P.dma("sp", h[:, k, :], x_p[tok0 + k * 128: tok0 + (k + 1) * 128, :], writes=[h_b[k]])
                norm_T(Gp, 0)
            P.handoff(regH_B, regH_A)
            P.handoff(regM_B, regM_A)
            P.mark('win')
            phase_win(Gp, False, seq, st)
            P.mark('gates')
            hgrn_g2(Gp)
            P.handoff([lf_b, bb_b, eb_b, enb_b], [oa_b])
            P.mark('attn')
            phase_attn_prompt(Gp, st, hgrn_side_steps(Gp))
            P.handoff([oa_b], [lf_b, bb_b, eb_b, enb_b])
            P.mark('wo')
            phase_wo_norm1(Gp)
            P.mark('norm1')
            P.handoff(regH_A, regH_B)
            P.handoff(regM_A, regM_B)
            P.mark('ffn')
            phase_ffn(Gp, False, seq, st == NST - 1, ptok0=tok0)
            P.mark('norm2')
            norm_T(Gp, 2)
            P.mark('ple')
            phase_ple_final_prompt(Gp, tok0, not last)
            P.mark('final')
        for hh in range(4):
            P.dma("sp", st_p[seq, hh, :, :], S[:, hh, :], reads=[S_b[hh]])

    P.mark('sample')
    Gs = {"n": NSEQ_S * TS, "dt": [(0, NSEQ_S * TS)], "stl": [(s_ * TS, TS) for s_ in range(NSEQ_S)]}
    for s_ in range(NSEQ_S):
        P.dma("sp", ck_s[s_, 0:WB - TS, :], ck_in[s_, TS:WB, :])
        P.dma("sp", cv_s[s_, 0:WB - TS, :], cv_in[s_, TS:WB, :])
    for c in range(NFF):
        P.dma("sp", convhist[:, c, :, :], conv_in[:, c, :, :], writes=[convhist_b[c]])
    P.dma("sp", h[0:NSEQ_S * TS, 0, :], x_s[:, :], writes=[h_b[0]])
    norm_T(Gs, 0)
    P.handoff(regH_B, regH_A)
    P.handoff(regM_B, regM_A)
    n = Gs["n"]
    for blk, kind in ((1, "f"), (0, "q"), (3, "g")):
        sl, slb = get_w(("win", blk))
        for hh in range(4):
            bk = hh
            P.pe_group([mm(banks[bk][:, 0:n], sl[:, kc, hh * 128:(hh + 1) * 128], aT[:, kc, 0:n], kc == 0, kc == 7)
                        for kc in range(8)], reads=[slb, aT_b], writes=[bank_b[bk]])
            if kind == "f":
                P.op("act", lambda e: e.activation(out=th[hh][:, 0:n], in_=banks[bk][:, 0:n], func=AF.Tanh, scale=0.5),
                     reads=[bank_b[bk]], writes=[th_b[hh]])
            elif kind == "q":
                P.op("act", lambda e: e.activation(out=sq[hh][:, 0:n], in_=banks[bk][:, 0:n], func=AF.Silu),
                     reads=[bank_b[bk]], writes=[sq_b[hh]])
            else:
                P.op("act", lambda e: e.activation(out=gs[hh][:, 0:n], in_=banks[bk][:, 0:n], func=AF.Silu),
                     reads=[bank_b[bk]], writes=[gs_b[hh]])
    sl, slb = get_w(("win", 2))
    for k, (col0, ntok) in enumerate(Gs["stl"]):
        bk = k % 4
        P.pe_group([mm(banks[bk][:ntok, :], aT[:, kc, col0:col0 + ntok], sl[:, kc, :], kc == 0, kc == 7) for kc in range(8)],
                   reads=[slb, aT_b], writes=[bank_b[bk]])
        P.op("act", lambda e: e.activation(out=ia_tok[:ntok, k, :], in_=banks[bk][:ntok, :], func=AF.Copy),
             reads=[bank_b[bk]], writes=[ia_b])
    hgrn_gates(Gs, True)
    hgrn_chunks(Gs, True)
    hgrn_out(Gs)
    slq, slqb = get_w(("win", 4))
    for k, (col0, ntok) in enumerate(Gs["stl"]):
        bk = k % 4
        P.pe_group([mm(banks[bk][:ntok, :], aT[:, kc, col0:col0 + ntok], slq[:, kc, :], kc == 0, kc == 7) for kc in range(8)],
                   reads=[slqb, aT_b], writes=[bank_b[bk]])
        P.op("act", lambda e: e.activation(out=stage[0][:ntok, :], in_=banks[bk][:ntok, :], func=AF.Copy),
             reads=[bank_b[bk]], writes=[stage_b[0]])
        rotary(Gs, stage[0], css, ntok, [stage_b[0]])
        P.op("dve", lambda e: e.tensor_copy(out=qkb[0][:ntok, :], in_=stage[0][:ntok, :]), reads=[stage_b[0]], writes=[qkb_b[0]])
        bv = bank_bf(6)
        P.pe_group([(lambda e, c=c: e.transpose(out=bv[:, c * 128:c * 128 + ntok], in_=qkb[0][:ntok, c * 128:(c + 1) * 128],
                                                identity=ident[:ntok, :ntok])) for c in range(4)],
                   reads=[qkb_b[0], const_b], writes=[bank_b[6]])
        P.op("act", lambda e: e.activation(out=QT[:, :, col0:col0 + ntok],
                                           in_=bv[:, 0:512].rearrange("p (c t) -> p c t", t=128)[:, :, 0:ntok], func=AF.Copy, scale=0.125),
             reads=[bank_b[6]], writes=[QT_b])
    slk, slkb = get_w(("win", 5))
    slv, slvb = get_w(("win", 6), oldest=ws["last"])
    for s_, (col0, ntok) in enumerate(Gs["stl"]):
        P.pe_group([mm(banks[0][:ntok, :], aT[:, kc, col0:col0 + ntok], slk[:, kc, :], kc == 0, kc == 7) for kc in range(8)],
                   reads=[slkb, aT_b], writes=[bank_b[0]])
        P.op("act", lambda e: e.activation(out=stage[1][:ntok, :], in_=banks[0][:ntok, :], func=AF.Copy),
             reads=[bank_b[0]], writes=[stage_b[1]])
        rotary(Gs, stage[1], css, ntok, [stage_b[1]])
        P.dma("sp", ck_s[s_, WB - TS:WB, :], stage[1][:ntok, :], reads=[stage_b[1]])
        P.op("dve", lambda e: e.tensor_copy(out=qkb[1][:ntok, :], in_=stage[1][:ntok, :]), reads=[stage_b[1]], writes=[qkb_b[1]])
        bv = bank_bf(7)
        P.pe_group([(lambda e, c=c: e.transpose(out=bv[:, c * 128:c * 128 + ntok], in_=qkb[1][:ntok, c * 128:(c + 1) * 128],
                                                identity=ident[:ntok, :ntok])) for c in range(4)],
                   reads=[qkb_b[1], const_b], writes=[bank_b[7]])
        P.op("act", lambda e: e.activation(out=KTn[:, :, 0:ntok],
                                           in_=bv[:, 0:512].rearrange("p (c t) -> p c t", t=128)[:, :, 0:ntok], func=AF.Copy),
             reads=[bank_b[7]], writes=[KTn_b])
        P.pe_group([mm(banks[1][:ntok, :], aT[:, kc, col0:col0 + ntok], slv[:, kc, :], kc == 0, kc == 7) for kc in range(8)],
                   reads=[slvb, aT_b], writes=[bank_b[1]])
        P.op("act", lambda e: e.activation(out=stage[2][:ntok, :], in_=banks[1][:ntok, :], func=AF.Copy),
             reads=[bank_b[1]], writes=[stage_b[2]])
        P.dma("sp", cv_s[s_, WB - TS:WB, :], stage[2][:ntok, :], reads=[stage_b[2]])
        P.op("dve", lambda e: e.tensor_copy(out=Vn[:ntok, :], in_=stage[2][:ntok, :]), reads=[stage_b[2]], writes=[Vn_b])
        if s_ == 0:
            sKT_b = [Buf("sKT%d" % i) for i in range(7)]; sVH_b = [Buf("sVH%d" % i) for i in range(7)]
            P.handoff([KT_b], sKT_b)
            P.handoff([VH_b], sVH_b)
        for sl_ in range(7):
            ci = sl_ % 2
            if sl_ < 3:
                for r_ in range(4):
                    srck = ck_in[s_, 512 * sl_:512 * (sl_ + 1), :].rearrange("(m r) c -> r m c", r=16)[r_]
                    srcv = cv_in[s_, 512 * sl_:512 * (sl_ + 1), :].rearrange("(m r) c -> r m c", r=16)[r_]
                    P.dma("pool", cbf[ci][32 * r_:32 * (r_ + 1), :], srck, writes=[cbf_b[ci]])
                    P.dma("pool", VH[32 * r_:32 * (r_ + 1), sl_, :], srcv, writes=[sVH_b[sl_]])
            else:
                kt = 9 + sl_
                P.dma("pool", cbf[ci][:, :], ck_in[s_, kt * 128:(kt + 1) * 128, :], writes=[cbf_b[ci]])
                P.dma("pool", VH[:, sl_, :], cv_in[s_, kt * 128:(kt + 1) * 128, :], writes=[sVH_b[sl_]])
            tb_ = 6 + (sl_ % 2)
            bv = bank_bf(tb_)
            P.pe_group([(lambda e, c=c: e.transpose(out=bv[:, c * 128:(c + 1) * 128], in_=cbf[ci][:, c * 128:(c + 1) * 128],
                                                    identity=ident[:, :])) for c in range(4)],
                       reads=[cbf_b[ci], const_b], writes=[bank_b[tb_]])
            P.op("act", lambda e: e.activation(out=KT[:, :, sl_ * 128:(sl_ + 1) * 128],
                                               in_=bv[:, 0:512].rearrange("p (c t) -> p c t", t=128), func=AF.Copy),
                 reads=[bank_b[tb_]], writes=[sKT_b[sl_]])
        qc = slice(col0, col0 + TS)
        sitems = []
        for sl_ in range(7):
            mi = 17 if sl_ < 3 else 16 - (9 + sl_)
            sitems.append((lambda pr, e2, sl_=sl_: KT[64 * e2:64 * e2 + 64, pr, sl_ * 128:(sl_ + 1) * 128],
                           lambda hd, sl_=sl_: VH[:, sl_, hd * 64:(hd + 1) * 64],
                           maskr[:, mi, 0:TS], 128, [sKT_b[sl_], sVH_b[sl_]]))
        sitems.append((lambda pr, e2: KTn[64 * e2:64 * e2 + 64, pr, 0:TS],
                       lambda hd: Vn[0:TS, hd * 64:(hd + 1) * 64],
                       maskr[0:TS, 0, 0:TS], TS, [KTn_b, Vn_b]))
        pend = []

        def s_qk(idx, it):
            ktf, vf, m_ap, nk, rd = it
            i2 = idx % 2
            sbk = 2 + 3 * i2
            sc2 = psum_all[:, sbk * 512:(sbk + 2) * 512].rearrange("p (e n) -> p e n", n=512)
            P.pe_group([mm(sc2[:nk, e2, pr * TS:(pr + 1) * TS], ktf(pr, e2), QT[64 * e2:64 * e2 + 64, pr, qc], True, True)
                        for pr in range(4) for e2 in range(2)], reads=rd + [QT_b], writes=[bank_b[sbk], bank_b[sbk + 1]])
            P.op("act", lambda e: e.activation(out=Pexp[i2][:nk, :, 0:16], in_=sc2[:nk, :, 0:16], func=AF.Exp),
                 reads=[bank_b[sbk], bank_b[sbk + 1]], writes=[Pexp_b[i2]])
            P.op("dve", lambda e: e.tensor_tensor(out=Pm[i2][:nk, :, 0:16].rearrange("p e (a t) -> p e a t", t=TS),
                                                  in0=Pexp[i2][:nk, :, 0:16].rearrange("p e (a t) -> p e a t", t=TS),
                                                  in1=m_ap.unsqueeze(1).unsqueeze(1).to_broadcast([nk, 2, 4, TS]), op=ALU.mult),
                 reads=[Pexp_b[i2], const_b], writes=[Pm_b[i2]])

        def s_pv(idx, it):
            ktf, vf, m_ap, nk, rd = it
            i2 = idx % 2
            obk = 4 + 3 * i2
            fns = []
            for pr in range(4):
                for e2 in range(2):
                    hd = 2 * pr + e2
                    rhs = Pm[i2][:nk, e2, pr * TS:(pr + 1) * TS]
                    fns.append(mm(banks[obk][64 * e2:64 * e2 + 64, pr * TS:(pr + 1) * TS], vf(hd), rhs, True, True))
                    fns.append(mm(banks[obk][64 * e2:64 * e2 + 64, 16 + pr * TS:16 + (pr + 1) * TS], ones[:nk, 0:64], rhs, True, True))
            P.pe_group(fns, reads=rd + [Pm_b[i2], const_b], writes=[bank_b[obk]])
            if idx == 0:
                P.op("dve", lambda e: e.tensor_copy(out=sacc[:, :], in_=banks[obk][:, 0:32]), reads=[bank_b[obk]], writes=[sacc_b])
            else:
                P.op("dve", lambda e: e.tensor_tensor(out=sacc[:, :], in0=banks[obk][:, 0:32], in1=sacc[:, :], op=ALU.add),
                     reads=[bank_b[obk], sacc_b], writes=[sacc_b])

        for idx, it in enumerate(sitems):
            s_qk(idx, it)
            pend.append((idx, it))
            if len(pend) > 1:
                s_pv(*pend.pop(0))
        while pend:
            s_pv(*pend.pop(0))
        P.op("dve", lambda e: e.reciprocal(out=sacc[:, 16:32], in_=sacc[:, 16:32]), reads=[sacc_b], writes=[sacc_b])
        P.op("dve", lambda e: e.tensor_tensor(out=mixT[:, 4:8, qc], in0=sacc[:, 0:16].rearrange("p (a t) -> p a t", t=TS),
                                              in1=sacc[:, 16:32].rearrange("p (a t) -> p a t", t=TS), op=ALU.mult),
             reads=[sacc_b], writes=[mixT_b])
    phase_wo(Gs)
    norm_T(Gs, 1)
    P.handoff(regH_A, regH_B)
    P.handoff(regM_A, regM_B)
    phase_ffn(Gs, True, 0, True)
    norm_T(Gs, 2)
    phase_ple(Gs, True, 0)
    phase_final(Gs, True, 0)

    P.mark('end')
    P.final_wait()
    return nc, P


_CACHE = {}


def _consts():
    maskr = np.zeros((20, 128, 128), np.float32)
    kk = np.arange(128)[:, None]
    qq = np.arange(128)[None, :]
    for off in range(17):
        diff = 128 * off + qq - kk
        m = np.zeros((128, 128), np.float32)
        for dil in (1, 4, 16):
            m += ((diff >= 0) & (diff % dil == 0) & (diff <= 128 * dil)).astype(np.float32)
        maskr[off] = m
    for t_ in range(TS):
        maskr[17, 32 * t_:32 * (t_ + 1), t_] = 1.0
    ii = np.arange(128)[:, None]; jj = np.arange(128)[None, :]
    maskr[18] = ((ii - jj) % 4 == 0).astype(np.float32)
    maskr[19] = (((ii - jj) % 4 == 0) & (jj <= ii)).astype(np.float32)
    maskr = np.ascontiguousarray(maskr.transpose(1, 0, 2))
    s = np.arange(64)[:, None]
    t = np.arange(64)[None, :]
    cz = (s <= t).astype(np.float32)
    causal = np.concatenate([cz, cz], axis=0)
    half = 8
    inv_freq = np.power(np.float32(500000.0), -np.arange(half, dtype=np.float32) * np.float32(2.0 / 16)).astype(np.float32)

    def cs(pos):
        ang = (pos.astype(np.float32)[:, None] * inv_freq[None, :]).astype(np.float32)
        return np.concatenate([np.cos(ang), np.sin(ang)], axis=1).astype(np.float32)

    csp = cs(np.arange(SEQ)).reshape(SEQ // 128, 128, 16).transpose(1, 0, 2)
    css = cs(PAST + np.arange(TS))
    return maskr, causal, np.ascontiguousarray(csp), np.ascontiguousarray(css)


def kernel(x_prompt, x_sample, state_hgrn, cache_win_k, cache_win_v, state_ffn_conv, p_prompt, p_sample,
           norm_attn_g, w_in, hgrn_lb_logits, hgrn_onorm_g, w_o, norm_ffn_g, w_gate, w_up, conv_w, conv_b,
           w_down, norm_ple_g, w_ple_gate, w_ple_proj, norm_final_g):
    f = lambda a: np.ascontiguousarray(np.asarray(a, dtype=np.float32))
    if "nc" not in _CACHE:
        _CACHE["nc"] = build_program()[0]
        _CACHE["consts"] = _consts()
    nc = _CACHE["nc"]
    maskr, causal, csp, css = _CACHE["consts"]
    x_prompt = f(x_prompt); x_sample = f(x_sample); p_prompt = f(p_prompt); p_sample = f(p_sample)
    state_hgrn = f(state_hgrn); cache_win_k = f(cache_win_k); cache_win_v = f(cache_win_v)
    state_ffn_conv = f(state_ffn_conv)

    def cols(g, nchunk):
        return np.ascontiguousarray(f(g).reshape(nchunk, 128).T)

    gcols = np.ascontiguousarray(np.stack([cols(norm_attn_g[0], 8), cols(norm_ffn_g[0], 8), cols(norm_ple_g[0], 8)], axis=1))
    gfin = np.ascontiguousarray(np.broadcast_to(f(norm_final_g)[None, :], (128, D)))
    lbl = np.ascontiguousarray(f(hgrn_lb_logits).reshape(2, 4, 128).transpose(2, 0, 1))
    gon = cols(hgrn_onorm_g[0], 4)
    convw = np.ascontiguousarray(f(conv_w)[0].reshape(3, NFF, 128).transpose(2, 1, 0))
    convb = cols(conv_b[0], NFF)
    shared = {
        "w_in": f(w_in)[0], "w_o": f(w_o)[0], "w_gate": f(w_gate)[0], "w_up": f(w_up)[0], "w_down": f(w_down)[0],
        "w_pg": f(w_ple_gate)[0], "w_pp": f(w_ple_proj)[0], "gcols": gcols, "gfin": gfin, "lbl": lbl, "gon": gon,
        "convw": convw, "convb": convb, "maskr": maskr, "causal": causal, "csp": csp, "css": css,
    }
    in_maps = []
    for c in range(NCORES):
        ps = slice(NSEQ_P * c, NSEQ_P * (c + 1))
        ss_ = slice(NSEQ_S * c, NSEQ_S * (c + 1))
        m = dict(shared)
        m["x_p"] = x_prompt[ps].reshape(NSEQ_P * SEQ, D)
        m["p_p"] = p_prompt[0, ps].reshape(NSEQ_P * SEQ, PLE)
        m["x_s"] = x_sample[ss_].reshape(NSEQ_S * TS, D)
        m["p_s"] = p_sample[0, ss_].reshape(NSEQ_S * TS, PLE)
        m["st_in"] = state_hgrn[0, ss_]
        m["ck_in"] = cache_win_k[0, ss_].reshape(NSEQ_S, WB, 512)
        m["cv_in"] = cache_win_v[0, ss_].reshape(NSEQ_S, WB, 512)
        m["conv_in"] = np.ascontiguousarray(state_ffn_conv[0, ss_].reshape(NSEQ_S, 2, NFF, 128).transpose(3, 2, 0, 1))
        in_maps.append(m)
    res = run_bass_kernel_spmd(nc, in_maps, core_ids=list(range(NCORES)))
    R = res.results
    cat = lambda k: np.concatenate([np.asarray(r[k]) for r in R], axis=0)
    y_p = cat("y_p").reshape(16, SEQ, D)
    y_s = cat("y_s").reshape(32, TS, D)
    st_p = cat("st_p")[None]
    ck_p = cat("ck_p").reshape(1, 16, SEQ, 8, 64)
    cv_p = cat("cv_p").reshape(1, 16, SEQ, 8, 64)
    conv_p = cat("conv_p").transpose(0, 3, 2, 1).reshape(1, 16, 2, DFF)
    st_s = cat("st_s")[None]
    ck_s = cat("ck_s").reshape(1, 32, WB, 8, 64)
    cv_s = cat("cv_s").reshape(1, 32, WB, 8, 64)
    conv_s = cat("conv_s").transpose(0, 3, 2, 1).reshape(1, 32, 2, DFF)
    outs = (y_p, y_s, st_p, ck_p, cv_p, conv_p, st_s, ck_s, cv_s, conv_s)
    return tuple(np.ascontiguousarray(o, dtype=np.float32) for o in outs)
```

```python
import sys
import numpy as np
import concourse.bass as bass
import concourse.mybir as mybir
from concourse.bass_utils import run_bass_kernel_spmd

F32 = mybir.dt.float32
BF16 = mybir.dt.bfloat16
AF = mybir.ActivationFunctionType
ALU = mybir.AluOpType

NCORES = 8
D = 1024
DIN = 3584
DFF = 2816
NFF = 22
PLE = 256
SEQ = 2048
T = 512
NT = 4
NST = SEQ // T
NSEQ_P = 2
NSEQ_S = 4
TS = 4
EPS = 1e-6
WB = 2048
PAST = 16384
NSLAB = 4


class Buf:
    __slots__ = ("name", "w", "r", "excl")

    def __init__(self, name, excl=False):
        self.name = name
        self.w = None
        self.r = {}
        self.excl = excl


class Prog:
    def __init__(self, nc):
        self.nc = nc
        self.eng = {"pe": nc.tensor, "act": nc.scalar, "dve": nc.vector, "pool": nc.gpsimd, "sp": nc.sync}
        self.sem = {}
        self.cnt = {}
        for e in ("pe", "act", "dve", "pool"):
            self.sem[e] = nc.alloc_semaphore(name="sem_" + e)
            self.cnt[e] = 0
        self.known = {e: {} for e in self.eng}
        self.ring = {}
        self.ring_n = {}
        for q, k in (("sp", 12), ("pool", 6), ("spc", 8)):
            self.ring[q] = [[nc.alloc_semaphore(name="dsem_%s%d" % (q, i)), 0] for i in range(k)]
            self.ring_n[q] = 0
        self.n_ins = 0
        self.marks = []
        self.desc = {}

    def _deps(self, reads, writes, e=None):
        toks = []
        for b in reads:
            if b.w is not None:
                toks.append(b.w)
            if b.excl:
                toks.extend(t for t in b.r.values() if t[3] != e)
        for b in writes:
            if b.w is not None:
                toks.append(b.w)
            toks.extend(b.r.values())
        return toks

    def _wait(self, e, toks):
        best = {}
        for t in toks:
            key, sem, val, src = t
            if src == "pe" and e == "pe":
                continue
            if key not in best or best[key][2] < val:
                best[key] = t
        for key, (k2, sem, val, src) in best.items():
            if self.known[e].get(key, 0) >= val:
                continue
            self.eng[e].wait_ge(sem, val)
            self.known[e][key] = val
            self.n_ins += 1

    def _mark(self, tok, reads, writes):
        for b in reads:
            b.r[tok[0]] = tok
        for b in writes:
            b.w = tok
            b.r = {}

    def op(self, e, fn, reads=(), writes=()):
        self._wait(e, self._deps(reads, writes, e))
        ins = fn(self.eng[e])
        self.cnt[e] += 1
        ins.then_inc(self.sem[e], 1)
        tok = (e, self.sem[e], self.cnt[e], e)
        self._mark(tok, reads, writes)
        self.n_ins += 1
        fr = sys._getframe(1)
        self.desc[(e, self.cnt[e])] = "%s:%d" % (fr.f_code.co_name, fr.f_lineno)
        return ins

    def pe_group(self, fns, reads=(), writes=()):
        self._wait("pe", self._deps(reads, writes))
        ins = None
        for fn in fns:
            ins = fn(self.eng["pe"])
            self.n_ins += 1
        self.cnt["pe"] += 1
        ins.then_inc(self.sem["pe"], 1)
        tok = ("pe", self.sem["pe"], self.cnt["pe"], "pe")
        self._mark(tok, reads, writes)
        fr = sys._getframe(1)
        self.desc[("pe", self.cnt["pe"])] = "%s:%d" % (fr.f_code.co_name, fr.f_lineno)

    def dma(self, q, out, in_, reads=(), writes=(), ring=None, **kw):
        rq = ring or q
        ring = self.ring[rq]
        slot = ring[self.ring_n[rq] % len(ring)]
        self.ring_n[rq] += 1
        toks = self._deps(reads, writes)
        key = "d_%s_%d" % (rq, id(slot))
        if slot[1] > 0:
            toks.append((key, slot[0], slot[1], "dma"))
        self._wait(q, toks)
        if q == "pool":
            kw.setdefault("max_dma_last_dim", 4096)
        ins = self.eng[q].dma_start(out=out, in_=in_, **kw)
        slot[1] += 16
        ins.then_inc(slot[0], 16)
        tok = (key, slot[0], slot[1], "dma")
        self._mark(tok, reads, writes)
        self.n_ins += 1

    def mark(self, name):
        self.marks.append((name, dict(self.cnt)))

    def handoff(self, old, new):
        toks = {}
        for b in old:
            if b.w is not None:
                t = b.w
                if t[0] not in toks or toks[t[0]][2] < t[2]:
                    toks[t[0]] = t
            for t in b.r.values():
                if t[0] not in toks or toks[t[0]][2] < t[2]:
                    toks[t[0]] = t
        for b in new:
            for k, t in toks.items():
                if k not in b.r or b.r[k][2] < t[2]:
                    b.r[k] = t

    def final_wait(self):
        for q in self.ring:
            for slot in self.ring[q]:
                if slot[1] > 0:
                    self.eng["sp"].wait_ge(slot[0], slot[1])


def build_program():
    nc = bass.Bass("TRN2", target_bir_lowering=False)
    P = Prog(nc)

    def din(name, shape, dt=F32):
        return nc.dram_tensor(name, list(shape), dt, kind="ExternalInput").ap()

    def dout(name, shape):
        return nc.dram_tensor(name, list(shape), F32, kind="ExternalOutput").ap()

    x_p = din("x_p", [NSEQ_P * SEQ, D])
    p_p = din("p_p", [NSEQ_P * SEQ, PLE])
    x_s = din("x_s", [NSEQ_S * TS, D])
    p_s = din("p_s", [NSEQ_S * TS, PLE])
    st_in = din("st_in", [NSEQ_S, 4, 128, 128])
    ck_in = din("ck_in", [NSEQ_S, WB, 512])
    cv_in = din("cv_in", [NSEQ_S, WB, 512])
    conv_in = din("conv_in", [128, NFF, NSEQ_S, 2])
    w_in = din("w_in", [D, DIN])
    w_o = din("w_o", [D, D])
    w_gate = din("w_gate", [D, DFF])
    w_up = din("w_up", [D, DFF])
    w_down = din("w_down", [DFF, D])
    w_pg = din("w_pg", [D, D])
    w_pp = din("w_pp", [PLE, D])
    gcols_d = din("gcols", [128, 3, 8])
    gfin_d = din("gfin", [128, D])
    lbl_d = din("lbl", [128, 2, 4])
    gon_d = din("gon", [128, 4])
    convw_d = din("convw", [128, NFF, 3])
    convb_d = din("convb", [128, NFF])
    maskr_d = din("maskr", [128, 20, 128])
    causal_d = din("causal", [128, 64])
    csp_d = din("csp", [128, SEQ // 128, 16])
    css_d = din("css", [TS, 16])

    y_p = dout("y_p", [NSEQ_P * SEQ, D])
    y_s = dout("y_s", [NSEQ_S * TS, D])
    st_p = dout("st_p", [NSEQ_P, 4, 128, 128])
    ck_p = dout("ck_p", [NSEQ_P, SEQ, 512])
    cv_p = dout("cv_p", [NSEQ_P, SEQ, 512])
    conv_p = dout("conv_p", [NSEQ_P, 128, NFF, 2])
    st_s = dout("st_s", [NSEQ_S, 4, 128, 128])
    ck_s = dout("ck_s", [NSEQ_S, WB, 512])
    cv_s = dout("cv_s", [NSEQ_S, WB, 512])
    conv_s = dout("conv_s", [NSEQ_S, 128, NFF, 2])
    out_buf = Buf("outputs")

    def sb(name, shape, dt=F32):
        return nc.alloc_sbuf_tensor("sb_" + name, list(shape), dt)

    psum_all = nc.alloc_psum_tensor("psum_all", [128, 8 * 512], F32)
    banks = [psum_all[:, i * 512:(i + 1) * 512] for i in range(8)]
    bank_b = [Buf("bank%d" % i, excl=True) for i in range(8)]

    ident = sb("ident", [128, 128], BF16)
    ones = sb("ones", [128, 128], BF16)
    causal = sb("causal", [128, 64], BF16)
    maskr = sb("maskr", [128, 20, 128], BF16)
    csp = sb("csp", [128, SEQ // 128, 16]); css = sb("css", [TS, 16])
    gcols = sb("gcols", [128, 3, 8]); gfin = sb("gfin", [128, D])
    lbl = sb("lbl", [128, 2, 4]); gon = sb("gon", [128, 4])
    c0 = sb("c0", [128, 4]); c1 = sb("c1", [128, 4]); nc1 = sb("nc1", [128, 4]); lbt = sb("lbt", [128, 4])
    convw = sb("convw", [128, NFF, 3]); convb = sb("convb", [128, NFF])
    rmask_p = sb("rmask_p", [128, T], BF16); rmask_s = sb("rmask_s", [128, 16], BF16)
    neghalf = sb("neghalf", [128, 8])
    const_b = Buf("const")

    KT = sb("KT", [128, 4, SEQ], BF16); KT_b = Buf("KT")
    VH = sb("VH", [128, 8, 512], BF16); VH_b = Buf("VH")
    Vp = sb("Vp", [128, 4, 4, 512], BF16); Vp_b = [Buf("Vp%d" % i) for i in range(4)]
    S = sb("S", [128, 4, 128]); S_bf = sb("S_bf", [128, 4, 128], BF16)
    S_b = [Buf("S%d" % i) for i in range(4)]; Sbf_b = [Buf("Sbf%d" % i) for i in range(4)]
    convhist = sb("convhist", [128, NFF, NSEQ_S, 2]); convhist_b = [Buf("ch%d" % i) for i in range(NFF)]

    h = sb("h", [128, NT, D]); h_b = [Buf("h%d" % i) for i in range(NT)]
    ss = sb("ss", [128, 16]); ms = sb("ms", [128, 16]); rstd = sb("rstd", [128, 16])
    ss_b = [Buf("ss%d" % i) for i in range(16)]
    aT = sb("aT", [128, 8, T], BF16); aT_b = Buf("aT")

    regM = sb("regM", [128, 22 * 1024 // 4])
    regM_bf = regM[:].bitcast(BF16)
    mixT = regM_bf[:, 0:4096].rearrange("p (c t) -> p c t", t=T)
    QT = regM_bf[:, 4096:6144].rearrange("p (c t) -> p c t", t=T)
    ia_tok = regM_bf[:, 6144:8192].rearrange("p (c t) -> p c t", t=512)
    stage = [regM[:, 4096 + i * 512: 4096 + (i + 1) * 512] for i in range(3)]
    gT = regM_bf.rearrange("p (c t) -> p c t", t=T)
    mixT_b = Buf("mixT"); QT_b = Buf("QT"); ia_b = Buf("ia"); stage_b = [Buf("stg%d" % i) for i in range(3)]
    gT_b = Buf("gT")
    yst = regM[:, 0:4096].rearrange("p (k d) -> p k d", d=D)
    yst_b = [Buf("yst%d" % i) for i in range(NT)]
    regM_A = [mixT_b, QT_b, ia_b] + stage_b
    regM_B = [gT_b] + yst_b
    qkb = [sb("qkb%d" % i, [128, 512], BF16) for i in range(4)]; qkb_b = [Buf("qkb%d" % i) for i in range(4)]
    stv = [sb("stv0", [128, 512]), stage[2]]; stv_b = [Buf("stv0"), stage_b[2]]

    HKB = 52
    regH = sb("regH", [128, HKB * 256])
    regH_bf = regH[:].bitcast(BF16)
    identf = regH[:, 0:128]
    o = [0]

    def carve(nelem_f32):
        a = o[0]; o[0] += nelem_f32
        assert o[0] <= HKB * 256
        return a

    sq = []; th = []
    for i in range(4):
        a = carve(512); sq.append(regH[:, a:a + 512])
    for i in range(4):
        a = carve(512); th.append(regH[:, a:a + 512])
    gs = []
    for i in range(4):
        a = carve(256); gs.append(regH_bf[:, 2 * a:2 * a + 512])
    a = carve(512); lf = regH[:, a:a + 512]
    oa_sb = regH[:, a:a + 2048].rearrange("p (h t) -> p h t", t=512)
    a = carve(512); bb = regH[:, a:a + 512]
    a = carve(512); eb = regH[:, a:a + 512]
    a = carve(512); enb = regH[:, a:a + 512]
    qdT = []; kdT = []; kdtok = []
    for i in range(4):
        a = carve(256); qdT.append(regH_bf[:, 2 * a:2 * a + 512])
    for i in range(4):
        a = carve(256); kdT.append(regH_bf[:, 2 * a:2 * a + 512])
    for i in range(4):
        a = carve(256); kdtok.append(regH_bf[:, 2 * a:2 * a + 512].rearrange("p (i k) -> p i k", k=128))
    a = carve(32); ebl = regH[:, a:a + 32].rearrange("p (h j) -> p h j", j=8)
    a = carve(128); AmA = regH_bf[:, 2 * a:2 * a + 256]
    a = carve(256); osq = regH_bf[:, 2 * a:2 * a + 512]

    a = carve(512); lnv = regH[:, a:a + 512]
    a = carve(512); rsv = regH[:, a:a + 512]
    a = carve(512); t1 = regH[:, a:a + 512]
    a = carve(512); StA = regH[:, a:a + 512]
    a = carve(256); rotw = regH[:, a:a + 256].rearrange("p (a h c) -> p a h c", a=4, h=8)
    hg_end = o[0]
    rot_b = Buf("rot"); rot2_b = Buf("rot2")
    sq_b = [Buf("sq%d" % i) for i in range(4)]; th_b = [Buf("th%d" % i) for i in range(4)]
    gs_b = [Buf("gs%d" % i) for i in range(4)]
    lf_b = Buf("lf"); bb_b = Buf("bb"); eb_b = Buf("eb"); enb_b = Buf("enb"); oa_b = Buf("oa")
    qdT_b = [Buf("qdT%d" % i) for i in range(4)]; kdT_b = [Buf("kdT%d" % i) for i in range(4)]
    kdtok_b = [Buf("kdtok%d" % i) for i in range(4)]; ebl_b = [Buf("ebl%d" % i) for i in range(4)]
    AmA_b = Buf("AmA"); osq_b = Buf("osq"); lnv_b = Buf("lnv"); rsv_b = Buf("rsv")
    t1_b = Buf("t1"); StA_b = Buf("StA")
    regH_A = sq_b + th_b + gs_b + [lf_b, bb_b, eb_b, enb_b] + qdT_b + kdT_b + kdtok_b + ebl_b + [AmA_b] + \
        [osq_b, lnv_b, rsv_b, t1_b, StA_b, oa_b, rot_b, rot2_b]
    o[0] = 0
    uext = []; ccv = []; scv = []
    for i in range(2):
        a = carve(520); uext.append(regH[:, a:a + 520])
    for i in range(2):
        a = carve(512); ccv.append(regH[:, a:a + 512])
    for i in range(2):
        a = carve(512); scv.append(regH[:, a:a + 512])
    pst = []
    for i in range(2):
        a = carve(256); pst.append(regH[:, a:a + 256])
    a = carve(512); pb = regH_bf[:, 2 * a:2 * a + 1024].rearrange("p (k c) -> p k c", c=256)
    a = carve(512); pT = regH_bf[:, 2 * a:2 * a + 1024].rearrange("p (c t) -> p c t", t=T)
    tg = []; t2 = []
    for i in range(2):
        a = carve(512); tg.append(regH[:, a:a + 512])
    for i in range(2):
        a = carve(512); t2.append(regH[:, a:a + 512])
    a = carve(1024); wpp_sb = regH_bf[:, 2 * a:2 * a + 2048].rearrange("p (c n) -> p c n", n=512)
    wpp_b = Buf("wpp")
    uext_b = [Buf("uext%d" % i) for i in range(2)]; ccv_b = [Buf("cc%d" % i) for i in range(2)]
    scv_b = [Buf("sc%d" % i) for i in range(2)]; pst_b = [Buf("pst%d" % i) for i in range(2)]
    pb_b = Buf("pb"); pT_b = Buf("pT"); tg_b = [Buf("tg%d" % i) for i in range(2)]; t2_b = [Buf("t2%d" % i) for i in range(2)]
    regH_B = uext_b + ccv_b + scv_b + pst_b + [pb_b, pT_b, wpp_b] + tg_b + t2_b

    Pexp = [sb("Pexp%d" % i, [128, 2, 512], BF16) for i in range(3)]; Pexp_b = [Buf("Pexp%d" % i) for i in range(3)]
    Pm = [sb("Pm%d" % i, [128, 2, 512], BF16) for i in range(3)]; Pm_b = [Buf("Pm%d" % i) for i in range(3)]
    junk = Pexp[0][:, :, :].rearrange("p e n -> p (e n)"); junk_b = Pexp_b[0]
    xs = [Pexp[1 + i][:, :, :].rearrange("p e n -> p (e n)") for i in range(2)]; xs_b = [Pexp_b[1 + i] for i in range(2)]
    lnL = lnv; rL = rsv; lnL_b = lnv_b; rL_b = rsv_b
    KTn = sb("KTn", [128, 4, 4], BF16); KTn_b = Buf("KTn")
    Vn = sb("Vn", [4, 512], BF16); Vn_b = Buf("Vn")
    sacc = sb("sacc", [128, 32]); sacc_b = Buf("sacc")
    cbf = [sb("cbf%d" % i, [128, 512], BF16) for i in range(2)]; cbf_b = [Buf("cbf%d" % i) for i in range(2)]

    slab = [sb("slab%d" % i, [128, 8, 512], BF16) for i in range(NSLAB)]
    slab_b = [Buf("slab%d" % i) for i in range(NSLAB)]
    slab_i = [0]

    def bank_bf(i):
        return banks[i].bitcast(BF16)

    def group_sched():
        L = [("win", b) for b in (1, 0, 3, 2, 4, 5, 6)]
        L += [("wo", 0), ("wo", 1)]
        for g0 in range(0, NFF, 4):
            L += [("gate", g0), ("up", g0)]
        for half in range(2):
            for g0 in range(0, NFF, 8):
                L.append(("down", half, g0))
        L += [("pg", 0), ("pg", 1)]
        return L

    sched = []
    for _ in range(NSEQ_P * NST + 1):
        sched += group_sched()
    ws = {"issued": 0, "consumed": 0}

    def w_parts(desc, sl):
        k = desc[0]
        col = lambda w, c0_, nc_: w[:, c0_:c0_ + nc_].rearrange("(kc p) n -> p kc n", p=128)
        if k == "win":
            return [(sl[:, 0:8, 0:512], col(w_in, desc[1] * 512, 512))]
        if k == "wo":
            return [(sl[:, 0:8, 0:512], col(w_o, desc[1] * 512, 512))]
        if k in ("gate", "up"):
            ncc = min(4, NFF - desc[1])
            return [(sl[:, 0:8, 0:ncc * 128], col(w_gate if k == "gate" else w_up, desc[1] * 128, ncc * 128))]
        if k == "down":
            half, g0 = desc[1], desc[2]
            ncc = min(8, NFF - g0)
            return [(sl[:, 0:ncc, :], w_down[g0 * 128:(g0 + ncc) * 128, half * 512:(half + 1) * 512].rearrange("(cc p) n -> p cc n", p=128))]
        if k == "pp":
            return [(sl[:, 0:2, :], col(w_pp, 0, 512)), (sl[:, 2:4, :], col(w_pp, 512, 512))]
        if k == "pg":
            return [(sl[:, 0:8, 0:512], col(w_pg, desc[1] * 512, 512))]
        raise ValueError(desc)

    def get_w(desc, oldest=None):
        j = ws["consumed"]
        assert sched[j] == desc, (sched[j], desc)
        if oldest is None:
            oldest = j
        while ws["issued"] < min(len(sched), oldest + NSLAB):
            i = ws["issued"]
            for dst, src in w_parts(sched[i], slab[i % NSLAB]):
                P.dma("pool", dst, src, writes=[slab_b[i % NSLAB]])
            ws["issued"] += 1
        assert ws["issued"] > j
        ws["consumed"] += 1
        ws["last"] = j
        return slab[j % NSLAB], slab_b[j % NSLAB]

    def w_prefetch():
        j = ws["consumed"]
        while ws["issued"] < min(len(sched), j + NSLAB):
            i = ws["issued"]
            for dst, src in w_parts(sched[i], slab[i % NSLAB]):
                P.dma("pool", dst, src, writes=[slab_b[i % NSLAB]])
            ws["issued"] += 1

    def mm(out, lhsT, rhs, start, stop):
        return lambda e: e.matmul(out, lhsT=lhsT, rhs=rhs, start=start, stop=stop)

    for dst, src in ((csp, csp_d), (gcols, gcols_d), (gfin, gfin_d), (lbl, lbl_d), (gon, gon_d),
                     (convw, convw_d), (convb, convb_d)):
        P.dma("sp", dst[:], src, writes=[const_b])
    P.dma("sp", css[:], css_d, writes=[const_b])
    P.dma("pool", maskr[:], maskr_d, writes=[const_b])
    P.dma("pool", causal[:], causal_d, writes=[const_b])
    P.op("pool", lambda e: e.memset(identf[:], 1.0), writes=[const_b])
    P.op("pool", lambda e: e.affine_select(out=identf[:], in_=identf[:], pattern=[[-1, 128]],
                                           compare_op=ALU.is_equal, fill=0.0, base=0, channel_multiplier=1),
         writes=[const_b])
    P.op("dve", lambda e: e.tensor_copy(out=ident[:], in_=identf[:]), reads=[const_b], writes=[const_b])
    P.op("pool", lambda e: e.memset(ones[:], 1.0), writes=[const_b])
    P.op("pool", lambda e: e.memset(neghalf[:], -0.5), writes=[const_b])
    P.op("pool", lambda e: e.memset(rmask_p[:], 1.0), writes=[const_b])
    P.op("pool", lambda e: e.memset(rmask_p[:].rearrange("p (c t) -> p c t", t=64)[:, :, 0:1], 0.0), writes=[const_b])
    P.op("pool", lambda e: e.memset(rmask_s[:], 1.0), writes=[const_b])
    P.op("pool", lambda e: e.memset(rmask_s[:].rearrange("p (c t) -> p c t", t=4)[:, :, 0:1], 0.0), writes=[const_b])
    P.op("dve", lambda e: e.tensor_tensor(out=lbt[:], in0=lbl[:, 1, :], in1=lbl[:, 0, :], op=ALU.subtract),
         reads=[const_b], writes=[const_b])
    P.op("act", lambda e: e.activation(out=lbt[:], in_=lbt[:], func=AF.Exp), reads=[const_b], writes=[const_b])
    P.op("dve", lambda e: e.tensor_scalar(out=lbt[:], in0=lbt[:], scalar1=1.0, scalar2=None, op0=ALU.add),
         reads=[const_b], writes=[const_b])
    P.op("dve", lambda e: e.reciprocal(out=lbt[:], in_=lbt[:]), reads=[const_b], writes=[const_b])
    P.op("dve", lambda e: e.tensor_scalar(out=c1[:], in0=lbt[:], scalar1=-0.5, scalar2=0.5, op0=ALU.mult, op1=ALU.add),
         reads=[const_b], writes=[const_b])
    P.op("dve", lambda e: e.tensor_scalar(out=c0[:], in0=lbt[:], scalar1=0.5, scalar2=0.5, op0=ALU.mult, op1=ALU.add),
         reads=[const_b], writes=[const_b])
    P.op("dve", lambda e: e.tensor_scalar(out=nc1[:], in0=c1[:], scalar1=-1.0, scalar2=None, op0=ALU.mult),
         reads=[const_b], writes=[const_b])

    P.handoff([const_b], regH_A + regH_B)
    ss_n = [0]

    def norm_A(k, ntok):
        si = ss_n[0] % 16
        ss_n[0] += 1
        P.op("act", lambda e: e.activation(out=junk[:ntok, :], in_=h[:ntok, k, :], func=AF.Square,
                                           accum_out=ss[:ntok, si:si + 1]),
             reads=[h_b[k]], writes=[junk_b, ss_b[si]])
        P.op("dve", lambda e: e.tensor_scalar(out=ms[:ntok, si:si + 1], in0=ss[:ntok, si:si + 1], scalar1=1.0 / D,
                                              scalar2=EPS, op0=ALU.mult, op1=ALU.add),
             reads=[ss_b[si]], writes=[ss_b[si]])
        P.op("pool", lambda e: e.tensor_tensor(out=rstd[:ntok, si:si + 1], in0=ms[:ntok, si:si + 1],
                                               in1=neghalf[:ntok, 0:1], op=ALU.pow),
             reads=[ss_b[si], const_b], writes=[ss_b[si]])
        return si

    def norm_B(k, ntok, si):
        xi = k % 2
        P.op("dve", lambda e: e.tensor_scalar(out=xs[xi][:ntok, :], in0=h[:ntok, k, :], scalar1=rstd[:ntok, si:si + 1],
                                              scalar2=None, op0=ALU.mult),
             reads=[h_b[k], ss_b[si]], writes=[xs_b[xi]])

    def norm_C(k, ntok):
        xi = k % 2
        bk = 6 + (k % 2)
        bv = bank_bf(bk)
        P.pe_group([(lambda e, c=c: e.transpose(out=bv[:, c * 128:c * 128 + ntok], in_=xs[xi][:ntok, c * 128:(c + 1) * 128],
                                                identity=ident[:ntok, :ntok])) for c in range(8)],
                   reads=[xs_b[xi], const_b], writes=[bank_b[bk]])

    def norm_D(k, col0, ntok, gi):
        bk = 6 + (k % 2)
        bv = bank_bf(bk)
        P.op("dve", lambda e: e.tensor_tensor(
            out=aT[:, :, col0:col0 + ntok],
            in0=bv.rearrange("p (c t) -> p c t", t=128)[:, :, 0:ntok],
            in1=gcols[:, gi, :].unsqueeze(2).to_broadcast([128, 8, ntok]), op=ALU.mult),
            reads=[bank_b[bk], const_b], writes=[aT_b])

    def norm_T(G, gi):
        dt_ = G["dt"]
        n_ = len(dt_)
        sis = {}
        order = []
        for k in range(min(2, n_)):
            order += [("A", k), ("B", k)]
        for k in range(n_):
            order.append(("C", k))
            if k + 2 < n_:
                order.append(("A", k + 2))
            order.append(("D", k))
            if k + 2 < n_:
                order.append(("B", k + 2))
        for what, k in order:
            col0, ntok = dt_[k]
            if what == "A":
                sis[k] = norm_A(k, ntok)
            elif what == "B":
                norm_B(k, ntok, sis[k])
            elif what == "C":
                norm_C(k, ntok)
            else:
                norm_D(k, col0, ntok, gi)

    def phase_wo_norm1(G):
        sl0, slb0 = get_w(("wo", 0))
        sl1, slb1 = get_w(("wo", 1), oldest=ws["last"])
        sls = [(sl0, slb0), (sl1, slb1)]
        dt_ = G["dt"]
        sis = []
        for k, (col0, ntok) in enumerate(dt_):
            for half in range(2):
                sl, slb = sls[half]
                bk = (2 * k + half) % 4
                P.pe_group([mm(banks[bk][:ntok, :], mixT[:, kc, col0:col0 + ntok], sl[:, kc, :], kc == 0, kc == 7)
                            for kc in range(8)], reads=[slb, mixT_b], writes=[bank_b[bk]])
            if k == len(dt_) - 1:
                w_prefetch()
            if k >= 1:
                norm_B(k - 1, dt_[k - 1][1], sis[k - 1])
                norm_C(k - 1, dt_[k - 1][1])
            for half in range(2):
                bk = (2 * k + half) % 4
                hv = h[:ntok, k, half * 512:(half + 1) * 512]
                P.op("dve", lambda e: e.tensor_tensor(out=hv, in0=banks[bk][:ntok, :], in1=hv, op=ALU.add),
                     reads=[bank_b[bk], h_b[k]], writes=[h_b[k]])
            sis.append(norm_A(k, ntok))
            if k >= 1:
                norm_D(k - 1, dt_[k - 1][0], dt_[k - 1][1], 1)
        k = len(dt_) - 1
        norm_B(k, dt_[k][1], sis[k])
        norm_C(k, dt_[k][1])
        norm_D(k, dt_[k][0], dt_[k][1], 1)

    def rotary(G, st_ap, cs_ap, ntok, bufs, dst_ap=None, dst_bufs=None):
        if dst_ap is None:
            dst_ap = st_ap; dst_bufs = bufs
        v = st_ap.rearrange("p (h c) -> p h c", c=64)
        dv = dst_ap.rearrange("p (h c) -> p h c", c=64)
        u = v[:ntok, :, 0:16].rearrange("p h (a c) -> p h a c", c=8)
        cosb = cs_ap[:ntok, 0:8].unsqueeze(1).unsqueeze(1).to_broadcast([ntok, 8, 2, 8])
        sinb = cs_ap[:ntok, 8:16].unsqueeze(1).unsqueeze(1).to_broadcast([ntok, 8, 2, 8])
        U1 = rotw[:ntok, 0:2, :, :].rearrange("p a h c -> p h a c")
        W = rotw[:ntok, 2:4, :, :].rearrange("p a h c -> p h a c")
        P.op("dve", lambda e: e.tensor_tensor(out=U1, in0=u, in1=cosb, op=ALU.mult), reads=bufs + [const_b], writes=[rot_b])
        P.op("dve", lambda e: e.tensor_tensor(out=W, in0=u, in1=sinb, op=ALU.mult), reads=bufs + [const_b], writes=[rot2_b])
        P.op("dve", lambda e: e.tensor_tensor(out=dv[:ntok, :, 0:8], in0=rotw[:ntok, 0, :, :], in1=rotw[:ntok, 3, :, :], op=ALU.subtract),
             reads=[rot_b, rot2_b], writes=dst_bufs)
        P.op("dve", lambda e: e.tensor_tensor(out=dv[:ntok, :, 8:16], in0=rotw[:ntok, 1, :, :], in1=rotw[:ntok, 2, :, :], op=ALU.add),
             reads=[rot_b, rot2_b], writes=dst_bufs)


    def phase_win(G, sample, seq, st):
        n = G["n"]
        for blk, kind in ((1, "f"), (0, "q"), (3, "g")):
            sl, slb = get_w(("win", blk))
            for hh in range(4):
                bk = hh
                P.pe_group([mm(banks[bk][:, 0:n], sl[:, kc, hh * 128:(hh + 1) * 128], aT[:, kc, 0:n], kc == 0, kc == 7)
                            for kc in range(8)], reads=[slb, aT_b], writes=[bank_b[bk]])
                if kind == "f":
                    P.op("act", lambda e: e.activation(out=th[hh][:, 0:n], in_=banks[bk][:, 0:n], func=AF.Tanh, scale=0.5),
                         reads=[bank_b[bk]], writes=[th_b[hh]])
                elif kind == "q":
                    P.op("act", lambda e: e.activation(out=sq[hh][:, 0:n], in_=banks[bk][:, 0:n], func=AF.Silu),
                         reads=[bank_b[bk]], writes=[sq_b[hh]])
                else:
                    P.op("act", lambda e: e.activation(out=gs[hh][:, 0:n], in_=banks[bk][:, 0:n], func=AF.Silu),
                         reads=[bank_b[bk]], writes=[gs_b[hh]])
        for blk, kind in ((2, "i"), (4, "q"), (5, "k"), (6, "v")):
            sl, slb = get_w(("win", blk))
            posts = []
            for k, (col0, ntok) in enumerate(G["stl"]):
                bk = k % 4
                gt = st * NT + k
                P.pe_group([mm(banks[bk][:ntok, :], aT[:, kc, col0:col0 + ntok], sl[:, kc, :], kc == 0, kc == 7)
                            for kc in range(8)], reads=[slb, aT_b], writes=[bank_b[bk]])
                if len(posts) >= 2:
                    posts.pop(0)()
                if kind == "i":
                    P.op("act", lambda e: e.activation(out=ia_tok[:ntok, k, :], in_=banks[bk][:ntok, :], func=AF.Copy),
                         reads=[bank_b[bk]], writes=[ia_b])
                    continue
                d2 = k % 2
                if kind == "v":
                    P.op("act", lambda e: e.activation(out=VH[:, gt % 8, :], in_=banks[bk][:, :], func=AF.Copy),
                         reads=[bank_b[bk]], writes=[VH_b])
                    P.op("dve", lambda e: e.tensor_copy(out=stv[d2][:, :], in_=banks[bk][:, :]), reads=[bank_b[bk]], writes=[stv_b[d2]])
                    P.dma("sp", cv_p[seq, gt * 128:(gt + 1) * 128, :], stv[d2][:, :], reads=[stv_b[d2]])
                    continue
                cs_ap = csp[:, gt, :]
                if kind == "q":
                    qb_ = qkb[d2]; qbb = qkb_b[d2]
                    P.op("act", lambda e: e.activation(out=qb_[:, :], in_=banks[bk][:, :], func=AF.Copy), reads=[bank_b[bk]], writes=[qbb])
                    rotary(G, banks[bk][:, :], cs_ap, 128, [bank_b[bk]], dst_ap=qb_[:, :], dst_bufs=[qbb])
                else:
                    qb_ = qkb[2 + d2]; qbb = qkb_b[2 + d2]
                    sk = stage[d2]; skb = stage_b[d2]
                    P.op("act", lambda e: e.activation(out=sk[:, :], in_=banks[bk][:, :], func=AF.Copy), reads=[bank_b[bk]], writes=[skb])
                    rotary(G, banks[bk][:, :], cs_ap, 128, [bank_b[bk]], dst_ap=sk[:, :], dst_bufs=[skb])
                    P.op("dve", lambda e: e.tensor_copy(out=qb_[:, :], in_=sk[:, :]), reads=[skb], writes=[qbb])
                    P.dma("sp", ck_p[seq, gt * 128:(gt + 1) * 128, :], sk[:, :], reads=[skb])

                def post(kind=kind, k=k, gt=gt, col0=col0, qb_=qb_, qbb=qbb):
                    tb_ = 6 + (k % 2)
                    bv = bank_bf(tb_)
                    P.pe_group([(lambda e, c=c: e.transpose(out=bv[:, c * 128:(c + 1) * 128], in_=qb_[:, c * 128:(c + 1) * 128],
                                                            identity=ident[:, :])) for c in range(4)],
                               reads=[qbb, const_b], writes=[bank_b[tb_]])
                    src = bv[:, 0:512].rearrange("p (c t) -> p c t", t=128)
                    if kind == "q":
                        P.op("act", lambda e: e.activation(out=QT[:, :, col0:col0 + 128], in_=src, func=AF.Copy, scale=0.125),
                             reads=[bank_b[tb_]], writes=[QT_b])
                    else:
                        P.op("act", lambda e: e.activation(out=KT[:, :, gt * 128:(gt + 1) * 128], in_=src, func=AF.Copy),
                             reads=[bank_b[tb_]], writes=[KT_b])
                posts.append(post)
            for p_ in posts:
                p_()
            if kind == "v":
                for rho in range(4):
                    for t_ in range(4):
                        P.dma("sp", Vp[32 * t_:32 * (t_ + 1), st % 4, rho, :], VH[rho:128:4, (4 * st + t_) % 8, :],
                              reads=[VH_b], writes=[Vp_b[st % 4]])
            hgrn_g1(G, {"i": 0, "q": 1, "k": 2, "v": 3}[kind])

    def hgrn_gates(G, sample):
        n = G["n"]
        rm = rmask_s if sample else rmask_p
        C = 4 if sample else 64
        nch = n // C
        for hh in range(4):
            P.op("act", lambda e: e.activation(out=lf[:, 0:n], in_=th[hh][:, 0:n], func=AF.Ln,
                                               scale=c1[:, hh:hh + 1], bias=c0[:, hh:hh + 1]),
                 reads=[th_b[hh], const_b], writes=[lf_b])
            P.op("dve", lambda e: e.tensor_tensor_scan(out=bb[:, 0:n], data0=rm[:, 0:n], data1=lf[:, 0:n], initial=0.0,
                                                       op0=ALU.mult, op1=ALU.add),
                 reads=[lf_b, const_b], writes=[bb_b])
            P.op("act", lambda e: e.activation(out=eb[:, 0:n], in_=bb[:, 0:n], func=AF.Exp), reads=[bb_b], writes=[eb_b])
            P.op("act", lambda e: e.activation(out=enb[:, 0:n], in_=bb[:, 0:n], func=AF.Exp, scale=-1.0),
                 reads=[bb_b], writes=[enb_b])
            P.op("dve", lambda e: e.scalar_tensor_tensor(out=qdT[hh][:, 0:n], in0=sq[hh][:, 0:n], scalar=128.0 ** -0.5,
                                                         in1=eb[:, 0:n], op0=ALU.mult, op1=ALU.mult),
                 reads=[sq_b[hh], eb_b], writes=[qdT_b[hh]])
            P.op("dve", lambda e: e.tensor_copy(out=ebl[:, hh, 0:nch],
                                                in_=eb[:, 0:n].rearrange("p (c t) -> p c t", t=C)[:, :, C - 1]),
                 reads=[eb_b], writes=[ebl_b[hh]])
            P.op("dve", lambda e: e.tensor_scalar(out=th[hh][:, 0:n], in0=th[hh][:, 0:n], scalar1=nc1[:, hh:hh + 1],
                                                  scalar2=c1[:, hh:hh + 1], op0=ALU.mult, op1=ALU.add),
                 reads=[th_b[hh], const_b], writes=[th_b[hh]])
            P.op("dve", lambda e: e.tensor_tensor(out=kdT[hh][:, 0:n], in0=th[hh][:, 0:n], in1=enb[:, 0:n], op=ALU.mult),
                 reads=[th_b[hh], enb_b], writes=[kdT_b[hh]])
            tb_ = 6 + (hh % 2)
            bv = bank_bf(tb_)
            fns = []
            for k, (col0, ntok) in enumerate(G["stl"]):
                fns.append(lambda e, k=k, col0=col0, ntok=ntok: e.transpose(
                    out=bv[:ntok, k * 128:(k + 1) * 128], in_=kdT[hh][:, col0:col0 + ntok], identity=ident[:, :]))
            P.pe_group(fns, reads=[kdT_b[hh], const_b], writes=[bank_b[tb_]])
            nst = len(G["stl"])
            ntok0 = G["stl"][0][1]
            P.op("act", lambda e: e.activation(out=kdtok[hh][:ntok0, 0:nst, :],
                                               in_=bv[:ntok0, 0:nst * 128].rearrange("p (i k) -> p i k", k=128), func=AF.Copy),
                 reads=[bank_b[tb_]], writes=[kdtok_b[hh]])

    def hgrn_chunks(G, sample, seq_of_chunk=None):
        n = G["n"]
        C = 4 if sample else 64
        nch = n // C
        S_all = S[:, :, :].rearrange("p h v -> p (h v)")
        Sbf_all = S_bf[:, :, :].rearrange("p h v -> p (h v)")

        def base_of(j):
            return 0 if sample else (j % 2) * 64

        def emit_A(j):
            cols = slice(j * C, (j + 1) * C)
            base = base_of(j)
            P.pe_group([mm(banks[4][base:base + C, hh * 64:hh * 64 + C], kdT[hh][:, cols], qdT[hh][:, cols], True, True)
                        for hh in range(4)], reads=kdT_b + qdT_b, writes=[bank_b[4]])
            P.op("dve", lambda e: e.tensor_tensor(
                out=AmA[base:base + C, :].rearrange("p (h t) -> p h t", t=64)[:, :, 0:C],
                in0=banks[4][base:base + C, 0:256].rearrange("p (h t) -> p h t", t=64)[:, :, 0:C],
                in1=causal[base:base + C, 0:C].unsqueeze(1).to_broadcast([C, 4, C]), op=ALU.mult),
                reads=[bank_b[4], const_b], writes=[AmA_b])

        if not sample:
            emit_A(0)
        for j in range(nch):
            cols = slice(j * C, (j + 1) * C)
            base = base_of(j)
            if sample:
                k = j
                P.dma("sp", S[:, :, :], st_in[j].rearrange("h k v -> k h v"), writes=S_b)
                P.op("act", lambda e: e.activation(out=Sbf_all, in_=S_all, func=AF.Copy), reads=S_b, writes=Sbf_b)
                emit_A(j)
            else:
                k = j // 2
            db = 6 + (j % 2)
            fns = []
            for hh in range(4):
                vtok = ia_tok[base:base + C, k, hh * 128:(hh + 1) * 128]
                fns.append(mm(banks[hh][:, cols], S_bf[:, hh, :], qdT[hh][:, cols], True, False))
                fns.append(mm(banks[hh][:, cols], vtok, AmA[base:base + C, hh * 64:hh * 64 + C], False, True))
                fns.append(mm(banks[db][:, hh * 128:(hh + 1) * 128], kdtok[hh][base:base + C, k, :], vtok, True, True))
            P.pe_group(fns, reads=Sbf_b + qdT_b + kdtok_b + [ia_b, AmA_b], writes=[bank_b[0], bank_b[1], bank_b[2], bank_b[3], bank_b[db]])
            if (not sample) and j + 1 < nch:
                emit_A(j + 1)
            P.op("dve", lambda e: e.tensor_tensor(out=StA[:, :], in0=banks[db][:, :], in1=S_all, op=ALU.add),
                 reads=[bank_b[db]] + S_b, writes=[StA_b])
            P.op("dve", lambda e: e.tensor_tensor(out=S[:, :, :], in0=StA[:, :].rearrange("p (h v) -> p h v", v=128),
                                                  in1=ebl[:, :, j:j + 1].to_broadcast([128, 4, 128]), op=ALU.mult),
                 reads=[StA_b] + ebl_b, writes=S_b)
            P.op("act", lambda e: e.activation(out=Sbf_all, in_=S_all, func=AF.Copy), reads=S_b, writes=Sbf_b)
            if sample:
                P.dma("sp", st_s[j].rearrange("h k v -> k h v"), S[:, :, :], reads=S_b)

    def hgrn_out(G):
        n = G["n"]
        for hh in range(4):
            P.op("act", lambda e: e.activation(out=osq[:, 0:n], in_=banks[hh][:, 0:n], func=AF.Square),
                 reads=[bank_b[hh]], writes=[osq_b])
            nb = 4 + (hh % 2)
            P.pe_group([mm(banks[nb][:, 0:n], ones[:, :], osq[:, 0:n], True, True)], reads=[osq_b, const_b], writes=[bank_b[nb]])
            P.op("act", lambda e: e.activation(out=lnv[:, 0:n], in_=banks[nb][:, 0:n], func=AF.Ln, scale=1.0 / 128, bias=EPS),
                 reads=[bank_b[nb]], writes=[lnv_b])
            P.op("act", lambda e: e.activation(out=rsv[:, 0:n], in_=lnv[:, 0:n], func=AF.Exp, scale=-0.5),
                 reads=[lnv_b], writes=[rsv_b])
            P.op("dve", lambda e: e.tensor_tensor(out=t1[:, 0:n], in0=banks[hh][:, 0:n], in1=rsv[:, 0:n], op=ALU.mult),
                 reads=[bank_b[hh], rsv_b], writes=[t1_b])
            P.op("dve", lambda e: e.scalar_tensor_tensor(out=mixT[:, hh, 0:n], in0=t1[:, 0:n], scalar=gon[:, hh:hh + 1],
                                                         in1=gs[hh][:, 0:n], op0=ALU.mult, op1=ALU.mult),
                 reads=[t1_b, gs_b[hh], const_b], writes=[mixT_b])

    att_i = [0]

    def attn_items(items, sbase=4, side=None, nslot=2):
        st_ = []

        def qk(it):
            i2 = att_i[0] % nslot
            att_i[0] += 1
            b0 = sbase + 2 * i2
            sc2 = psum_all[:, b0 * 512:(b0 + 2) * 512].rearrange("p (e n) -> p e n", n=512)
            nk, N = it["nk"], it["N"]
            P.pe_group([mm(sc2[:nk, e2, 0:N], it["kt_ap"][64 * e2:64 * e2 + 64, :], it["q_ap"][64 * e2:64 * e2 + 64, :], True, True)
                        for e2 in range(2)], reads=it["rd"] + [QT_b], writes=[bank_b[b0], bank_b[b0 + 1]])
            P.op("act", lambda e: e.activation(out=Pexp[i2][:nk, :, 0:N], in_=sc2[:nk, :, 0:N], func=AF.Exp),
                 reads=[bank_b[b0], bank_b[b0 + 1]], writes=[Pexp_b[i2]])
            nq = it["nq"]
            if it["m3"]:
                nj = N // nq
                P.op("dve", lambda e: e.tensor_tensor(out=Pm[i2][:nk, :, 0:N].rearrange("p e (b q) -> p e b q", q=nq),
                                                      in0=Pexp[i2][:nk, :, 0:N].rearrange("p e (b q) -> p e b q", q=nq),
                                                      in1=it["m_ap"].unsqueeze(1).to_broadcast([nk, 2, nj, nq]), op=ALU.mult),
                     reads=[Pexp_b[i2], const_b], writes=[Pm_b[i2]])
            else:
                P.op("dve", lambda e: e.tensor_tensor(out=Pm[i2][:nk, :, 0:N], in0=Pexp[i2][:nk, :, 0:N],
                                                      in1=it["m_ap"].unsqueeze(1).to_broadcast([nk, 2, N]), op=ALU.mult),
                     reads=[Pexp_b[i2], const_b], writes=[Pm_b[i2]])
            return i2

        def pv(it, i2):
            nk, N = it["nk"], it["N"]
            ob, lb_ = it["obank"], it["lbank"]
            oc = it["ocols"]
            f_, l_ = it["first"], it["last"]
            P.pe_group([mm(banks[ob][0:64, oc], it["v_aps"][0], Pm[i2][:nk, 0, 0:N], f_, l_),
                        mm(banks[ob][64:128, oc], it["v_aps"][1], Pm[i2][:nk, 1, 0:N], f_, l_),
                        mm(banks[lb_][0:64, oc], ones[:nk, 0:64], Pm[i2][:nk, 0, 0:N], f_, l_),
                        mm(banks[lb_][64:128, oc], ones[:nk, 0:64], Pm[i2][:nk, 1, 0:N], f_, l_)],
                       reads=it["rd"] + [Pm_b[i2], const_b], writes=[bank_b[ob], bank_b[lb_]])
            if it["fin"] is not None:
                it["fin"]()

        side = list(side or [])
        every = max(1, len(items) // (len(side) + 1)) if side else 0
        npv = [0]

        def pv_side(a, b):
            pv(a, b)
            npv[0] += 1
            if side and npv[0] % every == 0:
                side.pop(0)(sbase + 2 * b)

        for idx, it in enumerate(items):
            st_.append((it, qk(it)))
            if len(st_) > nslot - 1:
                a, b = st_.pop(0)
                pv_side(a, b)
        while st_:
            a, b = st_.pop(0)
            pv_side(a, b)
        while side:
            side.pop(0)(sbase)

    def attn_finish(pr, cols, ocols_mix, obank, lbank):
        n_ = cols.stop - cols.start
        P.op("act", lambda e: e.activation(out=lnL[:, 0:n_], in_=banks[lbank][:, cols], func=AF.Ln), reads=[bank_b[lbank]], writes=[lnL_b])
        P.op("act", lambda e: e.activation(out=rL[:, 0:n_], in_=lnL[:, 0:n_], func=AF.Exp, scale=-1.0), reads=[lnL_b], writes=[rL_b])
        P.op("dve", lambda e: e.tensor_tensor(out=mixT[:, 4 + pr, ocols_mix], in0=banks[obank][:, cols], in1=rL[:, 0:n_], op=ALU.mult),
             reads=[bank_b[obank], rL_b], writes=[mixT_b])

    def phase_attn_prompt(G, st, side):
        gt0 = st * NT
        items = []
        for pr in range(4):
            obank = 0; lbank = 1
            its = []
            kts = [gt0] + [kt for kt in range(max(0, gt0 - 4), gt0 + NT) if kt != gt0]
            for kt in kts:
                j_lo = max(0, kt - gt0); j_hi = NT - 1
                nj = j_hi - j_lo + 1
                off_lo = gt0 + j_lo - kt
                cols = slice(j_lo * 128, (j_hi + 1) * 128)
                its.append(dict(kt_ap=KT[:, pr, kt * 128:(kt + 1) * 128], q_ap=QT[:, pr, cols],
                                v_aps=[VH[:, kt % 8, (2 * pr) * 64:(2 * pr + 1) * 64], VH[:, kt % 8, (2 * pr + 1) * 64:(2 * pr + 2) * 64]],
                                m_ap=maskr[:, off_lo:off_lo + nj, :], m3=True, nq=128, nk=128, N=nj * 128,
                                obank=obank, lbank=lbank, ocols=cols, rd=[KT_b, VH_b], fin=None))
            for G_ in (st - 2, st - 3, st - 4):
                if G_ < 0:
                    continue
                for rho in range(4):
                    its.append(dict(kt_ap=KT[:, pr, 512 * G_:512 * (G_ + 1)].rearrange("p (i f) -> p f i", f=4)[:, rho, :],
                                    q_ap=QT[:, pr, 0:T].rearrange("p (i f) -> p f i", f=4)[:, rho, :],
                                    v_aps=[Vp[:, G_ % 4, rho, (2 * pr) * 64:(2 * pr + 1) * 64],
                                           Vp[:, G_ % 4, rho, (2 * pr + 1) * 64:(2 * pr + 2) * 64]],
                                    m_ap=maskr[:, 19 if G_ == st - 4 else 18, :], m3=False, nq=128, nk=128, N=128,
                                    obank=obank, lbank=lbank, ocols=slice(rho, T, 4), rd=[KT_b, Vp_b[G_ % 4]], fin=None))
            for ii, it in enumerate(its):
                it["first"] = (ii == 0)
                it["last"] = (ii == len(its) - 1)
            its[-1]["fin"] = (lambda pr=pr, obank=obank, lbank=lbank: attn_finish(pr, slice(0, T), slice(0, T), obank, lbank))
            items += its
        attn_items(items, sbase=2, side=side, nslot=3)

    def hgrn_g1(G, hh):
        n = G["n"]
        rm = rmask_p
        C = 64
        nch = n // C
        P.op("act", lambda e: e.activation(out=lf[:, 0:n], in_=th[hh][:, 0:n], func=AF.Ln,
                                           scale=c1[:, hh:hh + 1], bias=c0[:, hh:hh + 1]),
             reads=[th_b[hh], const_b], writes=[lf_b])
        P.op("dve", lambda e: e.tensor_tensor_scan(out=bb[:, 0:n], data0=rm[:, 0:n], data1=lf[:, 0:n], initial=0.0,
                                                   op0=ALU.mult, op1=ALU.add),
             reads=[lf_b, const_b], writes=[bb_b])
        P.op("act", lambda e: e.activation(out=eb[:, 0:n], in_=bb[:, 0:n], func=AF.Exp), reads=[bb_b], writes=[eb_b])
        P.op("act", lambda e: e.activation(out=enb[:, 0:n], in_=bb[:, 0:n], func=AF.Exp, scale=-1.0),
             reads=[bb_b], writes=[enb_b])
        P.op("dve", lambda e: e.scalar_tensor_tensor(out=qdT[hh][:, 0:n], in0=sq[hh][:, 0:n], scalar=128.0 ** -0.5,
                                                     in1=eb[:, 0:n], op0=ALU.mult, op1=ALU.mult),
             reads=[sq_b[hh], eb_b], writes=[qdT_b[hh]])
        P.op("dve", lambda e: e.tensor_copy(out=ebl[:, hh, 0:nch],
                                            in_=eb[:, 0:n].rearrange("p (c t) -> p c t", t=C)[:, :, C - 1]),
             reads=[eb_b], writes=[ebl_b[hh]])
        P.op("dve", lambda e: e.tensor_scalar(out=th[hh][:, 0:n], in0=th[hh][:, 0:n], scalar1=nc1[:, hh:hh + 1],
                                              scalar2=c1[:, hh:hh + 1], op0=ALU.mult, op1=ALU.add),
             reads=[th_b[hh], const_b], writes=[th_b[hh]])
        P.op("dve", lambda e: e.tensor_tensor(out=kdT[hh][:, 0:n], in0=th[hh][:, 0:n], in1=enb[:, 0:n], op=ALU.mult),
             reads=[th_b[hh], enb_b], writes=[kdT_b[hh]])

    def hgrn_g2(G):
        for hh in range(4):
            tb_ = 6 + (hh % 2)
            bv = bank_bf(tb_)
            P.pe_group([(lambda e, k=k, col0=col0: e.transpose(out=bv[:, k * 128:(k + 1) * 128], in_=kdT[hh][:, col0:col0 + 128],
                                                               identity=ident[:, :])) for k, (col0, ntok) in enumerate(G["stl"])],
                       reads=[kdT_b[hh], const_b], writes=[bank_b[tb_]])
            P.op("act", lambda e: e.activation(out=kdtok[hh][:, 0:NT, :],
                                               in_=bv[:, 0:NT * 128].rearrange("p (i k) -> p i k", k=128), func=AF.Copy),
                 reads=[bank_b[tb_]], writes=[kdtok_b[hh]])

    def hgrn_side_steps(G):
        n = G["n"]
        C = 64
        nch = n // C
        S_all = S[:, :, :].rearrange("p h v -> p (h v)")
        Sbf_all = S_bf[:, :, :].rearrange("p h v -> p (h v)")

        def emit_A(j, ba):
            cols = slice(j * C, (j + 1) * C)
            base = (j % 2) * 64
            P.pe_group([mm(banks[ba][base:base + C, 256 + hh * 64:256 + hh * 64 + C], kdT[hh][:, cols], qdT[hh][:, cols], True, True)
                        for hh in range(4)], reads=kdT_b + qdT_b, writes=[bank_b[ba]])
            P.op("dve", lambda e: e.tensor_tensor(
                out=AmA[base:base + C, :].rearrange("p (h t) -> p h t", t=64),
                in0=banks[ba][base:base + C, 256:512].rearrange("p (h t) -> p h t", t=64),
                in1=causal[base:base + C, 0:C].unsqueeze(1).to_broadcast([C, 4, C]), op=ALU.mult),
                reads=[bank_b[ba], const_b], writes=[AmA_b])

        def chunk(j, ba):
            if j == 0:
                emit_A(0, ba)
            cols = slice(j * C, (j + 1) * C)
            k = j // 2
            base = (j % 2) * 64
            oc = 0
            fns = []
            for hh in range(4):
                vtok = ia_tok[base:base + C, k, hh * 128:(hh + 1) * 128]
                fns.append(mm(banks[ba][:, oc + hh * 64:oc + hh * 64 + C], S_bf[:, hh, :], qdT[hh][:, cols], True, False))
                fns.append(mm(banks[ba][:, oc + hh * 64:oc + hh * 64 + C], vtok, AmA[base:base + C, hh * 64:hh * 64 + C], False, True))
                fns.append(mm(banks[ba + 1][:, hh * 128:(hh + 1) * 128], kdtok[hh][base:base + C, k, :], vtok, True, True))
            P.pe_group(fns, reads=Sbf_b + qdT_b + kdtok_b + [ia_b, AmA_b], writes=[bank_b[ba], bank_b[ba + 1]])
            if j + 1 < nch:
                emit_A(j + 1, ba)
            P.op("act", lambda e: e.activation(out=oa_sb[:, :, cols],
                                               in_=banks[ba][:, oc:oc + 256].rearrange("p (h t) -> p h t", t=64), func=AF.Copy),
                 reads=[bank_b[ba]], writes=[oa_b])
            P.op("dve", lambda e: e.tensor_tensor(out=StA[:, :], in0=banks[ba + 1][:, :], in1=S_all, op=ALU.add),
                 reads=[bank_b[ba + 1]] + S_b, writes=[StA_b])
            P.op("dve", lambda e: e.tensor_tensor(out=S[:, :, :], in0=StA[:, :].rearrange("p (h v) -> p h v", v=128),
                                                  in1=ebl[:, :, j:j + 1].to_broadcast([128, 4, 128]), op=ALU.mult),
                 reads=[StA_b] + ebl_b, writes=S_b)
            P.op("dve", lambda e: e.tensor_tensor(out=S_bf[:, :, :], in0=StA[:, :].rearrange("p (h v) -> p h v", v=128),
                                                  in1=ebl[:, :, j:j + 1].to_broadcast([128, 4, 128]), op=ALU.mult),
                 reads=[StA_b] + ebl_b, writes=Sbf_b)

        def hsq(hh, ba):
            P.op("dve", lambda e: e.tensor_tensor(out=qdT[hh][:, 0:n], in0=oa_sb[:, hh, 0:n], in1=oa_sb[:, hh, 0:n], op=ALU.mult),
                 reads=[oa_b], writes=[qdT_b[hh]])

        def hout(hh, ba):
            osq = qdT[hh]
            osq_b = qdT_b[hh]
            P.pe_group([mm(banks[ba][:, 0:n], ones[:, :], osq[:, 0:n], True, True)], reads=[osq_b, const_b], writes=[bank_b[ba]])
            P.op("act", lambda e: e.activation(out=lnv[:, 0:n], in_=banks[ba][:, 0:n], func=AF.Ln, scale=1.0 / 128, bias=EPS),
                 reads=[bank_b[ba]], writes=[lnv_b])
            P.op("act", lambda e: e.activation(out=rsv[:, 0:n], in_=lnv[:, 0:n], func=AF.Exp, scale=-0.5),
                 reads=[lnv_b], writes=[rsv_b])
            P.op("dve", lambda e: e.tensor_tensor(out=t1[:, 0:n], in0=oa_sb[:, hh, 0:n], in1=rsv[:, 0:n], op=ALU.mult),
                 reads=[oa_b, rsv_b], writes=[t1_b])
            P.op("dve", lambda e: e.scalar_tensor_tensor(out=mixT[:, hh, 0:n], in0=t1[:, 0:n], scalar=gon[:, hh:hh + 1],
                                                         in1=gs[hh][:, 0:n], op0=ALU.mult, op1=ALU.mult),
                 reads=[t1_b, gs_b[hh], const_b], writes=[mixT_b])

        steps = [(lambda ba, j=j: chunk(j, ba)) for j in range(nch)] + [(lambda ba, hh=hh: hsq(hh, ba)) for hh in range(4)] + \
            [(lambda ba, hh=hh: hout(hh, ba)) for hh in range(4)]
        return steps

    def phase_wo(G):
        for half in range(2):
            sl, slb = get_w(("wo", half))
            for k, (col0, ntok) in enumerate(G["dt"]):
                bk = (half * len(G["dt"]) + k) % 4
                P.pe_group([mm(banks[bk][:ntok, :], mixT[:, kc, col0:col0 + ntok], sl[:, kc, :], kc == 0, kc == 7)
                            for kc in range(8)], reads=[slb, mixT_b], writes=[bank_b[bk]])
                hv = h[:ntok, k, half * 512:(half + 1) * 512]
                P.op("dve", lambda e: e.tensor_tensor(out=hv, in0=banks[bk][:ntok, :], in1=hv, op=ALU.add),
                     reads=[bank_b[bk], h_b[k]], writes=[h_b[k]])

    def phase_ffn(G, sample, seq, last_st, ptok0=None):
        n = G["n"]
        nsq = NSEQ_S if sample else 1
        L = n // nsq
        ui = 0
        for g0 in range(0, NFF, 4):
            ncc = min(4, NFF - g0)
            if ptok0 is not None and g0 == 0:
                p_load(G, False, ptok0)
            if ptok0 is not None and g0 == 12:
                p_transposes(G, False, ptok0)
            slg, slgb = get_w(("gate", g0))
            slu, slub = get_w(("up", g0), oldest=ws["last"])
            for cc in range(ncc):
                c = g0 + cc
                bg = (2 * c) % 8; bu = bg + 1
                P.pe_group([mm(banks[bg][:, 0:n], slg[:, kc, cc * 128:(cc + 1) * 128], aT[:, kc, 0:n], kc == 0, kc == 7)
                            for kc in range(8)], reads=[slgb, aT_b], writes=[bank_b[bg]])
                P.pe_group([mm(banks[bu][:, 0:n], slu[:, kc, cc * 128:(cc + 1) * 128], aT[:, kc, 0:n], kc == 0, kc == 7)
                            for kc in range(8)], reads=[slub, aT_b], writes=[bank_b[bu]])
                u = uext[ui % 2]; ub = uext_b[ui % 2]
                cv_ = ccv[ui % 2]; cb_ = ccv_b[ui % 2]
                sv = scv[ui % 2]; svb = scv_b[ui % 2]
                ui += 1
                u3 = u[:, 0:nsq * (L + 2)].rearrange("p (s l) -> p s l", l=L + 2)
                c3 = cv_[:, 0:n].rearrange("p (s l) -> p s l", l=L)
                ps3 = banks[bg][:, 0:n].rearrange("p (s l) -> p s l", l=L)
                P.op("act", lambda e: e.activation(out=u3[:, :, 2:L + 2], in_=ps3, func=AF.Copy), reads=[bank_b[bg]], writes=[ub])
                P.op("act", lambda e: e.activation(out=cv_[:, 0:n], in_=banks[bg][:, 0:n], func=AF.Identity,
                                                   scale=convw[:, c, 2:3], bias=convb[:, c:c + 1]),
                     reads=[bank_b[bg], const_b], writes=[cb_])
                P.op("dve", lambda e: e.tensor_copy(out=u3[:, :, 0:2], in_=convhist[:, c, 0:nsq, :]),
                     reads=[convhist_b[c]], writes=[ub])
                P.op("dve", lambda e: e.tensor_copy(out=convhist[:, c, 0:nsq, :], in_=u3[:, :, L:L + 2]),
                     reads=[ub], writes=[convhist_b[c]])
                P.op("dve", lambda e: e.scalar_tensor_tensor(out=c3, in0=u3[:, :, 1:L + 1], scalar=convw[:, c, 1:2], in1=c3,
                                                             op0=ALU.mult, op1=ALU.add),
                     reads=[ub, cb_, const_b], writes=[cb_])
                P.op("dve", lambda e: e.scalar_tensor_tensor(out=c3, in0=u3[:, :, 0:L], scalar=convw[:, c, 0:1], in1=c3,
                                                             op0=ALU.mult, op1=ALU.add),
                     reads=[ub, cb_, const_b], writes=[cb_])
                P.op("act", lambda e: e.activation(out=sv[:, 0:n], in_=cv_[:, 0:n], func=AF.Silu), reads=[cb_], writes=[svb])
                P.op("dve", lambda e: e.tensor_tensor(out=gT[:, c, 0:n], in0=sv[:, 0:n], in1=banks[bu][:, 0:n], op=ALU.mult),
                     reads=[svb, bank_b[bu]], writes=[gT_b])
                if sample:
                    for s_ in range(NSEQ_S):
                        P.dma("sp", conv_s[s_, :, c, :], convhist[:, c, s_, :], reads=[convhist_b[c]])
                elif last_st:
                    P.dma("sp", conv_p[seq, :, c, :], convhist[:, c, 0, :], reads=[convhist_b[c]])
        ndt = len(G["dt"])
        for half in range(2):
            accb = [half * 4 + k for k in range(ndt)]
            for g0 in range(0, NFF, 8):
                ncc = min(8, NFF - g0)
                sld, sldb = get_w(("down", half, g0))
                fns = []
                for cc in range(ncc):
                    c = g0 + cc
                    for k, (col0, ntok) in enumerate(G["dt"]):
                        fns.append(mm(banks[accb[k]][:ntok, :], gT[:, c, col0:col0 + ntok], sld[:, cc, :], c == 0, c == NFF - 1))
                P.pe_group(fns, reads=[sldb, gT_b], writes=[bank_b[b] for b in accb])
            for k, (col0, ntok) in enumerate(G["dt"]):
                hv = h[:ntok, k, half * 512:(half + 1) * 512]
                P.op("dve", lambda e: e.tensor_tensor(out=hv, in0=banks[accb[k]][:ntok, :], in1=hv, op=ALU.add),
                     reads=[bank_b[accb[k]], h_b[k]], writes=[h_b[k]])

    def p_load(G, sample, tok0):
        P.dma("pool", wpp_sb[:, 0:2, :], w_pp[:, 0:512].rearrange("(kc p) n -> p kc n", p=128), writes=[wpp_b])
        P.dma("pool", wpp_sb[:, 2:4, :], w_pp[:, 512:1024].rearrange("(kc p) n -> p kc n", p=128), writes=[wpp_b])
        for k, (col0, ntok) in enumerate(G["dt"]):
            src = (p_s if sample else p_p)[tok0 + col0: tok0 + col0 + ntok, :]
            P.dma("pool", pb[:ntok, k, :], src, writes=[pb_b])

    def p_transposes(G, sample, tok0):
        for k, (col0, ntok) in enumerate(G["dt"]):
            tb_ = 6 + (k % 2)
            bv = bank_bf(tb_)
            P.pe_group([(lambda e, c=c: e.transpose(out=bv[:, c * 128:c * 128 + ntok], in_=pb[:ntok, k, c * 128:(c + 1) * 128],
                                                    identity=ident[:ntok, :ntok])) for c in range(2)],
                       reads=[pb_b, const_b], writes=[bank_b[tb_]])
            P.op("act", lambda e: e.activation(out=pT[:, :, col0:col0 + ntok],
                                               in_=bv[:, 0:256].rearrange("p (c t) -> p c t", t=128)[:, :, 0:ntok], func=AF.Copy),
                 reads=[bank_b[tb_]], writes=[pT_b])

    def phase_ple_final_prompt(G, tok0, has_next):
        slp, slpb = wpp_sb, wpp_b
        ti = 0
        u_ = 0
        dt_ = G["dt"]
        nsis = {}
        P.handoff([gT_b], yst_b)
        for half in range(2):
            sl, slb = get_w(("pg", half))
            for k, (col0, ntok) in enumerate(dt_):
                bg = (2 * u_) % 6; bp = bg + 1
                u_ += 1
                P.pe_group([mm(banks[bg][:ntok, :], aT[:, kc, col0:col0 + ntok], sl[:, kc, :], kc == 0, kc == 7)
                            for kc in range(8)], reads=[slb, aT_b], writes=[bank_b[bg]])
                P.pe_group([mm(banks[bp][:ntok, :], pT[:, kc, col0:col0 + ntok], slp[:, 2 * half + kc, :], kc == 0, kc == 1)
                            for kc in range(2)], reads=[slpb, pT_b], writes=[bank_b[bp]])
                tgv = tg[ti % 2]; tgb = tg_b[ti % 2]; t2v = t2[ti % 2]; t2b = t2_b[ti % 2]
                ti += 1
                P.op("act", lambda e: e.activation(out=tgv[:ntok, :], in_=banks[bg][:ntok, :], func=AF.Tanh, scale=0.5),
                     reads=[bank_b[bg]], writes=[tgb])
                P.op("dve", lambda e: e.scalar_tensor_tensor(out=t2v[:ntok, :], in0=tgv[:ntok, :], scalar=1.0, in1=banks[bp][:ntok, :],
                                                             op0=ALU.add, op1=ALU.mult),
                     reads=[tgb, bank_b[bp]], writes=[t2b])
                cs_ = slice(half * 512, (half + 1) * 512)
                P.op("dve", lambda e: e.scalar_tensor_tensor(out=yst[:ntok, k, cs_], in0=t2v[:ntok, :], scalar=0.5, in1=h[:ntok, k, cs_],
                                                             op0=ALU.mult, op1=ALU.add),
                     reads=[t2b, h_b[k]], writes=[yst_b[k]])
                if half == 1 and has_next:
                    nt0 = tok0 + T
                    P.dma("sp", h[:, k, :], x_p[nt0 + k * 128: nt0 + (k + 1) * 128, :], writes=[h_b[k]])
                    if k >= 2:
                        kk = k - 2
                        nsis[kk] = norm_A(kk, 128)
                        norm_B(kk, 128, nsis[kk])
        if has_next:
            norm_C(0, 128)
            norm_C(1, 128)
            for kk in (2, 3):
                nsis[kk] = norm_A(kk, 128)
            norm_D(0, 0, 128, 0)
            norm_B(2, 128, nsis[2])
            norm_C(2, 128)
            norm_D(1, 128, 128, 0)
            norm_B(3, 128, nsis[3])
            norm_C(3, 128)
            norm_D(2, 256, 128, 0)
            norm_D(3, 384, 128, 0)
        for k, (col0, ntok) in enumerate(dt_):
            si = ss_n[0] % 16
            ss_n[0] += 1
            P.op("act", lambda e: e.activation(out=junk[:ntok, :], in_=yst[:ntok, k, :], func=AF.Square,
                                               accum_out=ss[:ntok, si:si + 1]),
                 reads=[yst_b[k]], writes=[junk_b, ss_b[si]])
            P.op("dve", lambda e: e.tensor_scalar(out=ms[:ntok, si:si + 1], in0=ss[:ntok, si:si + 1], scalar1=1.0 / D,
                                                  scalar2=EPS, op0=ALU.mult, op1=ALU.add),
                 reads=[ss_b[si]], writes=[ss_b[si]])
            P.op("pool", lambda e: e.tensor_tensor(out=rstd[:ntok, si:si + 1], in0=ms[:ntok, si:si + 1],
                                                   in1=neghalf[:ntok, 0:1], op=ALU.pow),
                 reads=[ss_b[si], const_b], writes=[ss_b[si]])
            P.op("dve", lambda e: e.scalar_tensor_tensor(out=yst[:ntok, k, :], in0=yst[:ntok, k, :], scalar=rstd[:ntok, si:si + 1],
                                                         in1=gfin[:ntok, :], op0=ALU.mult, op1=ALU.mult),
                 reads=[yst_b[k], ss_b[si], const_b], writes=[yst_b[k]])
            P.dma("sp", y_p[tok0 + col0: tok0 + col0 + ntok, :], yst[:ntok, k, :], reads=[yst_b[k]])

    def phase_ple(G, sample, tok0):
        p_load(G, sample, tok0)
        p_transposes(G, sample, tok0)
        slp, slpb = wpp_sb, wpp_b
        ti = 0
        for half in range(2):
            sl, slb = get_w(("pg", half))
            for k, (col0, ntok) in enumerate(G["dt"]):
                bg = (2 * (half * len(G["dt"]) + k)) % 8; bp = bg + 1
                P.pe_group([mm(banks[bg][:ntok, :], aT[:, kc, col0:col0 + ntok], sl[:, kc, :], kc == 0, kc == 7)
                            for kc in range(8)], reads=[slb, aT_b], writes=[bank_b[bg]])
                P.pe_group([mm(banks[bp][:ntok, :], pT[:, kc, col0:col0 + ntok], slp[:, 2 * half + kc, :], kc == 0, kc == 1)
                            for kc in range(2)], reads=[slpb, pT_b], writes=[bank_b[bp]])
                tgv = tg[ti % 2]; tgb = tg_b[ti % 2]; t2v = t2[ti % 2]; t2b = t2_b[ti % 2]
                ti += 1
                P.op("act", lambda e: e.activation(out=tgv[:ntok, :], in_=banks[bg][:ntok, :], func=AF.Tanh, scale=0.5),
                     reads=[bank_b[bg]], writes=[tgb])
                P.op("dve", lambda e: e.scalar_tensor_tensor(out=t2v[:ntok, :], in0=tgv[:ntok, :], scalar=1.0, in1=banks[bp][:ntok, :],
                                                             op0=ALU.add, op1=ALU.mult),
                     reads=[tgb, bank_b[bp]], writes=[t2b])
                hv = h[:ntok, k, half * 512:(half + 1) * 512]
                P.op("dve", lambda e: e.scalar_tensor_tensor(out=hv, in0=t2v[:ntok, :], scalar=0.5, in1=hv, op0=ALU.mult, op1=ALU.add),
                     reads=[t2b, h_b[k]], writes=[h_b[k]])

    def phase_final(G, sample, tok0):
        for k, (col0, ntok) in enumerate(G["dt"]):
            si = ss_n[0] % 16
            ss_n[0] += 1
            P.op("act", lambda e: e.activation(out=junk[:ntok, :], in_=h[:ntok, k, :], func=AF.Square, accum_out=ss[:ntok, si:si + 1]),
                 reads=[h_b[k]], writes=[junk_b, ss_b[si]])
            P.op("dve", lambda e: e.tensor_scalar(out=ms[:ntok, si:si + 1], in0=ss[:ntok, si:si + 1], scalar1=1.0 / D, scalar2=EPS,
                                                  op0=ALU.mult, op1=ALU.add), reads=[ss_b[si]], writes=[ss_b[si]])
            P.op("pool", lambda e: e.tensor_tensor(out=rstd[:ntok, si:si + 1], in0=ms[:ntok, si:si + 1], in1=neghalf[:ntok, 0:1],
                                                   op=ALU.pow), reads=[ss_b[si], const_b], writes=[ss_b[si]])
            P.op("dve", lambda e: e.scalar_tensor_tensor(out=h[:ntok, k, :], in0=h[:ntok, k, :], scalar=rstd[:ntok, si:si + 1],
                                                         in1=gfin[:ntok, :], op0=ALU.mult, op1=ALU.mult),
                 reads=[h_b[k], ss_b[si], const_b], writes=[h_b[k]])
            dst = (y_s if sample else y_p)[tok0 + col0: tok0 + col0 + ntok, :]
            P.dma("sp", dst, h[:ntok, k, :], reads=[h_b[k]])

    Gp = {"n": T, "dt": [(i * 128, 128) for i in range(NT)], "stl": [(i * 128, 128) for i in range(NT)]}
    for seq in range(NSEQ_P):
        for hh in range(4):
            P.op("dve", lambda e, hh=hh: e.memset(S[:, hh, :], 0.0), writes=[S_b[hh]])
            P.op("dve", lambda e, hh=hh: e.memset(S_bf[:, hh, :], 0.0), writes=[Sbf_b[hh]])
        for c in range(NFF):
            P.op("pool", lambda e, c=c: e.memset(convhist[:, c, :, :], 0.0), writes=[convhist_b[c]])
        for st in range(NST):
            tok0 = seq * SEQ + st * T
            first = (seq == 0 and st == 0)
            last = (seq == NSEQ_P - 1 and st == NST - 1)
            P.mark('norm0')
            ci_ = seq * NST + st
            if ci_ < 2 * NSEQ_S:
                src_, dst_ = (ck_in, ck_s) if ci_ % 2 == 0 else (cv_in, cv_s)
                P.dma("sp", dst_[ci_ // 2, 0:WB - TS, :], src_[ci_ // 2, TS:WB, :], ring="spc")
            if first:
                for k in range(NT):
                    P.dma("sp", h[:, k, :], x_p[tok0 + k * 128: tok0 + (k + 1) * 128, :], writes=[h_b[k]])
                norm_T(Gp, 0)
            P.handoff(regH_B, regH_A)
            P.handoff(regM_B, regM_A)
            P.mark('win')
            phase_win(Gp, False, seq, st)
            P.mark('gates')
            hgrn_g2(Gp)
            P.handoff([lf_b, bb_b, eb_b, enb_b], [oa_b])
            P.mark('attn')
            phase_attn_prompt(Gp, st, hgrn_side_steps(Gp))
            P.handoff([oa_b], [lf_b, bb_b, eb_b, enb_b])
            P.mark('wo')
            phase_wo_norm1(Gp)
            P.mark('norm1')
            P.handoff(regH_A, regH_B)
            P.handoff(regM_A, regM_B)
            P.mark('ffn')
            phase_ffn(Gp, False, seq, st == NST - 1, ptok0=tok0)
            P.mark('norm2')
            norm_T(Gp, 2)
            P.mark('ple')
            phase_ple_final_prompt(Gp, tok0, not last)
            P.mark('final')
        for hh in range(4):
            P.dma("sp", st_p[seq, hh, :, :], S[:, hh, :], reads=[S_b[hh]])

    P.mark('sample')
    Gs = {"n": NSEQ_S * TS, "dt": [(0, NSEQ_S * TS)], "stl": [(s_ * TS, TS) for s_ in range(NSEQ_S)]}
    for c in range(NFF):
        P.dma("sp", convhist[:, c, :, :], conv_in[:, c, :, :], writes=[convhist_b[c]])
    P.dma("sp", h[0:NSEQ_S * TS, 0, :], x_s[:, :], writes=[h_b[0]])
    norm_T(Gs, 0)
    P.handoff(regH_B, regH_A)
    P.handoff(regM_B, regM_A)
    n = Gs["n"]
    for blk, kind in ((1, "f"), (0, "q"), (3, "g")):
        sl, slb = get_w(("win", blk))
        for hh in range(4):
            bk = hh
            P.pe_group([mm(banks[bk][:, 0:n], sl[:, kc, hh * 128:(hh + 1) * 128], aT[:, kc, 0:n], kc == 0, kc == 7)
                        for kc in range(8)], reads=[slb, aT_b], writes=[bank_b[bk]])
            if kind == "f":
                P.op("act", lambda e: e.activation(out=th[hh][:, 0:n], in_=banks[bk][:, 0:n], func=AF.Tanh, scale=0.5),
                     reads=[bank_b[bk]], writes=[th_b[hh]])
            elif kind == "q":
                P.op("act", lambda e: e.activation(out=sq[hh][:, 0:n], in_=banks[bk][:, 0:n], func=AF.Silu),
                     reads=[bank_b[bk]], writes=[sq_b[hh]])
            else:
                P.op("act", lambda e: e.activation(out=gs[hh][:, 0:n], in_=banks[bk][:, 0:n], func=AF.Silu),
                     reads=[bank_b[bk]], writes=[gs_b[hh]])
    sl, slb = get_w(("win", 2))
    for k, (col0, ntok) in enumerate(Gs["stl"]):
        bk = k % 4
        P.pe_group([mm(banks[bk][:ntok, :], aT[:, kc, col0:col0 + ntok], sl[:, kc, :], kc == 0, kc == 7) for kc in range(8)],
                   reads=[slb, aT_b], writes=[bank_b[bk]])
        P.op("act", lambda e: e.activation(out=ia_tok[:ntok, k, :], in_=banks[bk][:ntok, :], func=AF.Copy),
             reads=[bank_b[bk]], writes=[ia_b])
    hgrn_gates(Gs, True)
    hgrn_chunks(Gs, True)
    hgrn_out(Gs)
    slq, slqb = get_w(("win", 4))
    for k, (col0, ntok) in enumerate(Gs["stl"]):
        bk = k % 4
        P.pe_group([mm(banks[bk][:ntok, :], aT[:, kc, col0:col0 + ntok], slq[:, kc, :], kc == 0, kc == 7) for kc in range(8)],
                   reads=[slqb, aT_b], writes=[bank_b[bk]])
        P.op("act", lambda e: e.activation(out=stage[0][:ntok, :], in_=banks[bk][:ntok, :], func=AF.Copy),
             reads=[bank_b[bk]], writes=[stage_b[0]])
        rotary(Gs, stage[0], css, ntok, [stage_b[0]])
        P.op("dve", lambda e: e.tensor_copy(out=qkb[0][:ntok, :], in_=stage[0][:ntok, :]), reads=[stage_b[0]], writes=[qkb_b[0]])
        bv = bank_bf(6)
        P.pe_group([(lambda e, c=c: e.transpose(out=bv[:, c * 128:c * 128 + ntok], in_=qkb[0][:ntok, c * 128:(c + 1) * 128],
                                                identity=ident[:ntok, :ntok])) for c in range(4)],
                   reads=[qkb_b[0], const_b], writes=[bank_b[6]])
        P.op("act", lambda e: e.activation(out=QT[:, :, col0:col0 + ntok],
                                           in_=bv[:, 0:512].rearrange("p (c t) -> p c t", t=128)[:, :, 0:ntok], func=AF.Copy, scale=0.125),
             reads=[bank_b[6]], writes=[QT_b])
    slk, slkb = get_w(("win", 5))
    slv, slvb = get_w(("win", 6), oldest=ws["last"])
    for s_, (col0, ntok) in enumerate(Gs["stl"]):
        P.pe_group([mm(banks[0][:ntok, :], aT[:, kc, col0:col0 + ntok], slk[:, kc, :], kc == 0, kc == 7) for kc in range(8)],
                   reads=[slkb, aT_b], writes=[bank_b[0]])
        P.op("act", lambda e: e.activation(out=stage[1][:ntok, :], in_=banks[0][:ntok, :], func=AF.Copy),
             reads=[bank_b[0]], writes=[stage_b[1]])
        rotary(Gs, stage[1], css, ntok, [stage_b[1]])
        P.dma("sp", ck_s[s_, WB - TS:WB, :], stage[1][:ntok, :], reads=[stage_b[1]])
        P.op("dve", lambda e: e.tensor_copy(out=qkb[1][:ntok, :], in_=stage[1][:ntok, :]), reads=[stage_b[1]], writes=[qkb_b[1]])
        bv = bank_bf(7)
        P.pe_group([(lambda e, c=c: e.transpose(out=bv[:, c * 128:c * 128 + ntok], in_=qkb[1][:ntok, c * 128:(c + 1) * 128],
                                                identity=ident[:ntok, :ntok])) for c in range(4)],
                   reads=[qkb_b[1], const_b], writes=[bank_b[7]])
        P.op("act", lambda e: e.activation(out=KTn[:, :, 0:ntok],
                                           in_=bv[:, 0:512].rearrange("p (c t) -> p c t", t=128)[:, :, 0:ntok], func=AF.Copy),
             reads=[bank_b[7]], writes=[KTn_b])
        P.pe_group([mm(banks[1][:ntok, :], aT[:, kc, col0:col0 + ntok], slv[:, kc, :], kc == 0, kc == 7) for kc in range(8)],
                   reads=[slvb, aT_b], writes=[bank_b[1]])
        P.op("act", lambda e: e.activation(out=stage[2][:ntok, :], in_=banks[1][:ntok, :], func=AF.Copy),
             reads=[bank_b[1]], writes=[stage_b[2]])
        P.dma("sp", cv_s[s_, WB - TS:WB, :], stage[2][:ntok, :], reads=[stage_b[2]])
        P.op("dve", lambda e: e.tensor_copy(out=Vn[:ntok, :], in_=stage[2][:ntok, :]), reads=[stage_b[2]], writes=[Vn_b])
        if s_ == 0:
            sKT_b = [Buf("sKT%d" % i) for i in range(7)]; sVH_b = [Buf("sVH%d" % i) for i in range(7)]
            P.handoff([KT_b], sKT_b)
            P.handoff([VH_b], sVH_b)
        for sl_ in range(7):
            ci = sl_ % 2
            if sl_ < 3:
                for r_ in range(4):
                    srck = ck_in[s_, 512 * sl_:512 * (sl_ + 1), :].rearrange("(m r) c -> r m c", r=16)[r_]
                    srcv = cv_in[s_, 512 * sl_:512 * (sl_ + 1), :].rearrange("(m r) c -> r m c", r=16)[r_]
                    P.dma("pool", cbf[ci][32 * r_:32 * (r_ + 1), :], srck, writes=[cbf_b[ci]])
                    P.dma("pool", VH[32 * r_:32 * (r_ + 1), sl_, :], srcv, writes=[sVH_b[sl_]])
            else:
                kt = 9 + sl_
                P.dma("pool", cbf[ci][:, :], ck_in[s_, kt * 128:(kt + 1) * 128, :], writes=[cbf_b[ci]])
                P.dma("pool", VH[:, sl_, :], cv_in[s_, kt * 128:(kt + 1) * 128, :], writes=[sVH_b[sl_]])
            tb_ = 6 + (sl_ % 2)
            bv = bank_bf(tb_)
            P.pe_group([(lambda e, c=c: e.transpose(out=bv[:, c * 128:(c + 1) * 128], in_=cbf[ci][:, c * 128:(c + 1) * 128],
                                                    identity=ident[:, :])) for c in range(4)],
                       reads=[cbf_b[ci], const_b], writes=[bank_b[tb_]])
            P.op("act", lambda e: e.activation(out=KT[:, :, sl_ * 128:(sl_ + 1) * 128],
                                               in_=bv[:, 0:512].rearrange("p (c t) -> p c t", t=128), func=AF.Copy),
                 reads=[bank_b[tb_]], writes=[sKT_b[sl_]])
        qc = slice(col0, col0 + TS)
        sitems = []
        for sl_ in range(7):
            mi = 17 if sl_ < 3 else 16 - (9 + sl_)
            sitems.append((lambda pr, e2, sl_=sl_: KT[64 * e2:64 * e2 + 64, pr, sl_ * 128:(sl_ + 1) * 128],
                           lambda hd, sl_=sl_: VH[:, sl_, hd * 64:(hd + 1) * 64],
                           maskr[:, mi, 0:TS], 128, [sKT_b[sl_], sVH_b[sl_]]))
        sitems.append((lambda pr, e2: KTn[64 * e2:64 * e2 + 64, pr, 0:TS],
                       lambda hd: Vn[0:TS, hd * 64:(hd + 1) * 64],
                       maskr[0:TS, 0, 0:TS], TS, [KTn_b, Vn_b]))
        pend = []

        def s_qk(idx, it):
            ktf, vf, m_ap, nk, rd = it
            i2 = idx % 2
            sbk = 2 + 3 * i2
            sc2 = psum_all[:, sbk * 512:(sbk + 2) * 512].rearrange("p (e n) -> p e n", n=512)
            P.pe_group([mm(sc2[:nk, e2, pr * TS:(pr + 1) * TS], ktf(pr, e2), QT[64 * e2:64 * e2 + 64, pr, qc], True, True)
                        for pr in range(4) for e2 in range(2)], reads=rd + [QT_b], writes=[bank_b[sbk], bank_b[sbk + 1]])
            P.op("act", lambda e: e.activation(out=Pexp[i2][:nk, :, 0:16], in_=sc2[:nk, :, 0:16], func=AF.Exp),
                 reads=[bank_b[sbk], bank_b[sbk + 1]], writes=[Pexp_b[i2]])
            P.op("dve", lambda e: e.tensor_tensor(out=Pm[i2][:nk, :, 0:16].rearrange("p e (a t) -> p e a t", t=TS),
                                                  in0=Pexp[i2][:nk, :, 0:16].rearrange("p e (a t) -> p e a t", t=TS),
                                                  in1=m_ap.unsqueeze(1).unsqueeze(1).to_broadcast([nk, 2, 4, TS]), op=ALU.mult),
                 reads=[Pexp_b[i2], const_b], writes=[Pm_b[i2]])

        def s_pv(idx, it):
            ktf, vf, m_ap, nk, rd = it
            i2 = idx % 2
            obk = 4 + 3 * i2
            fns = []
            for pr in range(4):
                for e2 in range(2):
                    hd = 2 * pr + e2
                    rhs = Pm[i2][:nk, e2, pr * TS:(pr + 1) * TS]
                    fns.append(mm(banks[obk][64 * e2:64 * e2 + 64, pr * TS:(pr + 1) * TS], vf(hd), rhs, True, True))
                    fns.append(mm(banks[obk][64 * e2:64 * e2 + 64, 16 + pr * TS:16 + (pr + 1) * TS], ones[:nk, 0:64], rhs, True, True))
            P.pe_group(fns, reads=rd + [Pm_b[i2], const_b], writes=[bank_b[obk]])
            if idx == 0:
                P.op("dve", lambda e: e.tensor_copy(out=sacc[:, :], in_=banks[obk][:, 0:32]), reads=[bank_b[obk]], writes=[sacc_b])
            else:
                P.op("dve", lambda e: e.tensor_tensor(out=sacc[:, :], in0=banks[obk][:, 0:32], in1=sacc[:, :], op=ALU.add),
                     reads=[bank_b[obk], sacc_b], writes=[sacc_b])

        for idx, it in enumerate(sitems):
            s_qk(idx, it)
            pend.append((idx, it))
            if len(pend) > 1:
                s_pv(*pend.pop(0))
        while pend:
            s_pv(*pend.pop(0))
        P.op("dve", lambda e: e.reciprocal(out=sacc[:, 16:32], in_=sacc[:, 16:32]), reads=[sacc_b], writes=[sacc_b])
        P.op("dve", lambda e: e.tensor_tensor(out=mixT[:, 4:8, qc], in0=sacc[:, 0:16].rearrange("p (a t) -> p a t", t=TS),
                                              in1=sacc[:, 16:32].rearrange("p (a t) -> p a t", t=TS), op=ALU.mult),
             reads=[sacc_b], writes=[mixT_b])
    phase_wo(Gs)
    norm_T(Gs, 1)
    P.handoff(regH_A, regH_B)
    P.handoff(regM_A, regM_B)
    phase_ffn(Gs, True, 0, True)
    norm_T(Gs, 2)
    phase_ple(Gs, True, 0)
    phase_final(Gs, True, 0)

    P.mark('end')
    P.final_wait()
    return nc, P


_CACHE = {}


def _consts():
    maskr = np.zeros((20, 128, 128), np.float32)
    kk = np.arange(128)[:, None]
    qq = np.arange(128)[None, :]
    for off in range(17):
        diff = 128 * off + qq - kk
        m = np.zeros((128, 128), np.float32)
        for dil in (1, 4, 16):
            m += ((diff >= 0) & (diff % dil == 0) & (diff <= 128 * dil)).astype(np.float32)
        maskr[off] = m
    for t_ in range(TS):
        maskr[17, 32 * t_:32 * (t_ + 1), t_] = 1.0
    ii = np.arange(128)[:, None]; jj = np.arange(128)[None, :]
    maskr[18] = ((ii - jj) % 4 == 0).astype(np.float32)
    maskr[19] = (((ii - jj) % 4 == 0) & (jj <= ii)).astype(np.float32)
    maskr = np.ascontiguousarray(maskr.transpose(1, 0, 2))
    s = np.arange(64)[:, None]
    t = np.arange(64)[None, :]
    cz = (s <= t).astype(np.float32)
    causal = np.concatenate([cz, cz], axis=0)
    half = 8
    inv_freq = np.power(np.float32(500000.0), -np.arange(half, dtype=np.float32) * np.float32(2.0 / 16)).astype(np.float32)

    def cs(pos):
        ang = (pos.astype(np.float32)[:, None] * inv_freq[None, :]).astype(np.float32)
        return np.concatenate([np.cos(ang), np.sin(ang)], axis=1).astype(np.float32)

    csp = cs(np.arange(SEQ)).reshape(SEQ // 128, 128, 16).transpose(1, 0, 2)
    css = cs(PAST + np.arange(TS))
    return maskr, causal, np.ascontiguousarray(csp), np.ascontiguousarray(css)


def kernel(x_prompt, x_sample, state_hgrn, cache_win_k, cache_win_v, state_ffn_conv, p_prompt, p_sample,
           norm_attn_g, w_in, hgrn_lb_logits, hgrn_onorm_g, w_o, norm_ffn_g, w_gate, w_up, conv_w, conv_b,
           w_down, norm_ple_g, w_ple_gate, w_ple_proj, norm_final_g):
    f = lambda a: np.ascontiguousarray(np.asarray(a, dtype=np.float32))
    if "nc" not in _CACHE:
        _CACHE["nc"] = build_program()[0]
        _CACHE["consts"] = _consts()
    nc = _CACHE["nc"]
    maskr, causal, csp, css = _CACHE["consts"]
    x_prompt = f(x_prompt); x_sample = f(x_sample); p_prompt = f(p_prompt); p_sample = f(p_sample)
    state_hgrn = f(state_hgrn); cache_win_k = f(cache_win_k); cache_win_v = f(cache_win_v)
    state_ffn_conv = f(state_ffn_conv)

    def cols(g, nchunk):
        return np.ascontiguousarray(f(g).reshape(nchunk, 128).T)

    gcols = np.ascontiguousarray(np.stack([cols(norm_attn_g[0], 8), cols(norm_ffn_g[0], 8), cols(norm_ple_g[0], 8)], axis=1))
    gfin = np.ascontiguousarray(np.broadcast_to(f(norm_final_g)[None, :], (128, D)))
    lbl = np.ascontiguousarray(f(hgrn_lb_logits).reshape(2, 4, 128).transpose(2, 0, 1))
    gon = cols(hgrn_onorm_g[0], 4)
    convw = np.ascontiguousarray(f(conv_w)[0].reshape(3, NFF, 128).transpose(2, 1, 0))
    convb = cols(conv_b[0], NFF)
    shared = {
        "w_in": f(w_in)[0], "w_o": f(w_o)[0], "w_gate": f(w_gate)[0], "w_up": f(w_up)[0], "w_down": f(w_down)[0],
        "w_pg": f(w_ple_gate)[0], "w_pp": f(w_ple_proj)[0], "gcols": gcols, "gfin": gfin, "lbl": lbl, "gon": gon,
        "convw": convw, "convb": convb, "maskr": maskr, "causal": causal, "csp": csp, "css": css,
    }
    in_maps = []
    for c in range(NCORES):
        ps = slice(NSEQ_P * c, NSEQ_P * (c + 1))
        ss_ = slice(NSEQ_S * c, NSEQ_S * (c + 1))
        m = dict(shared)
        m["x_p"] = x_prompt[ps].reshape(NSEQ_P * SEQ, D)
        m["p_p"] = p_prompt[0, ps].reshape(NSEQ_P * SEQ, PLE)
        m["x_s"] = x_sample[ss_].reshape(NSEQ_S * TS, D)
        m["p_s"] = p_sample[0, ss_].reshape(NSEQ_S * TS, PLE)
        m["st_in"] = state_hgrn[0, ss_]
        m["ck_in"] = cache_win_k[0, ss_].reshape(NSEQ_S, WB, 512)
        m["cv_in"] = cache_win_v[0, ss_].reshape(NSEQ_S, WB, 512)
        m["conv_in"] = np.ascontiguousarray(state_ffn_conv[0, ss_].reshape(NSEQ_S, 2, NFF, 128).transpose(3, 2, 0, 1))
        in_maps.append(m)
    res = run_bass_kernel_spmd(nc, in_maps, core_ids=list(range(NCORES)))
    R = res.results
    cat = lambda k: np.concatenate([np.asarray(r[k]) for r in R], axis=0)
    y_p = cat("y_p").reshape(16, SEQ, D)
    y_s = cat("y_s").reshape(32, TS, D)
    st_p = cat("st_p")[None]
    ck_p = cat("ck_p").reshape(1, 16, SEQ, 8, 64)
    cv_p = cat("cv_p").reshape(1, 16, SEQ, 8, 64)
    conv_p = cat("conv_p").transpose(0, 3, 2, 1).reshape(1, 16, 2, DFF)
    st_s = cat("st_s")[None]
    ck_s = cat("ck_s").reshape(1, 32, WB, 8, 64)
    cv_s = cat("cv_s").reshape(1, 32, WB, 8, 64)
    conv_s = cat("conv_s").transpose(0, 3, 2, 1).reshape(1, 32, 2, DFF)
    outs = (y_p, y_s, st_p, ck_p, cv_p, conv_p, st_s, ck_s, cv_s, conv_s)
    return tuple(np.ascontiguousarray(o, dtype=np.float32) for o in outs)
```

```python
import sys
import numpy as np
import concourse.bass as bass
import concourse.mybir as mybir
from concourse.bass_utils import run_bass_kernel_spmd

F32 = mybir.dt.float32
BF16 = mybir.dt.bfloat16
AF = mybir.ActivationFunctionType
ALU = mybir.AluOpType

NCORES = 8
D = 1024
DIN = 3584
DFF = 2816
NFF = 22
PLE = 256
SEQ = 2048
T = 512
NT = 4
NST = SEQ // T
NSEQ_P = 2
NSEQ_S = 4
TS = 4
EPS = 1e-6
WB = 2048
PAST = 16384
NSLAB = 4


class Buf:
    __slots__ = ("name", "w", "r", "excl")

    def __init__(self, name, excl=False):
        self.name = name
        self.w = None
        self.r = {}
        self.excl = excl


class Prog:
    def __init__(self, nc):
        self.nc = nc
        self.eng = {"pe": nc.tensor, "act": nc.scalar, "dve": nc.vector, "pool": nc.gpsimd, "sp": nc.sync}
        self.sem = {}
        self.cnt = {}
        for e in ("pe", "act", "dve", "pool"):
            self.sem[e] = nc.alloc_semaphore(name="sem_" + e)
            self.cnt[e] = 0
        self.known = {e: {} for e in self.eng}
        self.ring = {}
        self.ring_n = {}
        for q, k in (("sp", 12), ("pool", 6), ("spc", 8)):
            self.ring[q] = [[nc.alloc_semaphore(name="dsem_%s%d" % (q, i)), 0] for i in range(k)]
            self.ring_n[q] = 0
        self.n_ins = 0
        self.marks = []
        self.desc = {}

    def _deps(self, reads, writes, e=None):
        toks = []
        for b in reads:
            if b.w is not None:
                toks.append(b.w)
            if b.excl:
                toks.extend(t for t in b.r.values() if t[3] != e)
        for b in writes:
            if b.w is not None:
                toks.append(b.w)
            toks.extend(b.r.values())
        return toks

    def _wait(self, e, toks):
        best = {}
        for t in toks:
            key, sem, val, src = t
            if src == "pe" and e == "pe":
                continue
            if key not in best or best[key][2] < val:
                best[key] = t
        for key, (k2, sem, val, src) in best.items():
            if self.known[e].get(key, 0) >= val:
                continue
            self.eng[e].wait_ge(sem, val)
            self.known[e][key] = val
            self.n_ins += 1

    def _mark(self, tok, reads, writes):
        for b in reads:
            b.r[tok[0]] = tok
        for b in writes:
            b.w = tok
            b.r = {}

    def op(self, e, fn, reads=(), writes=()):
        self._wait(e, self._deps(reads, writes, e))
        ins = fn(self.eng[e])
        self.cnt[e] += 1
        ins.then_inc(self.sem[e], 1)
        tok = (e, self.sem[e], self.cnt[e], e)
        self._mark(tok, reads, writes)
        self.n_ins += 1
        fr = sys._getframe(1)
        self.desc[(e, self.cnt[e])] = "%s:%d" % (fr.f_code.co_name, fr.f_lineno)
        return ins

    def pe_group(self, fns, reads=(), writes=()):
        self._wait("pe", self._deps(reads, writes))
        ins = None
        for fn in fns:
            ins = fn(self.eng["pe"])
            self.n_ins += 1
        self.cnt["pe"] += 1
        ins.then_inc(self.sem["pe"], 1)
        tok = ("pe", self.sem["pe"], self.cnt["pe"], "pe")
        self._mark(tok, reads, writes)
        fr = sys._getframe(1)
        self.desc[("pe", self.cnt["pe"])] = "%s:%d" % (fr.f_code.co_name, fr.f_lineno)

    def dma(self, q, out, in_, reads=(), writes=(), ring=None, **kw):
        rq = ring or q
        ring = self.ring[rq]
        slot = ring[self.ring_n[rq] % len(ring)]
        self.ring_n[rq] += 1
        toks = self._deps(reads, writes)
        key = "d_%s_%d" % (rq, id(slot))
        if slot[1] > 0:
            toks.append((key, slot[0], slot[1], "dma"))
        self._wait(q, toks)
        if q == "pool":
            kw.setdefault("max_dma_last_dim", 4096)
        ins = self.eng[q].dma_start(out=out, in_=in_, **kw)
        slot[1] += 16
        ins.then_inc(slot[0], 16)
        tok = (key, slot[0], slot[1], "dma")
        self._mark(tok, reads, writes)
        self.n_ins += 1

    def mark(self, name):
        self.marks.append((name, dict(self.cnt)))

    def handoff(self, old, new):
        toks = {}
        for b in old:
            if b.w is not None:
                t = b.w
                if t[0] not in toks or toks[t[0]][2] < t[2]:
                    toks[t[0]] = t
            for t in b.r.values():
                if t[0] not in toks or toks[t[0]][2] < t[2]:
                    toks[t[0]] = t
        for b in new:
            for k, t in toks.items():
                if k not in b.r or b.r[k][2] < t[2]:
                    b.r[k] = t

    def final_wait(self):
        for q in self.ring:
            for slot in self.ring[q]:
                if slot[1] > 0:
                    self.eng["sp"].wait_ge(slot[0], slot[1])


def build_program():
    nc = bass.Bass("TRN2", target_bir_lowering=False)
    P = Prog(nc)

    def din(name, shape, dt=F32):
        return nc.dram_tensor(name, list(shape), dt, kind="ExternalInput").ap()

    def dout(name, shape):
        return nc.dram_tensor(name, list(shape), F32, kind="ExternalOutput").ap()

    x_p = din("x_p", [NSEQ_P * SEQ, D])
    p_p = din("p_p", [NSEQ_P * SEQ, PLE])
    x_s = din("x_s", [NSEQ_S * TS, D])
    p_s = din("p_s", [NSEQ_S * TS, PLE])
    st_in = din("st_in", [NSEQ_S, 4, 128, 128])
    ck_in = din("ck_in", [NSEQ_S, WB, 512])
    cv_in = din("cv_in", [NSEQ_S, WB, 512])
    conv_in = din("conv_in", [128, NFF, NSEQ_S, 2])
    w_in = din("w_in", [D, DIN])
    w_o = din("w_o", [D, D])
    w_gate = din("w_gate", [D, DFF])
    w_up = din("w_up", [D, DFF])
    w_down = din("w_down", [DFF, D])
    w_pg = din("w_pg", [D, D])
    w_pp = din("w_pp", [PLE, D])
    gcols_d = din("gcols", [128, 3, 8])
    gfin_d = din("gfin", [128, D])
    lbl_d = din("lbl", [128, 2, 4])
    gon_d = din("gon", [128, 4])
    convw_d = din("convw", [128, NFF, 3])
    convb_d = din("convb", [128, NFF])
    maskr_d = din("maskr", [128, 20, 128])
    causal_d = din("causal", [128, 64])
    csp_d = din("csp", [128, SEQ // 128, 16])
    css_d = din("css", [TS, 16])

    y_p = dout("y_p", [NSEQ_P * SEQ, D])
    y_s = dout("y_s", [NSEQ_S * TS, D])
    st_p = dout("st_p", [NSEQ_P, 4, 128, 128])
    ck_p = dout("ck_p", [NSEQ_P, SEQ, 512])
    cv_p = dout("cv_p", [NSEQ_P, SEQ, 512])
    conv_p = dout("conv_p", [NSEQ_P, 128, NFF, 2])
    st_s = dout("st_s", [NSEQ_S, 4, 128, 128])
    ck_s = dout("ck_s", [NSEQ_S, WB, 512])
    cv_s = dout("cv_s", [NSEQ_S, WB, 512])
    conv_s = dout("conv_s", [NSEQ_S, 128, NFF, 2])
    out_buf = Buf("outputs")

    def sb(name, shape, dt=F32):
        return nc.alloc_sbuf_tensor("sb_" + name, list(shape), dt)

    psum_all = nc.alloc_psum_tensor("psum_all", [128, 8 * 512], F32)
    banks = [psum_all[:, i * 512:(i + 1) * 512] for i in range(8)]
    bank_b = [Buf("bank%d" % i, excl=True) for i in range(8)]

    ident = sb("ident", [128, 128], BF16)
    ones = sb("ones", [128, 128], BF16)
    causal = sb("causal", [128, 64], BF16)
    maskr = sb("maskr", [128, 20, 128], BF16)
    csp = sb("csp", [128, SEQ // 128, 16]); css = sb("css", [TS, 16])
    gcols = sb("gcols", [128, 3, 8]); gfin = sb("gfin", [128, D])
    lbl = sb("lbl", [128, 2, 4]); gon = sb("gon", [128, 4])
    c0 = sb("c0", [128, 4]); c1 = sb("c1", [128, 4]); nc1 = sb("nc1", [128, 4]); lbt = sb("lbt", [128, 4])
    convw = sb("convw", [128, NFF, 3]); convb = sb("convb", [128, NFF])
    rmask_p = sb("rmask_p", [128, T], BF16); rmask_s = sb("rmask_s", [128, 16], BF16)
    neghalf = sb("neghalf", [128, 8])
    const_b = Buf("const")

    KT = sb("KT", [128, 4, SEQ], BF16); KT_b = Buf("KT")
    VH = sb("VH", [128, 8, 512], BF16); VH_b = Buf("VH")
    Vp = sb("Vp", [128, 4, 4, 512], BF16); Vp_b = [Buf("Vp%d" % i) for i in range(4)]
    S = sb("S", [128, 4, 128]); S_bf = sb("S_bf", [128, 4, 128], BF16)
    S_b = [Buf("S%d" % i) for i in range(4)]; Sbf_b = [Buf("Sbf%d" % i) for i in range(4)]
    convhist = sb("convhist", [128, NFF, NSEQ_S, 2]); convhist_b = [Buf("ch%d" % i) for i in range(NFF)]

    h = sb("h", [128, NT, D]); h_b = [Buf("h%d" % i) for i in range(NT)]
    ss = sb("ss", [128, 16]); ms = sb("ms", [128, 16]); rstd = sb("rstd", [128, 16])
    ss_b = [Buf("ss%d" % i) for i in range(16)]
    aT = sb("aT", [128, 8, T], BF16); aT_b = Buf("aT")

    regM = sb("regM", [128, 22 * 1024 // 4])
    regM_bf = regM[:].bitcast(BF16)
    mixT = regM_bf[:, 0:4096].rearrange("p (c t) -> p c t", t=T)
    QT = regM_bf[:, 4096:6144].rearrange("p (c t) -> p c t", t=T)
    ia_tok = regM_bf[:, 6144:8192].rearrange("p (c t) -> p c t", t=512)
    stage = [regM[:, 4096 + i * 512: 4096 + (i + 1) * 512] for i in range(3)]
    gT = regM_bf.rearrange("p (c t) -> p c t", t=T)
    mixT_b = Buf("mixT"); QT_b = Buf("QT"); ia_b = Buf("ia"); stage_b = [Buf("stg%d" % i) for i in range(3)]
    gT_b = Buf("gT")
    yst = regM[:, 0:4096].rearrange("p (k d) -> p k d", d=D)
    yst_b = [Buf("yst%d" % i) for i in range(NT)]
    regM_A = [mixT_b, QT_b, ia_b] + stage_b
    regM_B = [gT_b] + yst_b
    qkb = [sb("qkb%d" % i, [128, 512], BF16) for i in range(4)]; qkb_b = [Buf("qkb%d" % i) for i in range(4)]
    stv = [sb("stv0", [128, 512]), stage[2]]; stv_b = [Buf("stv0"), stage_b[2]]

    HKB = 52
    regH = sb("regH", [128, HKB * 256])
    regH_bf = regH[:].bitcast(BF16)
    identf = regH[:, 0:128]
    o = [0]

    def carve(nelem_f32):
        a = o[0]; o[0] += nelem_f32
        assert o[0] <= HKB * 256
        return a

    sq = []; th = []
    for i in range(4):
        a = carve(512); sq.append(regH[:, a:a + 512])
    for i in range(4):
        a = carve(512); th.append(regH[:, a:a + 512])
    gs = []
    for i in range(4):
        a = carve(256); gs.append(regH_bf[:, 2 * a:2 * a + 512])
    a = carve(512); lf = regH[:, a:a + 512]
    oa_sb = regH[:, a:a + 2048].rearrange("p (h t) -> p h t", t=512)
    a = carve(512); bb = regH[:, a:a + 512]
    a = carve(512); eb = regH[:, a:a + 512]
    a = carve(512); enb = regH[:, a:a + 512]
    qdT = []; kdT = []; kdtok = []
    for i in range(4):
        a = carve(256); qdT.append(regH_bf[:, 2 * a:2 * a + 512])
    for i in range(4):
        a = carve(256); kdT.append(regH_bf[:, 2 * a:2 * a + 512])
    for i in range(4):
        a = carve(256); kdtok.append(regH_bf[:, 2 * a:2 * a + 512].rearrange("p (i k) -> p i k", k=128))
    a = carve(32); ebl = regH[:, a:a + 32].rearrange("p (h j) -> p h j", j=8)
    a = carve(128); AmA = regH_bf[:, 2 * a:2 * a + 256]
    a = carve(256); osq = regH_bf[:, 2 * a:2 * a + 512]

    a = carve(512); lnv = regH[:, a:a + 512]
    a = carve(512); rsv = regH[:, a:a + 512]
    a = carve(512); t1 = regH[:, a:a + 512]
    a = carve(512); StA = regH[:, a:a + 512]
    a = carve(256); rotw = regH[:, a:a + 256].rearrange("p (a h c) -> p a h c", a=4, h=8)
    hg_end = o[0]
    rot_b = Buf("rot"); rot2_b = Buf("rot2")
    sq_b = [Buf("sq%d" % i) for i in range(4)]; th_b = [Buf("th%d" % i) for i in range(4)]
    gs_b = [Buf("gs%d" % i) for i in range(4)]
    lf_b = Buf("lf"); bb_b = Buf("bb"); eb_b = Buf("eb"); enb_b = Buf("enb"); oa_b = Buf("oa")
    qdT_b = [Buf("qdT%d" % i) for i in range(4)]; kdT_b = [Buf("kdT%d" % i) for i in range(4)]
    kdtok_b = [Buf("kdtok%d" % i) for i in range(4)]; ebl_b = [Buf("ebl%d" % i) for i in range(4)]
    AmA_b = Buf("AmA"); osq_b = Buf("osq"); lnv_b = Buf("lnv"); rsv_b = Buf("rsv")
    t1_b = Buf("t1"); StA_b = Buf("StA")
    regH_A = sq_b + th_b + gs_b + [lf_b, bb_b, eb_b, enb_b] + qdT_b + kdT_b + kdtok_b + ebl_b + [AmA_b] + \
        [osq_b, lnv_b, rsv_b, t1_b, StA_b, oa_b, rot_b, rot2_b]
    o[0] = 0
    uext = []; ccv = []; scv = []
    for i in range(2):
        a = carve(520); uext.append(regH[:, a:a + 520])
    for i in range(2):
        a = carve(512); ccv.append(regH[:, a:a + 512])
    for i in range(2):
        a = carve(512); scv.append(regH[:, a:a + 512])
    pst = []
    for i in range(2):
        a = carve(256); pst.append(regH[:, a:a + 256])
    a = carve(512); pb = regH_bf[:, 2 * a:2 * a + 1024].rearrange("p (k c) -> p k c", c=256)
    a = carve(512); pT = regH_bf[:, 2 * a:2 * a + 1024].rearrange("p (c t) -> p c t", t=T)
    tg = []; t2 = []
    for i in range(2):
        a = carve(512); tg.append(regH[:, a:a + 512])
    for i in range(2):
        a = carve(512); t2.append(regH[:, a:a + 512])
    a = carve(1024); wpp_sb = regH_bf[:, 2 * a:2 * a + 2048].rearrange("p (c n) -> p c n", n=512)
    wpp_b = Buf("wpp")
    uext_b = [Buf("uext%d" % i) for i in range(2)]; ccv_b = [Buf("cc%d" % i) for i in range(2)]
    scv_b = [Buf("sc%d" % i) for i in range(2)]; pst_b = [Buf("pst%d" % i) for i in range(2)]
    pb_b = Buf("pb"); pT_b = Buf("pT"); tg_b = [Buf("tg%d" % i) for i in range(2)]; t2_b = [Buf("t2%d" % i) for i in range(2)]
    regH_B = uext_b + ccv_b + scv_b + pst_b + [pb_b, pT_b, wpp_b] + tg_b + t2_b

    Pexp = [sb("Pexp%d" % i, [128, 2, 512], BF16) for i in range(3)]; Pexp_b = [Buf("Pexp%d" % i) for i in range(3)]
    Pm = [sb("Pm%d" % i, [128, 2, 512], BF16) for i in range(3)]; Pm_b = [Buf("Pm%d" % i) for i in range(3)]
    junk = Pexp[0][:, :, :].rearrange("p e n -> p (e n)"); junk_b = Pexp_b[0]
    xs = [Pexp[1 + i][:, :, :].rearrange("p e n -> p (e n)") for i in range(2)]; xs_b = [Pexp_b[1 + i] for i in range(2)]
    lnL = lnv; rL = rsv; lnL_b = lnv_b; rL_b = rsv_b
    KTn = sb("KTn", [128, 4, 4], BF16); KTn_b = Buf("KTn")
    Vn = sb("Vn", [4, 512], BF16); Vn_b = Buf("Vn")
    sacc = sb("sacc", [128, 32]); sacc_b = Buf("sacc")
    cbf = [sb("cbf%d" % i, [128, 512], BF16) for i in range(2)]; cbf_b = [Buf("cbf%d" % i) for i in range(2)]

    slab = [sb("slab%d" % i, [128, 8, 512], BF16) for i in range(NSLAB)]
    slab_b = [Buf("slab%d" % i) for i in range(NSLAB)]
    slab_i = [0]

    def bank_bf(i):
        return banks[i].bitcast(BF16)

    def group_sched():
        L = [("win", b) for b in (1, 0, 3, 2, 4, 5, 6)]
        L += [("wo", 0), ("wo", 1)]
        for g0 in range(0, NFF, 4):
            L += [("gate", g0), ("up", g0)]
        for half in range(2):
            for g0 in range(0, NFF, 8):
                L.append(("down", half, g0))
        L += [("pg", 0), ("pg", 1)]
        return L

    sched = []
    for _ in range(NSEQ_P * NST + 1):
        sched += group_sched()
    ws = {"issued": 0, "consumed": 0}

    def w_parts(desc, sl):
        k = desc[0]
        col = lambda w, c0_, nc_: w[:, c0_:c0_ + nc_].rearrange("(kc p) n -> p kc n", p=128)
        if k == "win":
            return [(sl[:, 0:8, 0:512], col(w_in, desc[1] * 512, 512))]
        if k == "wo":
            return [(sl[:, 0:8, 0:512], col(w_o, desc[1] * 512, 512))]
        if k in ("gate", "up"):
            ncc = min(4, NFF - desc[1])
            return [(sl[:, 0:8, 0:ncc * 128], col(w_gate if k == "gate" else w_up, desc[1] * 128, ncc * 128))]
        if k == "down":
            half, g0 = desc[1], desc[2]
            ncc = min(8, NFF - g0)
            return [(sl[:, 0:ncc, :], w_down[g0 * 128:(g0 + ncc) * 128, half * 512:(half + 1) * 512].rearrange("(cc p) n -> p cc n", p=128))]
        if k == "pp":
            return [(sl[:, 0:2, :], col(w_pp, 0, 512)), (sl[:, 2:4, :], col(w_pp, 512, 512))]
        if k == "pg":
            return [(sl[:, 0:8, 0:512], col(w_pg, desc[1] * 512, 512))]
        raise ValueError(desc)

    def get_w(desc, oldest=None):
        j = ws["consumed"]
        assert sched[j] == desc, (sched[j], desc)
        if oldest is None:
            oldest = j
        while ws["issued"] < min(len(sched), oldest + NSLAB):
            i = ws["issued"]
            for dst, src in w_parts(sched[i], slab[i % NSLAB]):
                P.dma("pool", dst, src, writes=[slab_b[i % NSLAB]])
            ws["issued"] += 1
        assert ws["issued"] > j
        ws["consumed"] += 1
        ws["last"] = j
        return slab[j % NSLAB], slab_b[j % NSLAB]

    def w_prefetch():
        j = ws["consumed"]
        while ws["issued"] < min(len(sched), j + NSLAB):
            i = ws["issued"]
            for dst, src in w_parts(sched[i], slab[i % NSLAB]):
                P.dma("pool", dst, src, writes=[slab_b[i % NSLAB]])
            ws["issued"] += 1

    def mm(out, lhsT, rhs, start, stop):
        return lambda e: e.matmul(out, lhsT=lhsT, rhs=rhs, start=start, stop=stop)

    for dst, src in ((csp, csp_d), (gcols, gcols_d), (gfin, gfin_d), (lbl, lbl_d), (gon, gon_d),
                     (convw, convw_d), (convb, convb_d)):
        P.dma("sp", dst[:], src, writes=[const_b])
    P.dma("sp", css[:], css_d, writes=[const_b])
    P.dma("pool", maskr[:], maskr_d, writes=[const_b])
    P.dma("pool", causal[:], causal_d, writes=[const_b])
    P.op("pool", lambda e: e.memset(identf[:], 1.0), writes=[const_b])
    P.op("pool", lambda e: e.affine_select(out=identf[:], in_=identf[:], pattern=[[-1, 128]],
                                           compare_op=ALU.is_equal, fill=0.0, base=0, channel_multiplier=1),
         writes=[const_b])
    P.op("dve", lambda e: e.tensor_copy(out=ident[:], in_=identf[:]), reads=[const_b], writes=[const_b])
    P.op("pool", lambda e: e.memset(ones[:], 1.0), writes=[const_b])
    P.op("pool", lambda e: e.memset(neghalf[:], -0.5), writes=[const_b])
    P.op("pool", lambda e: e.memset(rmask_p[:], 1.0), writes=[const_b])
    P.op("pool", lambda e: e.memset(rmask_p[:].rearrange("p (c t) -> p c t", t=64)[:, :, 0:1], 0.0), writes=[const_b])
    P.op("pool", lambda e: e.memset(rmask_s[:], 1.0), writes=[const_b])
    P.op("pool", lambda e: e.memset(rmask_s[:].rearrange("p (c t) -> p c t", t=4)[:, :, 0:1], 0.0), writes=[const_b])
    P.op("dve", lambda e: e.tensor_tensor(out=lbt[:], in0=lbl[:, 1, :], in1=lbl[:, 0, :], op=ALU.subtract),
         reads=[const_b], writes=[const_b])
    P.op("act", lambda e: e.activation(out=lbt[:], in_=lbt[:], func=AF.Exp), reads=[const_b], writes=[const_b])
    P.op("dve", lambda e: e.tensor_scalar(out=lbt[:], in0=lbt[:], scalar1=1.0, scalar2=None, op0=ALU.add),
         reads=[const_b], writes=[const_b])
    P.op("dve", lambda e: e.reciprocal(out=lbt[:], in_=lbt[:]), reads=[const_b], writes=[const_b])
    P.op("dve", lambda e: e.tensor_scalar(out=c1[:], in0=lbt[:], scalar1=-0.5, scalar2=0.5, op0=ALU.mult, op1=ALU.add),
         reads=[const_b], writes=[const_b])
    P.op("dve", lambda e: e.tensor_scalar(out=c0[:], in0=lbt[:], scalar1=0.5, scalar2=0.5, op0=ALU.mult, op1=ALU.add),
         reads=[const_b], writes=[const_b])
    P.op("dve", lambda e: e.tensor_scalar(out=nc1[:], in0=c1[:], scalar1=-1.0, scalar2=None, op0=ALU.mult),
         reads=[const_b], writes=[const_b])

    P.handoff([const_b], regH_A + regH_B)
    ss_n = [0]

    def norm_A(k, ntok):
        si = ss_n[0] % 16
        ss_n[0] += 1
        P.op("act", lambda e: e.activation(out=junk[:ntok, :], in_=h[:ntok, k, :], func=AF.Square,
                                           accum_out=ss[:ntok, si:si + 1]),
             reads=[h_b[k]], writes=[junk_b, ss_b[si]])
        P.op("dve", lambda e: e.tensor_scalar(out=ms[:ntok, si:si + 1], in0=ss[:ntok, si:si + 1], scalar1=1.0 / D,
                                              scalar2=EPS, op0=ALU.mult, op1=ALU.add),
             reads=[ss_b[si]], writes=[ss_b[si]])
        P.op("pool", lambda e: e.tensor_tensor(out=rstd[:ntok, si:si + 1], in0=ms[:ntok, si:si + 1],
                                               in1=neghalf[:ntok, 0:1], op=ALU.pow),
             reads=[ss_b[si], const_b], writes=[ss_b[si]])
        return si

    def norm_B(k, ntok, si):
        xi = k % 2
        P.op("dve", lambda e: e.tensor_scalar(out=xs[xi][:ntok, :], in0=h[:ntok, k, :], scalar1=rstd[:ntok, si:si + 1],
                                              scalar2=None, op0=ALU.mult),
             reads=[h_b[k], ss_b[si]], writes=[xs_b[xi]])

    def norm_C(k, ntok):
        xi = k % 2
        bk = 6 + (k % 2)
        bv = bank_bf(bk)
        P.pe_group([(lambda e, c=c: e.transpose(out=bv[:, c * 128:c * 128 + ntok], in_=xs[xi][:ntok, c * 128:(c + 1) * 128],
                                                identity=ident[:ntok, :ntok])) for c in range(8)],
                   reads=[xs_b[xi], const_b], writes=[bank_b[bk]])

    def norm_D(k, col0, ntok, gi):
        bk = 6 + (k % 2)
        bv = bank_bf(bk)
        P.op("dve", lambda e: e.tensor_tensor(
            out=aT[:, :, col0:col0 + ntok],
            in0=bv.rearrange("p (c t) -> p c t", t=128)[:, :, 0:ntok],
            in1=gcols[:, gi, :].unsqueeze(2).to_broadcast([128, 8, ntok]), op=ALU.mult),
            reads=[bank_b[bk], const_b], writes=[aT_b])

    def norm_T(G, gi):
        dt_ = G["dt"]
        n_ = len(dt_)
        sis = {}
        order = []
        for k in range(min(2, n_)):
            order += [("A", k), ("B", k)]
        for k in range(n_):
            order.append(("C", k))
            if k + 2 < n_:
                order.append(("A", k + 2))
            order.append(("D", k))
            if k + 2 < n_:
                order.append(("B", k + 2))
        for what, k in order:
            col0, ntok = dt_[k]
            if what == "A":
                sis[k] = norm_A(k, ntok)
            elif what == "B":
                norm_B(k, ntok, sis[k])
            elif what == "C":
                norm_C(k, ntok)
            else:
                norm_D(k, col0, ntok, gi)

    def phase_wo_norm1(G):
        sl0, slb0 = get_w(("wo", 0))
        sl1, slb1 = get_w(("wo", 1), oldest=ws["last"])
        sls = [(sl0, slb0), (sl1, slb1)]
        dt_ = G["dt"]
        sis = []
        for k, (col0, ntok) in enumerate(dt_):
            for half in range(2):
                sl, slb = sls[half]
                bk = (2 * k + half) % 4
                P.pe_group([mm(banks[bk][:ntok, :], mixT[:, kc, col0:col0 + ntok], sl[:, kc, :], kc == 0, kc == 7)
                            for kc in range(8)], reads=[slb, mixT_b], writes=[bank_b[bk]])
            if k == len(dt_) - 1:
                w_prefetch()
            if k >= 1:
                norm_B(k - 1, dt_[k - 1][1], sis[k - 1])
                norm_C(k - 1, dt_[k - 1][1])
            for half in range(2):
                bk = (2 * k + half) % 4
                hv = h[:ntok, k, half * 512:(half + 1) * 512]
                P.op("dve", lambda e: e.tensor_tensor(out=hv, in0=banks[bk][:ntok, :], in1=hv, op=ALU.add),
                     reads=[bank_b[bk], h_b[k]], writes=[h_b[k]])
            sis.append(norm_A(k, ntok))
            if k >= 1:
                norm_D(k - 1, dt_[k - 1][0], dt_[k - 1][1], 1)
        k = len(dt_) - 1
        norm_B(k, dt_[k][1], sis[k])
        norm_C(k, dt_[k][1])
        norm_D(k, dt_[k][0], dt_[k][1], 1)

    def rotary(G, st_ap, cs_ap, ntok, bufs, dst_ap=None, dst_bufs=None):
        if dst_ap is None:
            dst_ap = st_ap; dst_bufs = bufs
        v = st_ap.rearrange("p (h c) -> p h c", c=64)
        dv = dst_ap.rearrange("p (h c) -> p h c", c=64)
        u = v[:ntok, :, 0:16].rearrange("p h (a c) -> p h a c", c=8)
        cosb = cs_ap[:ntok, 0:8].unsqueeze(1).unsqueeze(1).to_broadcast([ntok, 8, 2, 8])
        sinb = cs_ap[:ntok, 8:16].unsqueeze(1).unsqueeze(1).to_broadcast([ntok, 8, 2, 8])
        U1 = rotw[:ntok, 0:2, :, :].rearrange("p a h c -> p h a c")
        W = rotw[:ntok, 2:4, :, :].rearrange("p a h c -> p h a c")
        P.op("dve", lambda e: e.tensor_tensor(out=U1, in0=u, in1=cosb, op=ALU.mult), reads=bufs + [const_b], writes=[rot_b])
        P.op("dve", lambda e: e.tensor_tensor(out=W, in0=u, in1=sinb, op=ALU.mult), reads=bufs + [const_b], writes=[rot2_b])
        P.op("dve", lambda e: e.tensor_tensor(out=dv[:ntok, :, 0:8], in0=rotw[:ntok, 0, :, :], in1=rotw[:ntok, 3, :, :], op=ALU.subtract),
             reads=[rot_b, rot2_b], writes=dst_bufs)
        P.op("dve", lambda e: e.tensor_tensor(out=dv[:ntok, :, 8:16], in0=rotw[:ntok, 1, :, :], in1=rotw[:ntok, 2, :, :], op=ALU.add),
             reads=[rot_b, rot2_b], writes=dst_bufs)


    def phase_win(G, sample, seq, st):
        n = G["n"]
        for blk, kind in ((1, "f"), (0, "q"), (3, "g")):
            sl, slb = get_w(("win", blk))
            for hh in range(4):
                bk = hh
                P.pe_group([mm(banks[bk][:, 0:n], sl[:, kc, hh * 128:(hh + 1) * 128], aT[:, kc, 0:n], kc == 0, kc == 7)
                            for kc in range(8)], reads=[slb, aT_b], writes=[bank_b[bk]])
                if kind == "f":
                    P.op("act", lambda e: e.activation(out=th[hh][:, 0:n], in_=banks[bk][:, 0:n], func=AF.Tanh, scale=0.5),
                         reads=[bank_b[bk]], writes=[th_b[hh]])
                elif kind == "q":
                    P.op("act", lambda e: e.activation(out=sq[hh][:, 0:n], in_=banks[bk][:, 0:n], func=AF.Silu),
                         reads=[bank_b[bk]], writes=[sq_b[hh]])
                else:
                    P.op("act", lambda e: e.activation(out=gs[hh][:, 0:n], in_=banks[bk][:, 0:n], func=AF.Silu),
                         reads=[bank_b[bk]], writes=[gs_b[hh]])
        for blk, kind in ((2, "i"), (4, "q"), (5, "k"), (6, "v")):
            sl, slb = get_w(("win", blk))
            posts = []
            for k, (col0, ntok) in enumerate(G["stl"]):
                bk = k % 4
                gt = st * NT + k
                P.pe_group([mm(banks[bk][:ntok, :], aT[:, kc, col0:col0 + ntok], sl[:, kc, :], kc == 0, kc == 7)
                            for kc in range(8)], reads=[slb, aT_b], writes=[bank_b[bk]])
                if len(posts) >= 2:
                    posts.pop(0)()
                if kind == "i":
                    P.op("act", lambda e: e.activation(out=ia_tok[:ntok, k, :], in_=banks[bk][:ntok, :], func=AF.Copy),
                         reads=[bank_b[bk]], writes=[ia_b])
                    continue
                d2 = k % 2
                if kind == "v":
                    P.op("act", lambda e: e.activation(out=VH[:, gt % 8, :], in_=banks[bk][:, :], func=AF.Copy),
                         reads=[bank_b[bk]], writes=[VH_b])
                    P.op("dve", lambda e: e.tensor_copy(out=stv[d2][:, :], in_=banks[bk][:, :]), reads=[bank_b[bk]], writes=[stv_b[d2]])
                    P.dma("sp", cv_p[seq, gt * 128:(gt + 1) * 128, :], stv[d2][:, :], reads=[stv_b[d2]])
                    continue
                cs_ap = csp[:, gt, :]
                if kind == "q":
                    qb_ = qkb[d2]; qbb = qkb_b[d2]
                    P.op("act", lambda e: e.activation(out=qb_[:, :], in_=banks[bk][:, :], func=AF.Copy), reads=[bank_b[bk]], writes=[qbb])
                    rotary(G, banks[bk][:, :], cs_ap, 128, [bank_b[bk]], dst_ap=qb_[:, :], dst_bufs=[qbb])
                else:
                    qb_ = qkb[2 + d2]; qbb = qkb_b[2 + d2]
                    sk = stage[d2]; skb = stage_b[d2]
                    P.op("act", lambda e: e.activation(out=sk[:, :], in_=banks[bk][:, :], func=AF.Copy), reads=[bank_b[bk]], writes=[skb])
                    rotary(G, banks[bk][:, :], cs_ap, 128, [bank_b[bk]], dst_ap=sk[:, :], dst_bufs=[skb])
                    P.op("dve", lambda e: e.tensor_copy(out=qb_[:, :], in_=sk[:, :]), reads=[skb], writes=[qbb])
                    P.dma("sp", ck_p[seq, gt * 128:(gt + 1) * 128, :], sk[:, :], reads=[skb])

                def post(kind=kind, k=k, gt=gt, col0=col0, qb_=qb_, qbb=qbb):
                    tb_ = 6 + (k % 2)
                    bv = bank_bf(tb_)
                    P.pe_group([(lambda e, c=c: e.transpose(out=bv[:, c * 128:(c + 1) * 128], in_=qb_[:, c * 128:(c + 1) * 128],
                                                            identity=ident[:, :])) for c in range(4)],
                               reads=[qbb, const_b], writes=[bank_b[tb_]])
                    src = bv[:, 0:512].rearrange("p (c t) -> p c t", t=128)
                    if kind == "q":
                        P.op("act", lambda e: e.activation(out=QT[:, :, col0:col0 + 128], in_=src, func=AF.Copy, scale=0.125),
                             reads=[bank_b[tb_]], writes=[QT_b])
                    else:
                        P.op("act", lambda e: e.activation(out=KT[:, :, gt * 128:(gt + 1) * 128], in_=src, func=AF.Copy),
                             reads=[bank_b[tb_]], writes=[KT_b])
                posts.append(post)
            for p_ in posts:
                p_()
            if kind == "v":
                for rho in range(4):
                    for t_ in range(4):
                        P.dma("sp", Vp[32 * t_:32 * (t_ + 1), st % 4, rho, :], VH[rho:128:4, (4 * st + t_) % 8, :],
                              reads=[VH_b], writes=[Vp_b[st % 4]])
            hgrn_g1(G, {"i": 0, "q": 1, "k": 2, "v": 3}[kind])

    def hgrn_gates(G, sample):
        n = G["n"]
        rm = rmask_s if sample else rmask_p
        C = 4 if sample else 64
        nch = n // C
        for hh in range(4):
            P.op("act", lambda e: e.activation(out=lf[:, 0:n], in_=th[hh][:, 0:n], func=AF.Ln,
                                               scale=c1[:, hh:hh + 1], bias=c0[:, hh:hh + 1]),
                 reads=[th_b[hh], const_b], writes=[lf_b])
            P.op("dve", lambda e: e.tensor_tensor_scan(out=bb[:, 0:n], data0=rm[:, 0:n], data1=lf[:, 0:n], initial=0.0,
                                                       op0=ALU.mult, op1=ALU.add),
                 reads=[lf_b, const_b], writes=[bb_b])
            P.op("act", lambda e: e.activation(out=eb[:, 0:n], in_=bb[:, 0:n], func=AF.Exp), reads=[bb_b], writes=[eb_b])
            P.op("act", lambda e: e.activation(out=enb[:, 0:n], in_=bb[:, 0:n], func=AF.Exp, scale=-1.0),
                 reads=[bb_b], writes=[enb_b])
            P.op("dve", lambda e: e.scalar_tensor_tensor(out=qdT[hh][:, 0:n], in0=sq[hh][:, 0:n], scalar=128.0 ** -0.5,
                                                         in1=eb[:, 0:n], op0=ALU.mult, op1=ALU.mult),
                 reads=[sq_b[hh], eb_b], writes=[qdT_b[hh]])
            P.op("dve", lambda e: e.tensor_copy(out=ebl[:, hh, 0:nch],
                                                in_=eb[:, 0:n].rearrange("p (c t) -> p c t", t=C)[:, :, C - 1]),
                 reads=[eb_b], writes=[ebl_b[hh]])
            P.op("dve", lambda e: e.tensor_scalar(out=th[hh][:, 0:n], in0=th[hh][:, 0:n], scalar1=nc1[:, hh:hh + 1],
                                                  scalar2=c1[:, hh:hh + 1], op0=ALU.mult, op1=ALU.add),
                 reads=[th_b[hh], const_b], writes=[th_b[hh]])
            P.op("dve", lambda e: e.tensor_tensor(out=kdT[hh][:, 0:n], in0=th[hh][:, 0:n], in1=enb[:, 0:n], op=ALU.mult),
                 reads=[th_b[hh], enb_b], writes=[kdT_b[hh]])
            tb_ = 6 + (hh % 2)
            bv = bank_bf(tb_)
            fns = []
            for k, (col0, ntok) in enumerate(G["stl"]):
                fns.append(lambda e, k=k, col0=col0, ntok=ntok: e.transpose(
                    out=bv[:ntok, k * 128:(k + 1) * 128], in_=kdT[hh][:, col0:col0 + ntok], identity=ident[:, :]))
            P.pe_group(fns, reads=[kdT_b[hh], const_b], writes=[bank_b[tb_]])
            nst = len(G["stl"])
            ntok0 = G["stl"][0][1]
            P.op("act", lambda e: e.activation(out=kdtok[hh][:ntok0, 0:nst, :],
                                               in_=bv[:ntok0, 0:nst * 128].rearrange("p (i k) -> p i k", k=128), func=AF.Copy),
                 reads=[bank_b[tb_]], writes=[kdtok_b[hh]])

    def hgrn_chunks(G, sample, seq_of_chunk=None):
        n = G["n"]
        C = 4 if sample else 64
        nch = n // C
        S_all = S[:, :, :].rearrange("p h v -> p (h v)")
        Sbf_all = S_bf[:, :, :].rearrange("p h v -> p (h v)")

        def base_of(j):
            return 0 if sample else (j % 2) * 64

        def emit_A(j):
            cols = slice(j * C, (j + 1) * C)
            base = base_of(j)
            P.pe_group([mm(banks[4][base:base + C, hh * 64:hh * 64 + C], kdT[hh][:, cols], qdT[hh][:, cols], True, True)
                        for hh in range(4)], reads=kdT_b + qdT_b, writes=[bank_b[4]])
            P.op("dve", lambda e: e.tensor_tensor(
                out=AmA[base:base + C, :].rearrange("p (h t) -> p h t", t=64)[:, :, 0:C],
                in0=banks[4][base:base + C, 0:256].rearrange("p (h t) -> p h t", t=64)[:, :, 0:C],
                in1=causal[base:base + C, 0:C].unsqueeze(1).to_broadcast([C, 4, C]), op=ALU.mult),
                reads=[bank_b[4], const_b], writes=[AmA_b])

        if not sample:
            emit_A(0)
        for j in range(nch):
            cols = slice(j * C, (j + 1) * C)
            base = base_of(j)
            if sample:
                k = j
                P.dma("sp", S[:, :, :], st_in[j].rearrange("h k v -> k h v"), writes=S_b)
                P.op("act", lambda e: e.activation(out=Sbf_all, in_=S_all, func=AF.Copy), reads=S_b, writes=Sbf_b)
                emit_A(j)
            else:
                k = j // 2
            db = 6 + (j % 2)
            fns = []
            for hh in range(4):
                vtok = ia_tok[base:base + C, k, hh * 128:(hh + 1) * 128]
                fns.append(mm(banks[hh][:, cols], S_bf[:, hh, :], qdT[hh][:, cols], True, False))
                fns.append(mm(banks[hh][:, cols], vtok, AmA[base:base + C, hh * 64:hh * 64 + C], False, True))
                fns.append(mm(banks[db][:, hh * 128:(hh + 1) * 128], kdtok[hh][base:base + C, k, :], vtok, True, True))
            P.pe_group(fns, reads=Sbf_b + qdT_b + kdtok_b + [ia_b, AmA_b], writes=[bank_b[0], bank_b[1], bank_b[2], bank_b[3], bank_b[db]])
            if (not sample) and j + 1 < nch:
                emit_A(j + 1)
            P.op("dve", lambda e: e.tensor_tensor(out=StA[:, :], in0=banks[db][:, :], in1=S_all, op=ALU.add),
                 reads=[bank_b[db]] + S_b, writes=[StA_b])
            P.op("dve", lambda e: e.tensor_tensor(out=S[:, :, :], in0=StA[:, :].rearrange("p (h v) -> p h v", v=128),
                                                  in1=ebl[:, :, j:j + 1].to_broadcast([128, 4, 128]), op=ALU.mult),
                 reads=[StA_b] + ebl_b, writes=S_b)
            P.op("act", lambda e: e.activation(out=Sbf_all, in_=S_all, func=AF.Copy), reads=S_b, writes=Sbf_b)
            if sample:
                P.dma("sp", st_s[j].rearrange("h k v -> k h v"), S[:, :, :], reads=S_b)

    def hgrn_out(G):
        n = G["n"]
        for hh in range(4):
            P.op("act", lambda e: e.activation(out=osq[:, 0:n], in_=banks[hh][:, 0:n], func=AF.Square),
                 reads=[bank_b[hh]], writes=[osq_b])
            nb = 4 + (hh % 2)
            P.pe_group([mm(banks[nb][:, 0:n], ones[:, :], osq[:, 0:n], True, True)], reads=[osq_b, const_b], writes=[bank_b[nb]])
            P.op("act", lambda e: e.activation(out=lnv[:, 0:n], in_=banks[nb][:, 0:n], func=AF.Ln, scale=1.0 / 128, bias=EPS),
                 reads=[bank_b[nb]], writes=[lnv_b])
            P.op("act", lambda e: e.activation(out=rsv[:, 0:n], in_=lnv[:, 0:n], func=AF.Exp, scale=-0.5),
                 reads=[lnv_b], writes=[rsv_b])
            P.op("dve", lambda e: e.tensor_tensor(out=t1[:, 0:n], in0=banks[hh][:, 0:n], in1=rsv[:, 0:n], op=ALU.mult),
                 reads=[bank_b[hh], rsv_b], writes=[t1_b])
            P.op("dve", lambda e: e.scalar_tensor_tensor(out=mixT[:, hh, 0:n], in0=t1[:, 0:n], scalar=gon[:, hh:hh + 1],
                                                         in1=gs[hh][:, 0:n], op0=ALU.mult, op1=ALU.mult),
                 reads=[t1_b, gs_b[hh], const_b], writes=[mixT_b])

    att_i = [0]

    def attn_items(items, sbase=4, side=None, nslot=2):
        st_ = []

        def qk(it):
            i2 = att_i[0] % nslot
            att_i[0] += 1
            b0 = sbase + 2 * i2
            sc2 = psum_all[:, b0 * 512:(b0 + 2) * 512].rearrange("p (e n) -> p e n", n=512)
            nk, N = it["nk"], it["N"]
            P.pe_group([mm(sc2[:nk, e2, 0:N], it["kt_ap"][64 * e2:64 * e2 + 64, :], it["q_ap"][64 * e2:64 * e2 + 64, :], True, True)
                        for e2 in range(2)], reads=it["rd"] + [QT_b], writes=[bank_b[b0], bank_b[b0 + 1]])
            P.op("act", lambda e: e.activation(out=Pexp[i2][:nk, :, 0:N], in_=sc2[:nk, :, 0:N], func=AF.Exp),
                 reads=[bank_b[b0], bank_b[b0 + 1]], writes=[Pexp_b[i2]])
            nq = it["nq"]
            if it["m3"]:
                nj = N // nq
                P.op("dve", lambda e: e.tensor_tensor(out=Pm[i2][:nk, :, 0:N].rearrange("p e (b q) -> p e b q", q=nq),
                                                      in0=Pexp[i2][:nk, :, 0:N].rearrange("p e (b q) -> p e b q", q=nq),
                                                      in1=it["m_ap"].unsqueeze(1).to_broadcast([nk, 2, nj, nq]), op=ALU.mult),
                     reads=[Pexp_b[i2], const_b], writes=[Pm_b[i2]])
            else:
                P.op("dve", lambda e: e.tensor_tensor(out=Pm[i2][:nk, :, 0:N], in0=Pexp[i2][:nk, :, 0:N],
                                                      in1=it["m_ap"].unsqueeze(1).to_broadcast([nk, 2, N]), op=ALU.mult),
                     reads=[Pexp_b[i2], const_b], writes=[Pm_b[i2]])
            return i2

        def pv(it, i2):
            nk, N = it["nk"], it["N"]
            ob, lb_ = it["obank"], it["lbank"]
            oc = it["ocols"]
            f_, l_ = it["first"], it["last"]
            P.pe_group([mm(banks[ob][0:64, oc], it["v_aps"][0], Pm[i2][:nk, 0, 0:N], f_, l_),
                        mm(banks[ob][64:128, oc], it["v_aps"][1], Pm[i2][:nk, 1, 0:N], f_, l_),
                        mm(banks[lb_][0:64, oc], ones[:nk, 0:64], Pm[i2][:nk, 0, 0:N], f_, l_),
                        mm(banks[lb_][64:128, oc], ones[:nk, 0:64], Pm[i2][:nk, 1, 0:N], f_, l_)],
                       reads=it["rd"] + [Pm_b[i2], const_b], writes=[bank_b[ob], bank_b[lb_]])
            if it["fin"] is not None:
                it["fin"]()

        side = list(side or [])
        every = max(1, len(items) // (len(side) + 1)) if side else 0
        npv = [0]

        def pv_side(a, b):
            pv(a, b)
            npv[0] += 1
            if side and npv[0] % every == 0:
                side.pop(0)(sbase + 2 * b)

        for idx, it in enumerate(items):
            st_.append((it, qk(it)))
            if len(st_) > nslot - 1:
                a, b = st_.pop(0)
                pv_side(a, b)
        while st_:
            a, b = st_.pop(0)
            pv_side(a, b)
        while side:
            side.pop(0)(sbase)

    def attn_finish(pr, cols, ocols_mix, obank, lbank):
        n_ = cols.stop - cols.start
        P.op("act", lambda e: e.activation(out=lnL[:, 0:n_], in_=banks[lbank][:, cols], func=AF.Ln), reads=[bank_b[lbank]], writes=[lnL_b])
        P.op("act", lambda e: e.activation(out=rL[:, 0:n_], in_=lnL[:, 0:n_], func=AF.Exp, scale=-1.0), reads=[lnL_b], writes=[rL_b])
        P.op("dve", lambda e: e.tensor_tensor(out=mixT[:, 4 + pr, ocols_mix], in0=banks[obank][:, cols], in1=rL[:, 0:n_], op=ALU.mult),
             reads=[bank_b[obank], rL_b], writes=[mixT_b])

    def phase_attn_prompt(G, st, side):
        gt0 = st * NT
        items = []
        for pr in range(4):
            obank = 0; lbank = 1
            its = []
            kts = [gt0] + [kt for kt in range(max(0, gt0 - 4), gt0 + NT) if kt != gt0]
            for kt in kts:
                j_lo = max(0, kt - gt0); j_hi = NT - 1
                nj = j_hi - j_lo + 1
                off_lo = gt0 + j_lo - kt
                cols = slice(j_lo * 128, (j_hi + 1) * 128)
                its.append(dict(kt_ap=KT[:, pr, kt * 128:(kt + 1) * 128], q_ap=QT[:, pr, cols],
                                v_aps=[VH[:, kt % 8, (2 * pr) * 64:(2 * pr + 1) * 64], VH[:, kt % 8, (2 * pr + 1) * 64:(2 * pr + 2) * 64]],
                                m_ap=maskr[:, off_lo:off_lo + nj, :], m3=True, nq=128, nk=128, N=nj * 128,
                                obank=obank, lbank=lbank, ocols=cols, rd=[KT_b, VH_b], fin=None))
            for G_ in (st - 2, st - 3, st - 4):
                if G_ < 0:
                    continue
                for rho in range(4):
                    its.append(dict(kt_ap=KT[:, pr, 512 * G_:512 * (G_ + 1)].rearrange("p (i f) -> p f i", f=4)[:, rho, :],
                                    q_ap=QT[:, pr, 0:T].rearrange("p (i f) -> p f i", f=4)[:, rho, :],
                                    v_aps=[Vp[:, G_ % 4, rho, (2 * pr) * 64:(2 * pr + 1) * 64],
                                           Vp[:, G_ % 4, rho, (2 * pr + 1) * 64:(2 * pr + 2) * 64]],
                                    m_ap=maskr[:, 19 if G_ == st - 4 else 18, :], m3=False, nq=128, nk=128, N=128,
                                    obank=obank, lbank=lbank, ocols=slice(rho, T, 4), rd=[KT_b, Vp_b[G_ % 4]], fin=None))
            for ii, it in enumerate(its):
                it["first"] = (ii == 0)
                it["last"] = (ii == len(its) - 1)
            its[-1]["fin"] = (lambda pr=pr, obank=obank, lbank=lbank: attn_finish(pr, slice(0, T), slice(0, T), obank, lbank))
            items += its
        attn_items(items, sbase=2, side=side, nslot=3)

    def hgrn_g1(G, hh):
        n = G["n"]
        rm = rmask_p
        C = 64
        nch = n // C
        P.op("act", lambda e: e.activation(out=lf[:, 0:n], in_=th[hh][:, 0:n], func=AF.Ln,
                                           scale=c1[:, hh:hh + 1], bias=c0[:, hh:hh + 1]),
             reads=[th_b[hh], const_b], writes=[lf_b])
        P.op("dve", lambda e: e.tensor_tensor_scan(out=bb[:, 0:n], data0=rm[:, 0:n], data1=lf[:, 0:n], initial=0.0,
                                                   op0=ALU.mult, op1=ALU.add),
             reads=[lf_b, const_b], writes=[bb_b])
        P.op("act", lambda e: e.activation(out=eb[:, 0:n], in_=bb[:, 0:n], func=AF.Exp), reads=[bb_b], writes=[eb_b])
        P.op("act", lambda e: e.activation(out=enb[:, 0:n], in_=bb[:, 0:n], func=AF.Exp, scale=-1.0),
             reads=[bb_b], writes=[enb_b])
        P.op("dve", lambda e: e.scalar_tensor_tensor(out=qdT[hh][:, 0:n], in0=sq[hh][:, 0:n], scalar=128.0 ** -0.5,
                                                     in1=eb[:, 0:n], op0=ALU.mult, op1=ALU.mult),
             reads=[sq_b[hh], eb_b], writes=[qdT_b[hh]])
        P.op("dve", lambda e: e.tensor_copy(out=ebl[:, hh, 0:nch],
                                            in_=eb[:, 0:n].rearrange("p (c t) -> p c t", t=C)[:, :, C - 1]),
             reads=[eb_b], writes=[ebl_b[hh]])
        P.op("dve", lambda e: e.tensor_scalar(out=th[hh][:, 0:n], in0=th[hh][:, 0:n], scalar1=nc1[:, hh:hh + 1],
                                              scalar2=c1[:, hh:hh + 1], op0=ALU.mult, op1=ALU.add),
             reads=[th_b[hh], const_b], writes=[th_b[hh]])
        P.op("dve", lambda e: e.tensor_tensor(out=kdT[hh][:, 0:n], in0=th[hh][:, 0:n], in1=enb[:, 0:n], op=ALU.mult),
             reads=[th_b[hh], enb_b], writes=[kdT_b[hh]])

    def hgrn_g2(G):
        for hh in range(4):
            tb_ = 6 + (hh % 2)
            bv = bank_bf(tb_)
            P.pe_group([(lambda e, k=k, col0=col0: e.transpose(out=bv[:, k * 128:(k + 1) * 128], in_=kdT[hh][:, col0:col0 + 128],
                                                               identity=ident[:, :])) for k, (col0, ntok) in enumerate(G["stl"])],
                       reads=[kdT_b[hh], const_b], writes=[bank_b[tb_]])
            P.op("act", lambda e: e.activation(out=kdtok[hh][:, 0:NT, :],
                                               in_=bv[:, 0:NT * 128].rearrange("p (i k) -> p i k", k=128), func=AF.Copy),
                 reads=[bank_b[tb_]], writes=[kdtok_b[hh]])

    def hgrn_side_steps(G):
        n = G["n"]
        C = 64
        nch = n // C
        S_all = S[:, :, :].rearrange("p h v -> p (h v)")
        Sbf_all = S_bf[:, :, :].rearrange("p h v -> p (h v)")

        def emit_A(j, ba):
            cols = slice(j * C, (j + 1) * C)
            base = (j % 2) * 64
            P.pe_group([mm(banks[ba][base:base + C, 256 + hh * 64:256 + hh * 64 + C], kdT[hh][:, cols], qdT[hh][:, cols], True, True)
                        for hh in range(4)], reads=kdT_b + qdT_b, writes=[bank_b[ba]])
            P.op("dve", lambda e: e.tensor_tensor(
                out=AmA[base:base + C, :].rearrange("p (h t) -> p h t", t=64),
                in0=banks[ba][base:base + C, 256:512].rearrange("p (h t) -> p h t", t=64),
                in1=causal[base:base + C, 0:C].unsqueeze(1).to_broadcast([C, 4, C]), op=ALU.mult),
                reads=[bank_b[ba], const_b], writes=[AmA_b])

        def chunk(j, ba):
            if j == 0:
                emit_A(0, ba)
            cols = slice(j * C, (j + 1) * C)
            k = j // 2
            base = (j % 2) * 64
            oc = 0
            fns = []
            for hh in range(4):
                vtok = ia_tok[base:base + C, k, hh * 128:(hh + 1) * 128]
                fns.append(mm(banks[ba][:, oc + hh * 64:oc + hh * 64 + C], S_bf[:, hh, :], qdT[hh][:, cols], True, False))
                fns.append(mm(banks[ba][:, oc + hh * 64:oc + hh * 64 + C], vtok, AmA[base:base + C, hh * 64:hh * 64 + C], False, True))
                fns.append(mm(banks[ba + 1][:, hh * 128:(hh + 1) * 128], kdtok[hh][base:base + C, k, :], vtok, True, True))
            P.pe_group(fns, reads=Sbf_b + qdT_b + kdtok_b + [ia_b, AmA_b], writes=[bank_b[ba], bank_b[ba + 1]])
            if j + 1 < nch:
                emit_A(j + 1, ba)
            P.op("act", lambda e: e.activation(out=oa_sb[:, :, cols],
                                               in_=banks[ba][:, oc:oc + 256].rearrange("p (h t) -> p h t", t=64), func=AF.Copy),
                 reads=[bank_b[ba]], writes=[oa_b])
            P.op("dve", lambda e: e.tensor_tensor(out=StA[:, :], in0=banks[ba + 1][:, :], in1=S_all, op=ALU.add),
                 reads=[bank_b[ba + 1]] + S_b, writes=[StA_b])
            P.op("dve", lambda e: e.tensor_tensor(out=S[:, :, :], in0=StA[:, :].rearrange("p (h v) -> p h v", v=128),
                                                  in1=ebl[:, :, j:j + 1].to_broadcast([128, 4, 128]), op=ALU.mult),
                 reads=[StA_b] + ebl_b, writes=S_b)
            P.op("dve", lambda e: e.tensor_tensor(out=S_bf[:, :, :], in0=StA[:, :].rearrange("p (h v) -> p h v", v=128),
                                                  in1=ebl[:, :, j:j + 1].to_broadcast([128, 4, 128]), op=ALU.mult),
                 reads=[StA_b] + ebl_b, writes=Sbf_b)

        def hsq(hh, ba):
            P.op("dve", lambda e: e.tensor_tensor(out=qdT[hh][:, 0:n], in0=oa_sb[:, hh, 0:n], in1=oa_sb[:, hh, 0:n], op=ALU.mult),
                 reads=[oa_b], writes=[qdT_b[hh]])

        def hout(hh, ba):
            osq = qdT[hh]
            osq_b = qdT_b[hh]
            P.pe_group([mm(banks[ba][:, 0:n], ones[:, :], osq[:, 0:n], True, True)], reads=[osq_b, const_b], writes=[bank_b[ba]])
            P.op("act", lambda e: e.activation(out=lnv[:, 0:n], in_=banks[ba][:, 0:n], func=AF.Ln, scale=1.0 / 128, bias=EPS),
                 reads=[bank_b[ba]], writes=[lnv_b])
            P.op("act", lambda e: e.activation(out=rsv[:, 0:n], in_=lnv[:, 0:n], func=AF.Exp, scale=-0.5),
                 reads=[lnv_b], writes=[rsv_b])
            P.op("dve", lambda e: e.tensor_tensor(out=t1[:, 0:n], in0=oa_sb[:, hh, 0:n], in1=rsv[:, 0:n], op=ALU.mult),
                 reads=[oa_b, rsv_b], writes=[t1_b])
            P.op("dve", lambda e: e.scalar_tensor_tensor(out=mixT[:, hh, 0:n], in0=t1[:, 0:n], scalar=gon[:, hh:hh + 1],
                                                         in1=gs[hh][:, 0:n], op0=ALU.mult, op1=ALU.mult),
                 reads=[t1_b, gs_b[hh], const_b], writes=[mixT_b])

        steps = [(lambda ba, j=j: chunk(j, ba)) for j in range(nch)] + [(lambda ba, hh=hh: hsq(hh, ba)) for hh in range(4)] + \
            [(lambda ba, hh=hh: hout(hh, ba)) for hh in range(4)]
        return steps

    def phase_wo(G):
        for half in range(2):
            sl, slb = get_w(("wo", half))
            for k, (col0, ntok) in enumerate(G["dt"]):
                bk = (half * len(G["dt"]) + k) % 4
                P.pe_group([mm(banks[bk][:ntok, :], mixT[:, kc, col0:col0 + ntok], sl[:, kc, :], kc == 0, kc == 7)
                            for kc in range(8)], reads=[slb, mixT_b], writes=[bank_b[bk]])
                hv = h[:ntok, k, half * 512:(half + 1) * 512]
                P.op("dve", lambda e: e.tensor_tensor(out=hv, in0=banks[bk][:ntok, :], in1=hv, op=ALU.add),
                     reads=[bank_b[bk], h_b[k]], writes=[h_b[k]])

    def phase_ffn(G, sample, seq, last_st, ptok0=None):
        n = G["n"]
        nsq = NSEQ_S if sample else 1
        L = n // nsq
        ui = 0
        for g0 in range(0, NFF, 4):
            ncc = min(4, NFF - g0)
            if ptok0 is not None and g0 == 0:
                p_load(G, False, ptok0)
            if ptok0 is not None and g0 == 12:
                p_transposes(G, False, ptok0)
            slg, slgb = get_w(("gate", g0))
            slu, slub = get_w(("up", g0), oldest=ws["last"])
            for cc in range(ncc):
                c = g0 + cc
                bg = (2 * c) % 8; bu = bg + 1
                P.pe_group([mm(banks[bg][:, 0:n], slg[:, kc, cc * 128:(cc + 1) * 128], aT[:, kc, 0:n], kc == 0, kc == 7)
                            for kc in range(8)], reads=[slgb, aT_b], writes=[bank_b[bg]])
                P.pe_group([mm(banks[bu][:, 0:n], slu[:, kc, cc * 128:(cc + 1) * 128], aT[:, kc, 0:n], kc == 0, kc == 7)
                            for kc in range(8)], reads=[slub, aT_b], writes=[bank_b[bu]])
                u = uext[ui % 2]; ub = uext_b[ui % 2]
                cv_ = ccv[ui % 2]; cb_ = ccv_b[ui % 2]
                sv = scv[ui % 2]; svb = scv_b[ui % 2]
                ui += 1
                u3 = u[:, 0:nsq * (L + 2)].rearrange("p (s l) -> p s l", l=L + 2)
                c3 = cv_[:, 0:n].rearrange("p (s l) -> p s l", l=L)
                ps3 = banks[bg][:, 0:n].rearrange("p (s l) -> p s l", l=L)
                P.op("act", lambda e: e.activation(out=u3[:, :, 2:L + 2], in_=ps3, func=AF.Copy), reads=[bank_b[bg]], writes=[ub])
                P.op("act", lambda e: e.activation(out=cv_[:, 0:n], in_=banks[bg][:, 0:n], func=AF.Identity,
                                                   scale=convw[:, c, 2:3], bias=convb[:, c:c + 1]),
                     reads=[bank_b[bg], const_b], writes=[cb_])
                P.op("dve", lambda e: e.tensor_copy(out=u3[:, :, 0:2], in_=convhist[:, c, 0:nsq, :]),
                     reads=[convhist_b[c]], writes=[ub])
                P.op("dve", lambda e: e.tensor_copy(out=convhist[:, c, 0:nsq, :], in_=u3[:, :, L:L + 2]),
                     reads=[ub], writes=[convhist_b[c]])
                P.op("dve", lambda e: e.scalar_tensor_tensor(out=c3, in0=u3[:, :, 1:L + 1], scalar=convw[:, c, 1:2], in1=c3,
                                                             op0=ALU.mult, op1=ALU.add),
                     reads=[ub, cb_, const_b], writes=[cb_])
                P.op("dve", lambda e: e.scalar_tensor_tensor(out=c3, in0=u3[:, :, 0:L], scalar=convw[:, c, 0:1], in1=c3,
                                                             op0=ALU.mult, op1=ALU.add),
                     reads=[ub, cb_, const_b], writes=[cb_])
                P.op("act", lambda e: e.activation(out=sv[:, 0:n], in_=cv_[:, 0:n], func=AF.Silu), reads=[cb_], writes=[svb])
                P.op("dve", lambda e: e.tensor_tensor(out=gT[:, c, 0:n], in0=sv[:, 0:n], in1=banks[bu][:, 0:n], op=ALU.mult),
                     reads=[svb, bank_b[bu]], writes=[gT_b])
                if sample:
                    for s_ in range(NSEQ_S):
                        P.dma("sp", conv_s[s_, :, c, :], convhist[:, c, s_, :], reads=[convhist_b[c]])
                elif last_st:
                    P.dma("sp", conv_p[seq, :, c, :], convhist[:, c, 0, :], reads=[convhist_b[c]])
        ndt = len(G["dt"])
        for half in range(2):
            accb = [half * 4 + k for k in range(ndt)]
            for g0 in range(0, NFF, 8):
                ncc = min(8, NFF - g0)
                sld, sldb = get_w(("down", half, g0))
                fns = []
                for cc in range(ncc):
                    c = g0 + cc
                    for k, (col0, ntok) in enumerate(G["dt"]):
                        fns.append(mm(banks[accb[k]][:ntok, :], gT[:, c, col0:col0 + ntok], sld[:, cc, :], c == 0, c == NFF - 1))
                P.pe_group(fns, reads=[sldb, gT_b], writes=[bank_b[b] for b in accb])
            for k, (col0, ntok) in enumerate(G["dt"]):
                hv = h[:ntok, k, half * 512:(half + 1) * 512]
                P.op("dve", lambda e: e.tensor_tensor(out=hv, in0=banks[accb[k]][:ntok, :], in1=hv, op=ALU.add),
                     reads=[bank_b[accb[k]], h_b[k]], writes=[h_b[k]])

    def p_load(G, sample, tok0):
        P.dma("pool", wpp_sb[:, 0:2, :], w_pp[:, 0:512].rearrange("(kc p) n -> p kc n", p=128), writes=[wpp_b])
        P.dma("pool", wpp_sb[:, 2:4, :], w_pp[:, 512:1024].rearrange("(kc p) n -> p kc n", p=128), writes=[wpp_b])
        for k, (col0, ntok) in enumerate(G["dt"]):
            src = (p_s if sample else p_p)[tok0 + col0: tok0 + col0 + ntok, :]
            P.dma("pool", pb[:ntok, k, :], src, writes=[pb_b])

    def p_transposes(G, sample, tok0):
        for k, (col0, ntok) in enumerate(G["dt"]):
            tb_ = 6 + (k % 2)
            bv = bank_bf(tb_)
            P.pe_group([(lambda e, c=c: e.transpose(out=bv[:, c * 128:c * 128 + ntok], in_=pb[:ntok, k, c * 128:(c + 1) * 128],
                                                    identity=ident[:ntok, :ntok])) for c in range(2)],
                       reads=[pb_b, const_b], writes=[bank_b[tb_]])
            P.op("act", lambda e: e.activation(out=pT[:, :, col0:col0 + ntok],
                                               in_=bv[:, 0:256].rearrange("p (c t) -> p c t", t=128)[:, :, 0:ntok], func=AF.Copy),
                 reads=[bank_b[tb_]], writes=[pT_b])

    def phase_ple_final_prompt(G, tok0, has_next):
        slp, slpb = wpp_sb, wpp_b
        ti = 0
        u_ = 0
        dt_ = G["dt"]
        nsis = {}
        P.handoff([gT_b], yst_b)
        for half in range(2):
            sl, slb = get_w(("pg", half))
            for k, (col0, ntok) in enumerate(dt_):
                bg = (2 * u_) % 6; bp = bg + 1
                u_ += 1
                P.pe_group([mm(banks[bg][:ntok, :], aT[:, kc, col0:col0 + ntok], sl[:, kc, :], kc == 0, kc == 7)
                            for kc in range(8)], reads=[slb, aT_b], writes=[bank_b[bg]])
                P.pe_group([mm(banks[bp][:ntok, :], pT[:, kc, col0:col0 + ntok], slp[:, 2 * half + kc, :], kc == 0, kc == 1)
                            for kc in range(2)], reads=[slpb, pT_b], writes=[bank_b[bp]])
                tgv = tg[ti % 2]; tgb = tg_b[ti % 2]; t2v = t2[ti % 2]; t2b = t2_b[ti % 2]
                ti += 1
                P.op("act", lambda e: e.activation(out=tgv[:ntok, :], in_=banks[bg][:ntok, :], func=AF.Tanh, scale=0.5),
                     reads=[bank_b[bg]], writes=[tgb])
                P.op("dve", lambda e: e.scalar_tensor_tensor(out=t2v[:ntok, :], in0=tgv[:ntok, :], scalar=1.0, in1=banks[bp][:ntok, :],
                                                             op0=ALU.add, op1=ALU.mult),
                     reads=[tgb, bank_b[bp]], writes=[t2b])
                cs_ = slice(half * 512, (half + 1) * 512)
                P.op("dve", lambda e: e.scalar_tensor_tensor(out=yst[:ntok, k, cs_], in0=t2v[:ntok, :], scalar=0.5, in1=h[:ntok, k, cs_],
                                                             op0=ALU.mult, op1=ALU.add),
                     reads=[t2b, h_b[k]], writes=[yst_b[k]])
                if half == 1 and has_next:
                    nt0 = tok0 + T
                    P.dma("sp", h[:, k, :], x_p[nt0 + k * 128: nt0 + (k + 1) * 128, :], writes=[h_b[k]])
                    if k >= 2:
                        kk = k - 2
                        nsis[kk] = norm_A(kk, 128)
                        norm_B(kk, 128, nsis[kk])
        if has_next:
            norm_C(0, 128)
            norm_C(1, 128)
            for kk in (2, 3):
                nsis[kk] = norm_A(kk, 128)
            norm_D(0, 0, 128, 0)
            norm_B(2, 128, nsis[2])
            norm_C(2, 128)
            norm_D(1, 128, 128, 0)
            norm_B(3, 128, nsis[3])
            norm_C(3, 128)
            norm_D(2, 256, 128, 0)
            norm_D(3, 384, 128, 0)
        for k, (col0, ntok) in enumerate(dt_):
            si = ss_n[0] % 16
            ss_n[0] += 1
            P.op("act", lambda e: e.activation(out=junk[:ntok, :], in_=yst[:ntok, k, :], func=AF.Square,
                                               accum_out=ss[:ntok, si:si + 1]),
                 reads=[yst_b[k]], writes=[junk_b, ss_b[si]])
            P.op("dve", lambda e: e.tensor_scalar(out=ms[:ntok, si:si + 1], in0=ss[:ntok, si:si + 1], scalar1=1.0 / D,
                                                  scalar2=EPS, op0=ALU.mult, op1=ALU.add),
                 reads=[ss_b[si]], writes=[ss_b[si]])
            P.op("pool", lambda e: e.tensor_tensor(out=rstd[:ntok, si:si + 1], in0=ms[:ntok, si:si + 1],
                                                   in1=neghalf[:ntok, 0:1], op=ALU.pow),
                 reads=[ss_b[si], const_b], writes=[ss_b[si]])
            P.op("dve", lambda e: e.scalar_tensor_tensor(out=yst[:ntok, k, :], in0=yst[:ntok, k, :], scalar=rstd[:ntok, si:si + 1],
                                                         in1=gfin[:ntok, :], op0=ALU.mult, op1=ALU.mult),
                 reads=[yst_b[k], ss_b[si], const_b], writes=[yst_b[k]])
            P.dma("sp", y_p[tok0 + col0: tok0 + col0 + ntok, :], yst[:ntok, k, :], reads=[yst_b[k]])

    def phase_ple(G, sample, tok0):
        p_load(G, sample, tok0)
        p_transposes(G, sample, tok0)
        slp, slpb = wpp_sb, wpp_b
        ti = 0
        for half in range(2):
            sl, slb = get_w(("pg", half))
            for k, (col0, ntok) in enumerate(G["dt"]):
                bg = (2 * (half * len(G["dt"]) + k)) % 8; bp = bg + 1
                P.pe_group([mm(banks[bg][:ntok, :], aT[:, kc, col0:col0 + ntok], sl[:, kc, :], kc == 0, kc == 7)
                            for kc in range(8)], reads=[slb, aT_b], writes=[bank_b[bg]])
                P.pe_group([mm(banks[bp][:ntok, :], pT[:, kc, col0:col0 + ntok], slp[:, 2 * half + kc, :], kc == 0, kc == 1)
                            for kc in range(2)], reads=[slpb, pT_b], writes=[bank_b[bp]])
                tgv = tg[ti % 2]; tgb = tg_b[ti % 2]; t2v = t2[ti % 2]; t2b = t2_b[ti % 2]
                ti += 1
                P.op("act", lambda e: e.activation(out=tgv[:ntok, :], in_=banks[bg][:ntok, :], func=AF.Tanh, scale=0.5),
                     reads=[bank_b[bg]], writes=[tgb])
                P.op("dve", lambda e: e.scalar_tensor_tensor(out=t2v[:ntok, :], in0=tgv[:ntok, :], scalar=1.0, in1=banks[bp][:ntok, :],
                                                             op0=ALU.add, op1=ALU.mult),
                     reads=[tgb, bank_b[bp]], writes=[t2b])
                hv = h[:ntok, k, half * 512:(half + 1) * 512]
                P.op("dve", lambda e: e.scalar_tensor_tensor(out=hv, in0=t2v[:ntok, :], scalar=0.5, in1=hv, op0=ALU.mult, op1=ALU.add),
                     reads=[t2b, h_b[k]], writes=[h_b[k]])

    def phase_final(G, sample, tok0):
        for k, (col0, ntok) in enumerate(G["dt"]):
            si = ss_n[0] % 16
            ss_n[0] += 1
            P.op("act", lambda e: e.activation(out=junk[:ntok, :], in_=h[:ntok, k, :], func=AF.Square, accum_out=ss[:ntok, si:si + 1]),
                 reads=[h_b[k]], writes=[junk_b, ss_b[si]])
            P.op("dve", lambda e: e.tensor_scalar(out=ms[:ntok, si:si + 1], in0=ss[:ntok, si:si + 1], scalar1=1.0 / D, scalar2=EPS,
                                                  op0=ALU.mult, op1=ALU.add), reads=[ss_b[si]], writes=[ss_b[si]])
            P.op("pool", lambda e: e.tensor_tensor(out=rstd[:ntok, si:si + 1], in0=ms[:ntok, si:si + 1], in1=neghalf[:ntok, 0:1],
                                                   op=ALU.pow), reads=[ss_b[si], const_b], writes=[ss_b[si]])
            P.op("dve", lambda e: e.scalar_tensor_tensor(out=h[:ntok, k, :], in0=h[:ntok, k, :], scalar=rstd[:ntok, si:si + 1],
                                                         in1=gfin[:ntok, :], op0=ALU.mult, op1=ALU.mult),
                 reads=[h_b[k], ss_b[si], const_b], writes=[h_b[k]])
            dst = (y_s if sample else y_p)[tok0 + col0: tok0 + col0 + ntok, :]
            P.dma("sp", dst, h[:ntok, k, :], reads=[h_b[k]])

    Gp = {"n": T, "dt": [(i * 128, 128) for i in range(NT)], "stl": [(i * 128, 128) for i in range(NT)]}
    for seq in range(NSEQ_P):
        for hh in range(4):
            P.op("dve", lambda e, hh=hh: e.memset(S[:, hh, :], 0.0), writes=[S_b[hh]])
            P.op("dve", lambda e, hh=hh: e.memset(S_bf[:, hh, :], 0.0), writes=[Sbf_b[hh]])
        for c in range(NFF):
            P.op("pool", lambda e, c=c: e.memset(convhist[:, c, :, :], 0.0), writes=[convhist_b[c]])
        for st in range(NST):
            tok0 = seq * SEQ + st * T
            first = (seq == 0 and st == 0)
            last = (seq == NSEQ_P - 1 and st == NST - 1)
            P.mark('norm0')
            if first:
                for k in range(NT):
                    P.dma("sp", h[:, k, :], x_p[tok0 + k * 128: tok0 + (k + 1) * 128, :], writes=[h_b[k]])
                norm_T(Gp, 0)
            P.handoff(regH_B, regH_A)
            P.handoff(regM_B, regM_A)
            P.mark('win')
            phase_win(Gp, False, seq, st)
            P.mark('gates')
            hgrn_g2(Gp)
            P.handoff([lf_b, bb_b, eb_b, enb_b], [oa_b])
            P.mark('attn')
            ci_ = seq * NST + st
            if ci_ < 2 * NSEQ_S:
                src_, dst_ = (ck_in, ck_s) if ci_ % 2 == 0 else (cv_in, cv_s)
                P.dma("sp", dst_[ci_ // 2, 0:WB - TS, :], src_[ci_ // 2, TS:WB, :], ring="spc")
            phase_attn_prompt(Gp, st, hgrn_side_steps(Gp))
            P.handoff([oa_b], [lf_b, bb_b, eb_b, enb_b])
            P.mark('wo')
            phase_wo_norm1(Gp)
            P.mark('norm1')
            P.handoff(regH_A, regH_B)
            P.handoff(regM_A, regM_B)
            P.mark('ffn')
            phase_ffn(Gp, False, seq, st == NST - 1, ptok0=tok0)
            P.mark('norm2')
            norm_T(Gp, 2)
            P.mark('ple')
            phase_ple_final_prompt(Gp, tok0, not last)
            P.mark('final')
        for hh in range(4):
            P.dma("sp", st_p[seq, hh, :, :], S[:, hh, :], reads=[S_b[hh]])

    P.mark('sample')
    Gs = {"n": NSEQ_S * TS, "dt": [(0, NSEQ_S * TS)], "stl": [(s_ * TS, TS) for s_ in range(NSEQ_S)]}
    for c in range(NFF):
        P.dma("sp", convhist[:, c, :, :], conv_in[:, c, :, :], writes=[convhist_b[c]])
    P.dma("sp", h[0:NSEQ_S * TS, 0, :], x_s[:, :], writes=[h_b[0]])
    norm_T(Gs, 0)
    P.handoff(regH_B, regH_A)
    P.handoff(regM_B, regM_A)
    n = Gs["n"]
    for blk, kind in ((1, "f"), (0, "q"), (3, "g")):
        sl, slb = get_w(("win", blk))
        for hh in range(4):
            bk = hh
            P.pe_group([mm(banks[bk][:, 0:n], sl[:, kc, hh * 128:(hh + 1) * 128], aT[:, kc, 0:n], kc == 0, kc == 7)
                        for kc in range(8)], reads=[slb, aT_b], writes=[bank_b[bk]])
            if kind == "f":
                P.op("act", lambda e: e.activation(out=th[hh][:, 0:n], in_=banks[bk][:, 0:n], func=AF.Tanh, scale=0.5),
                     reads=[bank_b[bk]], writes=[th_b[hh]])
            elif kind == "q":
                P.op("act", lambda e: e.activation(out=sq[hh][:, 0:n], in_=banks[bk][:, 0:n], func=AF.Silu),
                     reads=[bank_b[bk]], writes=[sq_b[hh]])
            else:
                P.op("act", lambda e: e.activation(out=gs[hh][:, 0:n], in_=banks[bk][:, 0:n], func=AF.Silu),
                     reads=[bank_b[bk]], writes=[gs_b[hh]])
    sl, slb = get_w(("win", 2))
    for k, (col0, ntok) in enumerate(Gs["stl"]):
        bk = k % 4
        P.pe_group([mm(banks[bk][:ntok, :], aT[:, kc, col0:col0 + ntok], sl[:, kc, :], kc == 0, kc == 7) for kc in range(8)],
                   reads=[slb, aT_b], writes=[bank_b[bk]])
        P.op("act", lambda e: e.activation(out=ia_tok[:ntok, k, :], in_=banks[bk][:ntok, :], func=AF.Copy),
             reads=[bank_b[bk]], writes=[ia_b])
    hgrn_gates(Gs, True)
    hgrn_chunks(Gs, True)
    hgrn_out(Gs)
    slq, slqb = get_w(("win", 4))
    for k, (col0, ntok) in enumerate(Gs["stl"]):
        bk = k % 4
        P.pe_group([mm(banks[bk][:ntok, :], aT[:, kc, col0:col0 + ntok], slq[:, kc, :], kc == 0, kc == 7) for kc in range(8)],
                   reads=[slqb, aT_b], writes=[bank_b[bk]])
        P.op("act", lambda e: e.activation(out=stage[0][:ntok, :], in_=banks[bk][:ntok, :], func=AF.Copy),
             reads=[bank_b[bk]], writes=[stage_b[0]])
        rotary(Gs, stage[0], css, ntok, [stage_b[0]])
        P.op("dve", lambda e: e.tensor_copy(out=qkb[0][:ntok, :], in_=stage[0][:ntok, :]), reads=[stage_b[0]], writes=[qkb_b[0]])
        bv = bank_bf(6)
        P.pe_group([(lambda e, c=c: e.transpose(out=bv[:, c * 128:c * 128 + ntok], in_=qkb[0][:ntok, c * 128:(c + 1) * 128],
                                                identity=ident[:ntok, :ntok])) for c in range(4)],
                   reads=[qkb_b[0], const_b], writes=[bank_b[6]])
        P.op("act", lambda e: e.activation(out=QT[:, :, col0:col0 + ntok],
                                           in_=bv[:, 0:512].rearrange("p (c t) -> p c t", t=128)[:, :, 0:ntok], func=AF.Copy, scale=0.125),
             reads=[bank_b[6]], writes=[QT_b])
    slk, slkb = get_w(("win", 5))
    slv, slvb = get_w(("win", 6), oldest=ws["last"])
    for s_, (col0, ntok) in enumerate(Gs["stl"]):
        P.pe_group([mm(banks[0][:ntok, :], aT[:, kc, col0:col0 + ntok], slk[:, kc, :], kc == 0, kc == 7) for kc in range(8)],
                   reads=[slkb, aT_b], writes=[bank_b[0]])
        P.op("act", lambda e: e.activation(out=stage[1][:ntok, :], in_=banks[0][:ntok, :], func=AF.Copy),
             reads=[bank_b[0]], writes=[stage_b[1]])
        rotary(Gs, stage[1], css, ntok, [stage_b[1]])
        P.dma("sp", ck_s[s_, WB - TS:WB, :], stage[1][:ntok, :], reads=[stage_b[1]])
        P.op("dve", lambda e: e.tensor_copy(out=qkb[1][:ntok, :], in_=stage[1][:ntok, :]), reads=[stage_b[1]], writes=[qkb_b[1]])
        bv = bank_bf(7)
        P.pe_group([(lambda e, c=c: e.transpose(out=bv[:, c * 128:c * 128 + ntok], in_=qkb[1][:ntok, c * 128:(c + 1) * 128],
                                                identity=ident[:ntok, :ntok])) for c in range(4)],
                   reads=[qkb_b[1], const_b], writes=[bank_b[7]])
        P.op("act", lambda e: e.activation(out=KTn[:, :, 0:ntok],
                                           in_=bv[:, 0:512].rearrange("p (c t) -> p c t", t=128)[:, :, 0:ntok], func=AF.Copy),
             reads=[bank_b[7]], writes=[KTn_b])
        P.pe_group([mm(banks[1][:ntok, :], aT[:, kc, col0:col0 + ntok], slv[:, kc, :], kc == 0, kc == 7) for kc in range(8)],
                   reads=[slvb, aT_b], writes=[bank_b[1]])
        P.op("act", lambda e: e.activation(out=stage[2][:ntok, :], in_=banks[1][:ntok, :], func=AF.Copy),
             reads=[bank_b[1]], writes=[stage_b[2]])
        P.dma("sp", cv_s[s_, WB - TS:WB, :], stage[2][:ntok, :], reads=[stage_b[2]])
        P.op("dve", lambda e: e.tensor_copy(out=Vn[:ntok, :], in_=stage[2][:ntok, :]), reads=[stage_b[2]], writes=[Vn_b])
        if s_ == 0:
            sKT_b = [Buf("sKT%d" % i) for i in range(7)]; sVH_b = [Buf("sVH%d" % i) for i in range(7)]
            P.handoff([KT_b], sKT_b)
            P.handoff([VH_b], sVH_b)
        for sl_ in range(7):
            ci = sl_ % 2
            if sl_ < 3:
                for r_ in range(4):
                    srck = ck_in[s_, 512 * sl_:512 * (sl_ + 1), :].rearrange("(m r) c -> r m c", r=16)[r_]
                    srcv = cv_in[s_, 512 * sl_:512 * (sl_ + 1), :].rearrange("(m r) c -> r m c", r=16)[r_]
                    P.dma("pool", cbf[ci][32 * r_:32 * (r_ + 1), :], srck, writes=[cbf_b[ci]])
                    P.dma("pool", VH[32 * r_:32 * (r_ + 1), sl_, :], srcv, writes=[sVH_b[sl_]])
            else:
                kt = 9 + sl_
                P.dma("pool", cbf[ci][:, :], ck_in[s_, kt * 128:(kt + 1) * 128, :], writes=[cbf_b[ci]])
                P.dma("pool", VH[:, sl_, :], cv_in[s_, kt * 128:(kt + 1) * 128, :], writes=[sVH_b[sl_]])
            tb_ = 6 + (sl_ % 2)
            bv = bank_bf(tb_)
            P.pe_group([(lambda e, c=c: e.transpose(out=bv[:, c * 128:(c + 1) * 128], in_=cbf[ci][:, c * 128:(c + 1) * 128],
                                                    identity=ident[:, :])) for c in range(4)],
                       reads=[cbf_b[ci], const_b], writes=[bank_b[tb_]])
            P.op("act", lambda e: e.activation(out=KT[:, :, sl_ * 128:(sl_ + 1) * 128],
                                               in_=bv[:, 0:512].rearrange("p (c t) -> p c t", t=128), func=AF.Copy),
                 reads=[bank_b[tb_]], writes=[sKT_b[sl_]])
        qc = slice(col0, col0 + TS)
        sitems = []
        for sl_ in range(7):
            mi = 17 if sl_ < 3 else 16 - (9 + sl_)
            sitems.append((lambda pr, e2, sl_=sl_: KT[64 * e2:64 * e2 + 64, pr, sl_ * 128:(sl_ + 1) * 128],
                           lambda hd, sl_=sl_: VH[:, sl_, hd * 64:(hd + 1) * 64],
                           maskr[:, mi, 0:TS], 128, [sKT_b[sl_], sVH_b[sl_]]))
        sitems.append((lambda pr, e2: KTn[64 * e2:64 * e2 + 64, pr, 0:TS],
                       lambda hd: Vn[0:TS, hd * 64:(hd + 1) * 64],
                       maskr[0:TS, 0, 0:TS], TS, [KTn_b, Vn_b]))
        pend = []

        def s_qk(idx, it):
            ktf, vf, m_ap, nk, rd = it
            i2 = idx % 2
            sbk = 2 + 3 * i2
            sc2 = psum_all[:, sbk * 512:(sbk + 2) * 512].rearrange("p (e n) -> p e n", n=512)
            P.pe_group([mm(sc2[:nk, e2, pr * TS:(pr + 1) * TS], ktf(pr, e2), QT[64 * e2:64 * e2 + 64, pr, qc], True, True)
                        for pr in range(4) for e2 in range(2)], reads=rd + [QT_b], writes=[bank_b[sbk], bank_b[sbk + 1]])
            P.op("act", lambda e: e.activation(out=Pexp[i2][:nk, :, 0:16], in_=sc2[:nk, :, 0:16], func=AF.Exp),
                 reads=[bank_b[sbk], bank_b[sbk + 1]], writes=[Pexp_b[i2]])
            P.op("dve", lambda e: e.tensor_tensor(out=Pm[i2][:nk, :, 0:16].rearrange("p e (a t) -> p e a t", t=TS),
                                                  in0=Pexp[i2][:nk, :, 0:16].rearrange("p e (a t) -> p e a t", t=TS),
                                                  in1=m_ap.unsqueeze(1).unsqueeze(1).to_broadcast([nk, 2, 4, TS]), op=ALU.mult),
                 reads=[Pexp_b[i2], const_b], writes=[Pm_b[i2]])

        def s_pv(idx, it):
            ktf, vf, m_ap, nk, rd = it
            i2 = idx % 2
            obk = 4 + 3 * i2
            fns = []
            for pr in range(4):
                for e2 in range(2):
                    hd = 2 * pr + e2
                    rhs = Pm[i2][:nk, e2, pr * TS:(pr + 1) * TS]
                    fns.append(mm(banks[obk][64 * e2:64 * e2 + 64, pr * TS:(pr + 1) * TS], vf(hd), rhs, True, True))
                    fns.append(mm(banks[obk][64 * e2:64 * e2 + 64, 16 + pr * TS:16 + (pr + 1) * TS], ones[:nk, 0:64], rhs, True, True))
            P.pe_group(fns, reads=rd + [Pm_b[i2], const_b], writes=[bank_b[obk]])
            if idx == 0:
                P.op("dve", lambda e: e.tensor_copy(out=sacc[:, :], in_=banks[obk][:, 0:32]), reads=[bank_b[obk]], writes=[sacc_b])
            else:
                P.op("dve", lambda e: e.tensor_tensor(out=sacc[:, :], in0=banks[obk][:, 0:32], in1=sacc[:, :], op=ALU.add),
                     reads=[bank_b[obk], sacc_b], writes=[sacc_b])

        for idx, it in enumerate(sitems):
            s_qk(idx, it)
            pend.append((idx, it))
            if len(pend) > 1:
                s_pv(*pend.pop(0))
        while pend:
            s_pv(*pend.pop(0))
        P.op("dve", lambda e: e.reciprocal(out=sacc[:, 16:32], in_=sacc[:, 16:32]), reads=[sacc_b], writes=[sacc_b])
        P.op("dve", lambda e: e.tensor_tensor(out=mixT[:, 4:8, qc], in0=sacc[:, 0:16].rearrange("p (a t) -> p a t", t=TS),
                                              in1=sacc[:, 16:32].rearrange("p (a t) -> p a t", t=TS), op=ALU.mult),
             reads=[sacc_b], writes=[mixT_b])
    phase_wo(Gs)
    norm_T(Gs, 1)
    P.handoff(regH_A, regH_B)
    P.handoff(regM_A, regM_B)
    phase_ffn(Gs, True, 0, True)
    norm_T(Gs, 2)
    phase_ple(Gs, True, 0)
    phase_final(Gs, True, 0)

    P.mark('end')
    P.final_wait()
    return nc, P


_CACHE = {}


def _consts():
    maskr = np.zeros((20, 128, 128), np.float32)
    kk = np.arange(128)[:, None]
    qq = np.arange(128)[None, :]
    for off in range(17):
        diff = 128 * off + qq - kk
        m = np.zeros((128, 128), np.float32)
        for dil in (1, 4, 16):
            m += ((diff >= 0) & (diff % dil == 0) & (diff <= 128 * dil)).astype(np.float32)
        maskr[off] = m
    for t_ in range(TS):
        maskr[17, 32 * t_:32 * (t_ + 1), t_] = 1.0
    ii = np.arange(128)[:, None]; jj = np.arange(128)[None, :]
    maskr[18] = ((ii - jj) % 4 == 0).astype(np.float32)
    maskr[19] = (((ii - jj) % 4 == 0) & (jj <= ii)).astype(np.float32)
    maskr = np.ascontiguousarray(maskr.transpose(1, 0, 2))
    s = np.arange(64)[:, None]
    t = np.arange(64)[None, :]
    cz = (s <= t).astype(np.float32)
    causal = np.concatenate([cz, cz], axis=0)
    half = 8
    inv_freq = np.power(np.float32(500000.0), -np.arange(half, dtype=np.float32) * np.float32(2.0 / 16)).astype(np.float32)

    def cs(pos):
        ang = (pos.astype(np.float32)[:, None] * inv_freq[None, :]).astype(np.float32)
        return np.concatenate([np.cos(ang), np.sin(ang)], axis=1).astype(np.float32)

    csp = cs(np.arange(SEQ)).reshape(SEQ // 128, 128, 16).transpose(1, 0, 2)
    css = cs(PAST + np.arange(TS))
    return maskr, causal, np.ascontiguousarray(csp), np.ascontiguousarray(css)


def kernel(x_prompt, x_sample, state_hgrn, cache_win_k, cache_win_v, state_ffn_conv, p_prompt, p_sample,
           norm_attn_g, w_in, hgrn_lb_logits, hgrn_onorm_g, w_o, norm_ffn_g, w_gate, w_up, conv_w, conv_b,
           w_down, norm_ple_g, w_ple_gate, w_ple_proj, norm_final_g):
    f = lambda a: np.ascontiguousarray(np.asarray(a, dtype=np.float32))
    if "nc" not in _CACHE:
        _CACHE["nc"] = build_program()[0]
        _CACHE["consts"] = _consts()
    nc = _CACHE["nc"]
    maskr, causal, csp, css = _CACHE["consts"]
    x_prompt = f(x_prompt); x_sample = f(x_sample); p_prompt = f(p_prompt); p_sample = f(p_sample)
    state_hgrn = f(state_hgrn); cache_win_k = f(cache_win_k); cache_win_v = f(cache_win_v)
    state_ffn_conv = f(state_ffn_conv)

    def cols(g, nchunk):
        return np.ascontiguousarray(f(g).reshape(nchunk, 128).T)

    gcols = np.ascontiguousarray(np.stack([cols(norm_attn_g[0], 8), cols(norm_ffn_g[0], 8), cols(norm_ple_g[0], 8)], axis=1))
    gfin = np.ascontiguousarray(np.broadcast_to(f(norm_final_g)[None, :], (128, D)))
    lbl = np.ascontiguousarray(f(hgrn_lb_logits).reshape(2, 4, 128).transpose(2, 0, 1))
    gon = cols(hgrn_onorm_g[0], 4)
    convw = np.ascontiguousarray(f(conv_w)[0].reshape(3, NFF, 128).transpose(2, 1, 0))
    convb = cols(conv_b[0], NFF)
    shared = {
        "w_in": f(w_in)[0], "w_o": f(w_o)[0], "w_gate": f(w_gate)[0], "w_up": f(w_up)[0], "w_down": f(w_down)[0],
        "w_pg": f(w_ple_gate)[0], "w_pp": f(w_ple_proj)[0], "gcols": gcols, "gfin": gfin, "lbl": lbl, "gon": gon,
        "convw": convw, "convb": convb, "maskr": maskr, "causal": causal, "csp": csp, "css": css,
    }
    in_maps = []
    for c in range(NCORES):
        ps = slice(NSEQ_P * c, NSEQ_P * (c + 1))
        ss_ = slice(NSEQ_S * c, NSEQ_S * (c + 1))
        m = dict(shared)
        m["x_p"] = x_prompt[ps].reshape(NSEQ_P * SEQ, D)
        m["p_p"] = p_prompt[0, ps].reshape(NSEQ_P * SEQ, PLE)
        m["x_s"] = x_sample[ss_].reshape(NSEQ_S * TS, D)
        m["p_s"] = p_sample[0, ss_].reshape(NSEQ_S * TS, PLE)
        m["st_in"] = state_hgrn[0, ss_]
        m["ck_in"] = cache_win_k[0, ss_].reshape(NSEQ_S, WB, 512)
        m["cv_in"] = cache_win_v[0, ss_].reshape(NSEQ_S, WB, 512)
        m["conv_in"] = np.ascontiguousarray(state_ffn_conv[0, ss_].reshape(NSEQ_S, 2, NFF, 128).transpose(3, 2, 0, 1))
        in_maps.append(m)
    res = run_bass_kernel_spmd(nc, in_maps, core_ids=list(range(NCORES)))
    R = res.results
    cat = lambda k: np.concatenate([np.asarray(r[k]) for r in R], axis=0)
    y_p = cat("y_p").reshape(16, SEQ, D)
    y_s = cat("y_s").reshape(32, TS, D)
    st_p = cat("st_p")[None]
    ck_p = cat("ck_p").reshape(1, 16, SEQ, 8, 64)
    cv_p = cat("cv_p").reshape(1, 16, SEQ, 8, 64)
    conv_p = cat("conv_p").transpose(0, 3, 2, 1).reshape(1, 16, 2, DFF)
    st_s = cat("st_s")[None]
    ck_s = cat("ck_s").reshape(1, 32, WB, 8, 64)
    cv_s = cat("cv_s").reshape(1, 32, WB, 8, 64)
    conv_s = cat("conv_s").transpose(0, 3, 2, 1).reshape(1, 32, 2, DFF)
    outs = (y_p, y_s, st_p, ck_p, cv_p, conv_p, st_s, ck_s, cv_s, conv_s)
    return tuple(np.ascontiguousarray(o, dtype=np.float32) for o in outs)
```

```python
import sys
import numpy as np
import concourse.bass as bass
import concourse.mybir as mybir
from concourse.bass_utils import run_bass_kernel_spmd

F32 = mybir.dt.float32
BF16 = mybir.dt.bfloat16
AF = mybir.ActivationFunctionType
ALU = mybir.AluOpType

NCORES = 8
D = 1024
DIN = 3584
DFF = 2816
NFF = 22
PLE = 256
SEQ = 2048
T = 512
NT = 4
NST = SEQ // T
NSEQ_P = 2
NSEQ_S = 4
TS = 4
EPS = 1e-6
WB = 2048
PAST = 16384
NSLAB = 4


class Buf:
    __slots__ = ("name", "w", "r", "excl")

    def __init__(self, name, excl=False):
        self.name = name
        self.w = None
        self.r = {}
        self.excl = excl


class Prog:
    def __init__(self, nc):
        self.nc = nc
        self.eng = {"pe": nc.tensor, "act": nc.scalar, "dve": nc.vector, "pool": nc.gpsimd, "sp": nc.sync}
        self.sem = {}
        self.cnt = {}
        for e in ("pe", "act", "dve", "pool"):
            self.sem[e] = nc.alloc_semaphore(name="sem_" + e)
            self.cnt[e] = 0
        self.known = {e: {} for e in self.eng}
        self.ring = {}
        self.ring_n = {}
        for q, k in (("sp", 12), ("pool", 6), ("spc", 8)):
            self.ring[q] = [[nc.alloc_semaphore(name="dsem_%s%d" % (q, i)), 0] for i in range(k)]
            self.ring_n[q] = 0
        self.n_ins = 0
        self.marks = []
        self.desc = {}

    def _deps(self, reads, writes, e=None):
        toks = []
        for b in reads:
            if b.w is not None:
                toks.append(b.w)
            if b.excl:
                toks.extend(t for t in b.r.values() if t[3] != e)
        for b in writes:
            if b.w is not None:
                toks.append(b.w)
            toks.extend(b.r.values())
        return toks

    def _wait(self, e, toks):
        best = {}
        for t in toks:
            key, sem, val, src = t
            if src == "pe" and e == "pe":
                continue
            if key not in best or best[key][2] < val:
                best[key] = t
        for key, (k2, sem, val, src) in best.items():
            if self.known[e].get(key, 0) >= val:
                continue
            self.eng[e].wait_ge(sem, val)
            self.known[e][key] = val
            self.n_ins += 1

    def _mark(self, tok, reads, writes):
        for b in reads:
            b.r[tok[0]] = tok
        for b in writes:
            b.w = tok
            b.r = {}

    def op(self, e, fn, reads=(), writes=()):
        self._wait(e, self._deps(reads, writes, e))
        ins = fn(self.eng[e])
        self.cnt[e] += 1
        ins.then_inc(self.sem[e], 1)
        tok = (e, self.sem[e], self.cnt[e], e)
        self._mark(tok, reads, writes)
        self.n_ins += 1
        fr = sys._getframe(1)
        self.desc[(e, self.cnt[e])] = "%s:%d" % (fr.f_code.co_name, fr.f_lineno)
        return ins

    def pe_group(self, fns, reads=(), writes=()):
        self._wait("pe", self._deps(reads, writes))
        ins = None
        for fn in fns:
            ins = fn(self.eng["pe"])
            self.n_ins += 1
        self.cnt["pe"] += 1
        ins.then_inc(self.sem["pe"], 1)
        tok = ("pe", self.sem["pe"], self.cnt["pe"], "pe")
        self._mark(tok, reads, writes)
        fr = sys._getframe(1)
        self.desc[("pe", self.cnt["pe"])] = "%s:%d" % (fr.f_code.co_name, fr.f_lineno)

    def dma(self, q, out, in_, reads=(), writes=(), ring=None, **kw):
        rq = ring or q
        ring = self.ring[rq]
        slot = ring[self.ring_n[rq] % len(ring)]
        self.ring_n[rq] += 1
        toks = self._deps(reads, writes)
        key = "d_%s_%d" % (rq, id(slot))
        if slot[1] > 0:
            toks.append((key, slot[0], slot[1], "dma"))
        self._wait(q, toks)
        if q == "pool":
            kw.setdefault("max_dma_last_dim", 4096)
        ins = self.eng[q].dma_start(out=out, in_=in_, **kw)
        slot[1] += 16
        ins.then_inc(slot[0], 16)
        tok = (key, slot[0], slot[1], "dma")
        self._mark(tok, reads, writes)
        self.n_ins += 1

    def mark(self, name):
        self.marks.append((name, dict(self.cnt)))

    def handoff(self, old, new):
        toks = {}
        for b in old:
            if b.w is not None:
                t = b.w
                if t[0] not in toks or toks[t[0]][2] < t[2]:
                    toks[t[0]] = t
            for t in b.r.values():
                if t[0] not in toks or toks[t[0]][2] < t[2]:
                    toks[t[0]] = t
        for b in new:
            for k, t in toks.items():
                if k not in b.r or b.r[k][2] < t[2]:
                    b.r[k] = t

    def final_wait(self):
        for q in self.ring:
            for slot in self.ring[q]:
                if slot[1] > 0:
                    self.eng["sp"].wait_ge(slot[0], slot[1])


def build_program():
    nc = bass.Bass("TRN2", target_bir_lowering=False)
    P = Prog(nc)

    def din(name, shape, dt=F32):
        return nc.dram_tensor(name, list(shape), dt, kind="ExternalInput").ap()

    def dout(name, shape):
        return nc.dram_tensor(name, list(shape), F32, kind="ExternalOutput").ap()

    x_p = din("x_p", [NSEQ_P * SEQ, D])
    p_p = din("p_p", [NSEQ_P * SEQ, PLE])
    x_s = din("x_s", [NSEQ_S * TS, D])
    p_s = din("p_s", [NSEQ_S * TS, PLE])
    st_in = din("st_in", [NSEQ_S, 4, 128, 128])
    ck_in = din("ck_in", [NSEQ_S, WB, 512])
    cv_in = din("cv_in", [NSEQ_S, WB, 512])
    conv_in = din("conv_in", [128, NFF, NSEQ_S, 2])
    w_in = din("w_in", [D, DIN])
    w_o = din("w_o", [D, D])
    w_gate = din("w_gate", [D, DFF])
    w_up = din("w_up", [D, DFF])
    w_down = din("w_down", [DFF, D])
    w_pg = din("w_pg", [D, D])
    w_pp = din("w_pp", [PLE, D])
    gcols_d = din("gcols", [128, 3, 8])
    gfin_d = din("gfin", [128, D])
    lbl_d = din("lbl", [128, 2, 4])
    gon_d = din("gon", [128, 4])
    convw_d = din("convw", [128, NFF, 3])
    convb_d = din("convb", [128, NFF])
    maskr_d = din("maskr", [128, 20, 128])
    causal_d = din("causal", [128, 64])
    csp_d = din("csp", [128, SEQ // 128, 16])
    css_d = din("css", [TS, 16])

    y_p = dout("y_p", [NSEQ_P * SEQ, D])
    y_s = dout("y_s", [NSEQ_S * TS, D])
    st_p = dout("st_p", [NSEQ_P, 4, 128, 128])
    ck_p = dout("ck_p", [NSEQ_P, SEQ, 512])
    cv_p = dout("cv_p", [NSEQ_P, SEQ, 512])
    conv_p = dout("conv_p", [NSEQ_P, 128, NFF, 2])
    st_s = dout("st_s", [NSEQ_S, 4, 128, 128])
    ck_s = dout("ck_s", [NSEQ_S, WB, 512])
    cv_s = dout("cv_s", [NSEQ_S, WB, 512])
    conv_s = dout("conv_s", [NSEQ_S, 128, NFF, 2])
    out_buf = Buf("outputs")

    def sb(name, shape, dt=F32):
        return nc.alloc_sbuf_tensor("sb_" + name, list(shape), dt)

    psum_all = nc.alloc_psum_tensor("psum_all", [128, 8 * 512], F32)
    banks = [psum_all[:, i * 512:(i + 1) * 512] for i in range(8)]
    bank_b = [Buf("bank%d" % i, excl=True) for i in range(8)]

    ident = sb("ident", [128, 128], BF16)
    ones = sb("ones", [128, 128], BF16)
    causal = sb("causal", [128, 64], BF16)
    maskr = sb("maskr", [128, 20, 128], BF16)
    csp = sb("csp", [128, SEQ // 128, 16]); css = sb("css", [TS, 16])
    gcols = sb("gcols", [128, 3, 8]); gfin = sb("gfin", [128, D])
    lbl = sb("lbl", [128, 2, 4]); gon = sb("gon", [128, 4])
    c0 = sb("c0", [128, 4]); c1 = sb("c1", [128, 4]); nc1 = sb("nc1", [128, 4]); lbt = sb("lbt", [128, 4])
    convw = sb("convw", [128, NFF, 3]); convb = sb("convb", [128, NFF])
    rmask_p = sb("rmask_p", [128, T], BF16); rmask_s = sb("rmask_s", [128, 16], BF16)
    neghalf = sb("neghalf", [128, 8])
    const_b = Buf("const")

    KT = sb("KT", [128, 4, SEQ], BF16); KT_b = Buf("KT")
    VH = sb("VH", [128, 8, 512], BF16); VH_b = Buf("VH")
    Vp = sb("Vp", [128, 4, 4, 512], BF16); Vp_b = [Buf("Vp%d" % i) for i in range(4)]
    S = sb("S", [128, 4, 128]); S_bf = sb("S_bf", [128, 4, 128], BF16)
    S_b = [Buf("S%d" % i) for i in range(4)]; Sbf_b = [Buf("Sbf%d" % i) for i in range(4)]
    convhist = sb("convhist", [128, NFF, NSEQ_S, 2]); convhist_b = [Buf("ch%d" % i) for i in range(NFF)]

    h = sb("h", [128, NT, D]); h_b = [Buf("h%d" % i) for i in range(NT)]
    ss = sb("ss", [128, 16]); ms = sb("ms", [128, 16]); rstd = sb("rstd", [128, 16])
    ss_b = [Buf("ss%d" % i) for i in range(16)]
    aT = sb("aT", [128, 8, T], BF16); aT_b = Buf("aT")

    regM = sb("regM", [128, 22 * 1024 // 4])
    regM_bf = regM[:].bitcast(BF16)
    mixT = regM_bf[:, 0:4096].rearrange("p (c t) -> p c t", t=T)
    QT = regM_bf[:, 4096:6144].rearrange("p (c t) -> p c t", t=T)
    ia_tok = regM_bf[:, 6144:8192].rearrange("p (c t) -> p c t", t=512)
    stage = [regM[:, 4096 + i * 512: 4096 + (i + 1) * 512] for i in range(3)]
    gT = regM_bf.rearrange("p (c t) -> p c t", t=T)
    mixT_b = Buf("mixT"); QT_b = Buf("QT"); ia_b = Buf("ia"); stage_b = [Buf("stg%d" % i) for i in range(3)]
    gT_b = Buf("gT")
    yst = regM[:, 0:4096].rearrange("p (k d) -> p k d", d=D)
    yst_b = [Buf("yst%d" % i) for i in range(NT)]
    regM_A = [mixT_b, QT_b, ia_b] + stage_b
    regM_B = [gT_b] + yst_b
    qkb = [sb("qkb%d" % i, [128, 512], BF16) for i in range(4)]; qkb_b = [Buf("qkb%d" % i) for i in range(4)]
    stv = [sb("stv0", [128, 512]), stage[2]]; stv_b = [Buf("stv0"), stage_b[2]]

    HKB = 52
    regH = sb("regH", [128, HKB * 256])
    regH_bf = regH[:].bitcast(BF16)
    identf = regH[:, 0:128]
    o = [0]

    def carve(nelem_f32):
        a = o[0]; o[0] += nelem_f32
        assert o[0] <= HKB * 256
        return a

    sq = []; th = []
    for i in range(4):
        a = carve(512); sq.append(regH[:, a:a + 512])
    for i in range(4):
        a = carve(512); th.append(regH[:, a:a + 512])
    gs = []
    for i in range(4):
        a = carve(256); gs.append(regH_bf[:, 2 * a:2 * a + 512])
    a = carve(512); lf = regH[:, a:a + 512]
    oa_sb = regH[:, a:a + 2048].rearrange("p (h t) -> p h t", t=512)
    a = carve(512); bb = regH[:, a:a + 512]
    a = carve(512); eb = regH[:, a:a + 512]
    a = carve(512); enb = regH[:, a:a + 512]
    qdT = []; kdT = []; kdtok = []
    for i in range(4):
        a = carve(256); qdT.append(regH_bf[:, 2 * a:2 * a + 512])
    for i in range(4):
        a = carve(256); kdT.append(regH_bf[:, 2 * a:2 * a + 512])
    for i in range(4):
        a = carve(256); kdtok.append(regH_bf[:, 2 * a:2 * a + 512].rearrange("p (i k) -> p i k", k=128))
    a = carve(32); ebl = regH[:, a:a + 32].rearrange("p (h j) -> p h j", j=8)
    a = carve(128); AmA = regH_bf[:, 2 * a:2 * a + 256]
    a = carve(256); osq = regH_bf[:, 2 * a:2 * a + 512]

    a = carve(512); lnv = regH[:, a:a + 512]
    a = carve(512); rsv = regH[:, a:a + 512]
    a = carve(512); t1 = regH[:, a:a + 512]
    a = carve(512); StA = regH[:, a:a + 512]
    a = carve(256); rotw = regH[:, a:a + 256].rearrange("p (a h c) -> p a h c", a=4, h=8)
    hg_end = o[0]
    rot_b = Buf("rot"); rot2_b = Buf("rot2")
    sq_b = [Buf("sq%d" % i) for i in range(4)]; th_b = [Buf("th%d" % i) for i in range(4)]
    gs_b = [Buf("gs%d" % i) for i in range(4)]
    lf_b = Buf("lf"); bb_b = Buf("bb"); eb_b = Buf("eb"); enb_b = Buf("enb"); oa_b = Buf("oa")
    qdT_b = [Buf("qdT%d" % i) for i in range(4)]; kdT_b = [Buf("kdT%d" % i) for i in range(4)]
    kdtok_b = [Buf("kdtok%d" % i) for i in range(4)]; ebl_b = [Buf("ebl%d" % i) for i in range(4)]
    AmA_b = Buf("AmA"); osq_b = Buf("osq"); lnv_b = Buf("lnv"); rsv_b = Buf("rsv")
    t1_b = Buf("t1"); StA_b = Buf("StA")
    regH_A = sq_b + th_b + gs_b + [lf_b, bb_b, eb_b, enb_b] + qdT_b + kdT_b + kdtok_b + ebl_b + [AmA_b] + \
        [osq_b, lnv_b, rsv_b, t1_b, StA_b, oa_b, rot_b, rot2_b]
    o[0] = 0
    uext = []; ccv = []; scv = []
    for i in range(2):
        a = carve(520); uext.append(regH[:, a:a + 520])
    for i in range(2):
        a = carve(512); ccv.append(regH[:, a:a + 512])
    for i in range(2):
        a = carve(512); scv.append(regH[:, a:a + 512])
    pst = []
    for i in range(2):
        a = carve(256); pst.append(regH[:, a:a + 256])
    a = carve(512); pb = regH_bf[:, 2 * a:2 * a + 1024].rearrange("p (k c) -> p k c", c=256)
    a = carve(512); pT = regH_bf[:, 2 * a:2 * a + 1024].rearrange("p (c t) -> p c t", t=T)
    tg = []; t2 = []
    for i in range(2):
        a = carve(512); tg.append(regH[:, a:a + 512])
    for i in range(2):
        a = carve(512); t2.append(regH[:, a:a + 512])
    a = carve(1024); wpp_sb = regH_bf[:, 2 * a:2 * a + 2048].rearrange("p (c n) -> p c n", n=512)
    wpp_b = Buf("wpp")
    uext_b = [Buf("uext%d" % i) for i in range(2)]; ccv_b = [Buf("cc%d" % i) for i in range(2)]
    scv_b = [Buf("sc%d" % i) for i in range(2)]; pst_b = [Buf("pst%d" % i) for i in range(2)]
    pb_b = Buf("pb"); pT_b = Buf("pT"); tg_b = [Buf("tg%d" % i) for i in range(2)]; t2_b = [Buf("t2%d" % i) for i in range(2)]
    regH_B = uext_b + ccv_b + scv_b + pst_b + [pb_b, pT_b, wpp_b] + tg_b + t2_b

    Pexp = [sb("Pexp%d" % i, [128, 2, 512], BF16) for i in range(3)]; Pexp_b = [Buf("Pexp%d" % i) for i in range(3)]
    Pm = [sb("Pm%d" % i, [128, 2, 512], BF16) for i in range(3)]; Pm_b = [Buf("Pm%d" % i) for i in range(3)]
    junk = Pexp[0][:, :, :].rearrange("p e n -> p (e n)"); junk_b = Pexp_b[0]
    xs = [Pexp[1 + i][:, :, :].rearrange("p e n -> p (e n)") for i in range(2)]; xs_b = [Pexp_b[1 + i] for i in range(2)]
    lnL = lnv; rL = rsv; lnL_b = lnv_b; rL_b = rsv_b
    KTn = sb("KTn", [128, 4, 4], BF16); KTn_b = Buf("KTn")
    Vn = sb("Vn", [4, 512], BF16); Vn_b = Buf("Vn")
    sacc = sb("sacc", [128, 32]); sacc_b = Buf("sacc")
    cbf = [sb("cbf%d" % i, [128, 512], BF16) for i in range(2)]; cbf_b = [Buf("cbf%d" % i) for i in range(2)]

    slab = [sb("slab%d" % i, [128, 8, 512], BF16) for i in range(NSLAB)]
    slab_b = [Buf("slab%d" % i) for i in range(NSLAB)]
    slab_i = [0]

    def bank_bf(i):
        return banks[i].bitcast(BF16)

    def group_sched():
        L = [("win", b) for b in (1, 0, 3, 2, 4, 5, 6)]
        L += [("wo", 0), ("wo", 1)]
        for g0 in range(0, NFF, 4):
            L += [("gate", g0), ("up", g0)]
        for half in range(2):
            for g0 in range(0, NFF, 8):
                L.append(("down", half, g0))
        L += [("pg", 0), ("pg", 1)]
        return L

    sched = []
    for _ in range(NSEQ_P * NST + 1):
        sched += group_sched()
    ws = {"issued": 0, "consumed": 0}

    def w_parts(desc, sl):
        k = desc[0]
        col = lambda w, c0_, nc_: w[:, c0_:c0_ + nc_].rearrange("(kc p) n -> p kc n", p=128)
        if k == "win":
            return [(sl[:, 0:8, 0:512], col(w_in, desc[1] * 512, 512))]
        if k == "wo":
            return [(sl[:, 0:8, 0:512], col(w_o, desc[1] * 512, 512))]
        if k in ("gate", "up"):
            ncc = min(4, NFF - desc[1])
            return [(sl[:, 0:8, 0:ncc * 128], col(w_gate if k == "gate" else w_up, desc[1] * 128, ncc * 128))]
        if k == "down":
            half, g0 = desc[1], desc[2]
            ncc = min(8, NFF - g0)
            return [(sl[:, 0:ncc, :], w_down[g0 * 128:(g0 + ncc) * 128, half * 512:(half + 1) * 512].rearrange("(cc p) n -> p cc n", p=128))]
        if k == "pp":
            return [(sl[:, 0:2, :], col(w_pp, 0, 512)), (sl[:, 2:4, :], col(w_pp, 512, 512))]
        if k == "pg":
            return [(sl[:, 0:8, 0:512], col(w_pg, desc[1] * 512, 512))]
        raise ValueError(desc)

    def get_w(desc, oldest=None):
        j = ws["consumed"]
        assert sched[j] == desc, (sched[j], desc)
        if oldest is None:
            oldest = j
        while ws["issued"] < min(len(sched), oldest + NSLAB):
            i = ws["issued"]
            for dst, src in w_parts(sched[i], slab[i % NSLAB]):
                P.dma("pool", dst, src, writes=[slab_b[i % NSLAB]])
            ws["issued"] += 1
        assert ws["issued"] > j
        ws["consumed"] += 1
        ws["last"] = j
        return slab[j % NSLAB], slab_b[j % NSLAB]

    def w_prefetch():
        j = ws["consumed"]
        while ws["issued"] < min(len(sched), j + NSLAB):
            i = ws["issued"]
            for dst, src in w_parts(sched[i], slab[i % NSLAB]):
                P.dma("pool", dst, src, writes=[slab_b[i % NSLAB]])
            ws["issued"] += 1

    def mm(out, lhsT, rhs, start, stop):
        return lambda e: e.matmul(out, lhsT=lhsT, rhs=rhs, start=start, stop=stop)

    for dst, src in ((csp, csp_d), (gcols, gcols_d), (gfin, gfin_d), (lbl, lbl_d), (gon, gon_d),
                     (convw, convw_d), (convb, convb_d)):
        P.dma("sp", dst[:], src, writes=[const_b])
    P.dma("sp", css[:], css_d, writes=[const_b])
    P.dma("pool", maskr[:], maskr_d, writes=[const_b])
    P.dma("pool", causal[:], causal_d, writes=[const_b])
    P.op("pool", lambda e: e.memset(identf[:], 1.0), writes=[const_b])
    P.op("pool", lambda e: e.affine_select(out=identf[:], in_=identf[:], pattern=[[-1, 128]],
                                           compare_op=ALU.is_equal, fill=0.0, base=0, channel_multiplier=1),
         writes=[const_b])
    P.op("dve", lambda e: e.tensor_copy(out=ident[:], in_=identf[:]), reads=[const_b], writes=[const_b])
    P.op("pool", lambda e: e.memset(ones[:], 1.0), writes=[const_b])
    P.op("pool", lambda e: e.memset(neghalf[:], -0.5), writes=[const_b])
    P.op("pool", lambda e: e.memset(rmask_p[:], 1.0), writes=[const_b])
    P.op("pool", lambda e: e.memset(rmask_p[:].rearrange("p (c t) -> p c t", t=64)[:, :, 0:1], 0.0), writes=[const_b])
    P.op("pool", lambda e: e.memset(rmask_s[:], 1.0), writes=[const_b])
    P.op("pool", lambda e: e.memset(rmask_s[:].rearrange("p (c t) -> p c t", t=4)[:, :, 0:1], 0.0), writes=[const_b])
    P.op("dve", lambda e: e.tensor_tensor(out=lbt[:], in0=lbl[:, 1, :], in1=lbl[:, 0, :], op=ALU.subtract),
         reads=[const_b], writes=[const_b])
    P.op("act", lambda e: e.activation(out=lbt[:], in_=lbt[:], func=AF.Exp), reads=[const_b], writes=[const_b])
    P.op("dve", lambda e: e.tensor_scalar(out=lbt[:], in0=lbt[:], scalar1=1.0, scalar2=None, op0=ALU.add),
         reads=[const_b], writes=[const_b])
    P.op("dve", lambda e: e.reciprocal(out=lbt[:], in_=lbt[:]), reads=[const_b], writes=[const_b])
    P.op("dve", lambda e: e.tensor_scalar(out=c1[:], in0=lbt[:], scalar1=-0.5, scalar2=0.5, op0=ALU.mult, op1=ALU.add),
         reads=[const_b], writes=[const_b])
    P.op("dve", lambda e: e.tensor_scalar(out=c0[:], in0=lbt[:], scalar1=0.5, scalar2=0.5, op0=ALU.mult, op1=ALU.add),
         reads=[const_b], writes=[const_b])
    P.op("dve", lambda e: e.tensor_scalar(out=nc1[:], in0=c1[:], scalar1=-1.0, scalar2=None, op0=ALU.mult),
         reads=[const_b], writes=[const_b])

    P.handoff([const_b], regH_A + regH_B)
    ss_n = [0]

    def norm_A(k, ntok):
        si = ss_n[0] % 16
        ss_n[0] += 1
        P.op("act", lambda e: e.activation(out=junk[:ntok, :], in_=h[:ntok, k, :], func=AF.Square,
                                           accum_out=ss[:ntok, si:si + 1]),
             reads=[h_b[k]], writes=[junk_b, ss_b[si]])
        P.op("dve", lambda e: e.tensor_scalar(out=ms[:ntok, si:si + 1], in0=ss[:ntok, si:si + 1], scalar1=1.0 / D,
                                              scalar2=EPS, op0=ALU.mult, op1=ALU.add),
             reads=[ss_b[si]], writes=[ss_b[si]])
        P.op("pool", lambda e: e.tensor_tensor(out=rstd[:ntok, si:si + 1], in0=ms[:ntok, si:si + 1],
                                               in1=neghalf[:ntok, 0:1], op=ALU.pow),
             reads=[ss_b[si], const_b], writes=[ss_b[si]])
        return si

    def norm_B(k, ntok, si):
        xi = k % 2
        P.op("dve", lambda e: e.tensor_scalar(out=xs[xi][:ntok, :], in0=h[:ntok, k, :], scalar1=rstd[:ntok, si:si + 1],
                                              scalar2=None, op0=ALU.mult),
             reads=[h_b[k], ss_b[si]], writes=[xs_b[xi]])

    def norm_C(k, ntok):
        xi = k % 2
        bk = 6 + (k % 2)
        bv = bank_bf(bk)
        P.pe_group([(lambda e, c=c: e.transpose(out=bv[:, c * 128:c * 128 + ntok], in_=xs[xi][:ntok, c * 128:(c + 1) * 128],
                                                identity=ident[:ntok, :ntok])) for c in range(8)],
                   reads=[xs_b[xi], const_b], writes=[bank_b[bk]])

    def norm_D(k, col0, ntok, gi):
        bk = 6 + (k % 2)
        bv = bank_bf(bk)
        P.op("dve", lambda e: e.tensor_tensor(
            out=aT[:, :, col0:col0 + ntok],
            in0=bv.rearrange("p (c t) -> p c t", t=128)[:, :, 0:ntok],
            in1=gcols[:, gi, :].unsqueeze(2).to_broadcast([128, 8, ntok]), op=ALU.mult),
            reads=[bank_b[bk], const_b], writes=[aT_b])

    def norm_T(G, gi):
        dt_ = G["dt"]
        n_ = len(dt_)
        sis = {}
        order = []
        for k in range(min(2, n_)):
            order += [("A", k), ("B", k)]
        for k in range(n_):
            order.append(("C", k))
            if k + 2 < n_:
                order.append(("A", k + 2))
            order.append(("D", k))
            if k + 2 < n_:
                order.append(("B", k + 2))
        for what, k in order:
            col0, ntok = dt_[k]
            if what == "A":
                sis[k] = norm_A(k, ntok)
            elif what == "B":
                norm_B(k, ntok, sis[k])
            elif what == "C":
                norm_C(k, ntok)
            else:
                norm_D(k, col0, ntok, gi)

    def phase_wo_norm1(G):
        sl0, slb0 = get_w(("wo", 0))
        sl1, slb1 = get_w(("wo", 1), oldest=ws["last"])
        sls = [(sl0, slb0), (sl1, slb1)]
        dt_ = G["dt"]
        sis = []
        for k, (col0, ntok) in enumerate(dt_):
            for half in range(2):
                sl, slb = sls[half]
                bk = (2 * k + half) % 4
                P.pe_group([mm(banks[bk][:ntok, :], mixT[:, kc, col0:col0 + ntok], sl[:, kc, :], kc == 0, kc == 7)
                            for kc in range(8)], reads=[slb, mixT_b], writes=[bank_b[bk]])
            if k == len(dt_) - 1:
                w_prefetch()
            if k >= 1:
                norm_B(k - 1, dt_[k - 1][1], sis[k - 1])
                norm_C(k - 1, dt_[k - 1][1])
            for half in range(2):
                bk = (2 * k + half) % 4
                hv = h[:ntok, k, half * 512:(half + 1) * 512]
                P.op("dve", lambda e: e.tensor_tensor(out=hv, in0=banks[bk][:ntok, :], in1=hv, op=ALU.add),
                     reads=[bank_b[bk], h_b[k]], writes=[h_b[k]])
            sis.append(norm_A(k, ntok))
            if k >= 1:
                norm_D(k - 1, dt_[k - 1][0], dt_[k - 1][1], 1)
        k = len(dt_) - 1
        norm_B(k, dt_[k][1], sis[k])
        norm_C(k, dt_[k][1])
        norm_D(k, dt_[k][0], dt_[k][1], 1)

    def rotary(G, st_ap, cs_ap, ntok, bufs, dst_ap=None, dst_bufs=None):
        if dst_ap is None:
            dst_ap = st_ap; dst_bufs = bufs
        v = st_ap.rearrange("p (h c) -> p h c", c=64)
        dv = dst_ap.rearrange("p (h c) -> p h c", c=64)
        u = v[:ntok, :, 0:16].rearrange("p h (a c) -> p h a c", c=8)
        cosb = cs_ap[:ntok, 0:8].unsqueeze(1).unsqueeze(1).to_broadcast([ntok, 8, 2, 8])
        sinb = cs_ap[:ntok, 8:16].unsqueeze(1).unsqueeze(1).to_broadcast([ntok, 8, 2, 8])
        U1 = rotw[:ntok, 0:2, :, :].rearrange("p a h c -> p h a c")
        W = rotw[:ntok, 2:4, :, :].rearrange("p a h c -> p h a c")
        P.op("dve", lambda e: e.tensor_tensor(out=U1, in0=u, in1=cosb, op=ALU.mult), reads=bufs + [const_b], writes=[rot_b])
        P.op("dve", lambda e: e.tensor_tensor(out=W, in0=u, in1=sinb, op=ALU.mult), reads=bufs + [const_b], writes=[rot2_b])
        P.op("dve", lambda e: e.tensor_tensor(out=dv[:ntok, :, 0:8], in0=rotw[:ntok, 0, :, :], in1=rotw[:ntok, 3, :, :], op=ALU.subtract),
             reads=[rot_b, rot2_b], writes=dst_bufs)
        P.op("dve", lambda e: e.tensor_tensor(out=dv[:ntok, :, 8:16], in0=rotw[:ntok, 1, :, :], in1=rotw[:ntok, 2, :, :], op=ALU.add),
             reads=[rot_b, rot2_b], writes=dst_bufs)


    def phase_win(G, sample, seq, st):
        n = G["n"]
        for blk, kind in ((1, "f"), (0, "q"), (3, "g")):
            sl, slb = get_w(("win", blk))
            for hh in range(4):
                bk = hh
                P.pe_group([mm(banks[bk][:, 0:n], sl[:, kc, hh * 128:(hh + 1) * 128], aT[:, kc, 0:n], kc == 0, kc == 7)
                            for kc in range(8)], reads=[slb, aT_b], writes=[bank_b[bk]])
                if kind == "f":
                    P.op("act", lambda e: e.activation(out=th[hh][:, 0:n], in_=banks[bk][:, 0:n], func=AF.Tanh, scale=0.5),
                         reads=[bank_b[bk]], writes=[th_b[hh]])
                elif kind == "q":
                    P.op("act", lambda e: e.activation(out=sq[hh][:, 0:n], in_=banks[bk][:, 0:n], func=AF.Silu),
                         reads=[bank_b[bk]], writes=[sq_b[hh]])
                else:
                    P.op("act", lambda e: e.activation(out=gs[hh][:, 0:n], in_=banks[bk][:, 0:n], func=AF.Silu),
                         reads=[bank_b[bk]], writes=[gs_b[hh]])
        for blk, kind in ((2, "i"), (4, "q"), (5, "k"), (6, "v")):
            sl, slb = get_w(("win", blk))
            posts = []
            for k, (col0, ntok) in enumerate(G["stl"]):
                bk = k % 4
                gt = st * NT + k
                P.pe_group([mm(banks[bk][:ntok, :], aT[:, kc, col0:col0 + ntok], sl[:, kc, :], kc == 0, kc == 7)
                            for kc in range(8)], reads=[slb, aT_b], writes=[bank_b[bk]])
                if len(posts) >= 2:
                    posts.pop(0)()
                if kind == "i":
                    P.op("act", lambda e: e.activation(out=ia_tok[:ntok, k, :], in_=banks[bk][:ntok, :], func=AF.Copy),
                         reads=[bank_b[bk]], writes=[ia_b])
                    continue
                d2 = k % 2
                if kind == "v":
                    P.op("act", lambda e: e.activation(out=VH[:, gt % 8, :], in_=banks[bk][:, :], func=AF.Copy),
                         reads=[bank_b[bk]], writes=[VH_b])
                    P.op("dve", lambda e: e.tensor_copy(out=stv[d2][:, :], in_=banks[bk][:, :]), reads=[bank_b[bk]], writes=[stv_b[d2]])
                    P.dma("sp", cv_p[seq, gt * 128:(gt + 1) * 128, :], stv[d2][:, :], reads=[stv_b[d2]])
                    continue
                cs_ap = csp[:, gt, :]
                if kind == "q":
                    qb_ = qkb[d2]; qbb = qkb_b[d2]
                    P.op("act", lambda e: e.activation(out=qb_[:, :], in_=banks[bk][:, :], func=AF.Copy), reads=[bank_b[bk]], writes=[qbb])
                    rotary(G, banks[bk][:, :], cs_ap, 128, [bank_b[bk]], dst_ap=qb_[:, :], dst_bufs=[qbb])
                else:
                    qb_ = qkb[2 + d2]; qbb = qkb_b[2 + d2]
                    sk = stage[d2]; skb = stage_b[d2]
                    P.op("act", lambda e: e.activation(out=sk[:, :], in_=banks[bk][:, :], func=AF.Copy), reads=[bank_b[bk]], writes=[skb])
                    rotary(G, banks[bk][:, :], cs_ap, 128, [bank_b[bk]], dst_ap=sk[:, :], dst_bufs=[skb])
                    P.op("dve", lambda e: e.tensor_copy(out=qb_[:, :], in_=sk[:, :]), reads=[skb], writes=[qbb])
                    P.dma("sp", ck_p[seq, gt * 128:(gt + 1) * 128, :], sk[:, :], reads=[skb])

                def post(kind=kind, k=k, gt=gt, col0=col0, qb_=qb_, qbb=qbb):
                    tb_ = 6 + (k % 2)
                    bv = bank_bf(tb_)
                    P.pe_group([(lambda e, c=c: e.transpose(out=bv[:, c * 128:(c + 1) * 128], in_=qb_[:, c * 128:(c + 1) * 128],
                                                            identity=ident[:, :])) for c in range(4)],
                               reads=[qbb, const_b], writes=[bank_b[tb_]])
                    src = bv[:, 0:512].rearrange("p (c t) -> p c t", t=128)
                    if kind == "q":
                        P.op("act", lambda e: e.activation(out=QT[:, :, col0:col0 + 128], in_=src, func=AF.Copy, scale=0.125),
                             reads=[bank_b[tb_]], writes=[QT_b])
                    else:
                        P.op("act", lambda e: e.activation(out=KT[:, :, gt * 128:(gt + 1) * 128], in_=src, func=AF.Copy),
                             reads=[bank_b[tb_]], writes=[KT_b])
                posts.append(post)
            for p_ in posts:
                p_()
            if kind == "v":
                for rho in range(4):
                    for t_ in range(4):
                        P.dma("sp", Vp[32 * t_:32 * (t_ + 1), st % 4, rho, :], VH[rho:128:4, (4 * st + t_) % 8, :],
                              reads=[VH_b], writes=[Vp_b[st % 4]])
            hgrn_g1(G, {"i": 0, "q": 1, "k": 2, "v": 3}[kind])

    def hgrn_gates(G, sample):
        n = G["n"]
        rm = rmask_s if sample else rmask_p
        C = 4 if sample else 64
        nch = n // C
        for hh in range(4):
            P.op("act", lambda e: e.activation(out=lf[:, 0:n], in_=th[hh][:, 0:n], func=AF.Ln,
                                               scale=c1[:, hh:hh + 1], bias=c0[:, hh:hh + 1]),
                 reads=[th_b[hh], const_b], writes=[lf_b])
            P.op("dve", lambda e: e.tensor_tensor_scan(out=bb[:, 0:n], data0=rm[:, 0:n], data1=lf[:, 0:n], initial=0.0,
                                                       op0=ALU.mult, op1=ALU.add),
                 reads=[lf_b, const_b], writes=[bb_b])
            P.op("act", lambda e: e.activation(out=eb[:, 0:n], in_=bb[:, 0:n], func=AF.Exp), reads=[bb_b], writes=[eb_b])
            P.op("act", lambda e: e.activation(out=enb[:, 0:n], in_=bb[:, 0:n], func=AF.Exp, scale=-1.0),
                 reads=[bb_b], writes=[enb_b])
            P.op("dve", lambda e: e.scalar_tensor_tensor(out=qdT[hh][:, 0:n], in0=sq[hh][:, 0:n], scalar=128.0 ** -0.5,
                                                         in1=eb[:, 0:n], op0=ALU.mult, op1=ALU.mult),
                 reads=[sq_b[hh], eb_b], writes=[qdT_b[hh]])
            P.op("dve", lambda e: e.tensor_copy(out=ebl[:, hh, 0:nch],
                                                in_=eb[:, 0:n].rearrange("p (c t) -> p c t", t=C)[:, :, C - 1]),
                 reads=[eb_b], writes=[ebl_b[hh]])
            P.op("dve", lambda e: e.tensor_scalar(out=th[hh][:, 0:n], in0=th[hh][:, 0:n], scalar1=nc1[:, hh:hh + 1],
                                                  scalar2=c1[:, hh:hh + 1], op0=ALU.mult, op1=ALU.add),
                 reads=[th_b[hh], const_b], writes=[th_b[hh]])
            P.op("dve", lambda e: e.tensor_tensor(out=kdT[hh][:, 0:n], in0=th[hh][:, 0:n], in1=enb[:, 0:n], op=ALU.mult),
                 reads=[th_b[hh], enb_b], writes=[kdT_b[hh]])
            tb_ = 6 + (hh % 2)
            bv = bank_bf(tb_)
            fns = []
            for k, (col0, ntok) in enumerate(G["stl"]):
                fns.append(lambda e, k=k, col0=col0, ntok=ntok: e.transpose(
                    out=bv[:ntok, k * 128:(k + 1) * 128], in_=kdT[hh][:, col0:col0 + ntok], identity=ident[:, :]))
            P.pe_group(fns, reads=[kdT_b[hh], const_b], writes=[bank_b[tb_]])
            nst = len(G["stl"])
            ntok0 = G["stl"][0][1]
            P.op("act", lambda e: e.activation(out=kdtok[hh][:ntok0, 0:nst, :],
                                               in_=bv[:ntok0, 0:nst * 128].rearrange("p (i k) -> p i k", k=128), func=AF.Copy),
                 reads=[bank_b[tb_]], writes=[kdtok_b[hh]])

    def hgrn_chunks(G, sample, seq_of_chunk=None):
        n = G["n"]
        C = 4 if sample else 64
        nch = n // C
        S_all = S[:, :, :].rearrange("p h v -> p (h v)")
        Sbf_all = S_bf[:, :, :].rearrange("p h v -> p (h v)")

        def base_of(j):
            return 0 if sample else (j % 2) * 64

        def emit_A(j):
            cols = slice(j * C, (j + 1) * C)
            base = base_of(j)
            P.pe_group([mm(banks[4][base:base + C, hh * 64:hh * 64 + C], kdT[hh][:, cols], qdT[hh][:, cols], True, True)
                        for hh in range(4)], reads=kdT_b + qdT_b, writes=[bank_b[4]])
            P.op("dve", lambda e: e.tensor_tensor(
                out=AmA[base:base + C, :].rearrange("p (h t) -> p h t", t=64)[:, :, 0:C],
                in0=banks[4][base:base + C, 0:256].rearrange("p (h t) -> p h t", t=64)[:, :, 0:C],
                in1=causal[base:base + C, 0:C].unsqueeze(1).to_broadcast([C, 4, C]), op=ALU.mult),
                reads=[bank_b[4], const_b], writes=[AmA_b])

        if not sample:
            emit_A(0)
        for j in range(nch):
            cols = slice(j * C, (j + 1) * C)
            base = base_of(j)
            if sample:
                k = j
                P.dma("sp", S[:, :, :], st_in[j].rearrange("h k v -> k h v"), writes=S_b)
                P.op("act", lambda e: e.activation(out=Sbf_all, in_=S_all, func=AF.Copy), reads=S_b, writes=Sbf_b)
                emit_A(j)
            else:
                k = j // 2
            db = 6 + (j % 2)
            fns = []
            for hh in range(4):
                vtok = ia_tok[base:base + C, k, hh * 128:(hh + 1) * 128]
                fns.append(mm(banks[hh][:, cols], S_bf[:, hh, :], qdT[hh][:, cols], True, False))
                fns.append(mm(banks[hh][:, cols], vtok, AmA[base:base + C, hh * 64:hh * 64 + C], False, True))
                fns.append(mm(banks[db][:, hh * 128:(hh + 1) * 128], kdtok[hh][base:base + C, k, :], vtok, True, True))
            P.pe_group(fns, reads=Sbf_b + qdT_b + kdtok_b + [ia_b, AmA_b], writes=[bank_b[0], bank_b[1], bank_b[2], bank_b[3], bank_b[db]])
            if (not sample) and j + 1 < nch:
                emit_A(j + 1)
            P.op("dve", lambda e: e.tensor_tensor(out=StA[:, :], in0=banks[db][:, :], in1=S_all, op=ALU.add),
                 reads=[bank_b[db]] + S_b, writes=[StA_b])
            P.op("dve", lambda e: e.tensor_tensor(out=S[:, :, :], in0=StA[:, :].rearrange("p (h v) -> p h v", v=128),
                                                  in1=ebl[:, :, j:j + 1].to_broadcast([128, 4, 128]), op=ALU.mult),
                 reads=[StA_b] + ebl_b, writes=S_b)
            P.op("act", lambda e: e.activation(out=Sbf_all, in_=S_all, func=AF.Copy), reads=S_b, writes=Sbf_b)
            if sample:
                P.dma("sp", st_s[j].rearrange("h k v -> k h v"), S[:, :, :], reads=S_b)

    def hgrn_out(G):
        n = G["n"]
        for hh in range(4):
            P.op("act", lambda e: e.activation(out=osq[:, 0:n], in_=banks[hh][:, 0:n], func=AF.Square),
                 reads=[bank_b[hh]], writes=[osq_b])
            nb = 4 + (hh % 2)
            P.pe_group([mm(banks[nb][:, 0:n], ones[:, :], osq[:, 0:n], True, True)], reads=[osq_b, const_b], writes=[bank_b[nb]])
            P.op("act", lambda e: e.activation(out=lnv[:, 0:n], in_=banks[nb][:, 0:n], func=AF.Ln, scale=1.0 / 128, bias=EPS),
                 reads=[bank_b[nb]], writes=[lnv_b])
            P.op("act", lambda e: e.activation(out=rsv[:, 0:n], in_=lnv[:, 0:n], func=AF.Exp, scale=-0.5),
                 reads=[lnv_b], writes=[rsv_b])
            P.op("dve", lambda e: e.tensor_tensor(out=t1[:, 0:n], in0=banks[hh][:, 0:n], in1=rsv[:, 0:n], op=ALU.mult),
                 reads=[bank_b[hh], rsv_b], writes=[t1_b])
            P.op("dve", lambda e: e.scalar_tensor_tensor(out=mixT[:, hh, 0:n], in0=t1[:, 0:n], scalar=gon[:, hh:hh + 1],
                                                         in1=gs[hh][:, 0:n], op0=ALU.mult, op1=ALU.mult),
                 reads=[t1_b, gs_b[hh], const_b], writes=[mixT_b])

    att_i = [0]

    def attn_items(items, sbase=4, side=None, nslot=2, pre=0):
        st_ = []

        def qk(it):
            i2 = att_i[0] % nslot
            att_i[0] += 1
            b0 = sbase + 2 * i2
            sc2 = psum_all[:, b0 * 512:(b0 + 2) * 512].rearrange("p (e n) -> p e n", n=512)
            nk, N = it["nk"], it["N"]
            P.pe_group([mm(sc2[:nk, e2, 0:N], it["kt_ap"][64 * e2:64 * e2 + 64, :], it["q_ap"][64 * e2:64 * e2 + 64, :], True, True)
                        for e2 in range(2)], reads=it["rd"] + [QT_b], writes=[bank_b[b0], bank_b[b0 + 1]])
            P.op("act", lambda e: e.activation(out=Pexp[i2][:nk, :, 0:N], in_=sc2[:nk, :, 0:N], func=AF.Exp),
                 reads=[bank_b[b0], bank_b[b0 + 1]], writes=[Pexp_b[i2]])
            nq = it["nq"]
            if it["m3"]:
                nj = N // nq
                P.op("dve", lambda e: e.tensor_tensor(out=Pm[i2][:nk, :, 0:N].rearrange("p e (b q) -> p e b q", q=nq),
                                                      in0=Pexp[i2][:nk, :, 0:N].rearrange("p e (b q) -> p e b q", q=nq),
                                                      in1=it["m_ap"].unsqueeze(1).to_broadcast([nk, 2, nj, nq]), op=ALU.mult),
                     reads=[Pexp_b[i2], const_b], writes=[Pm_b[i2]])
            else:
                P.op("dve", lambda e: e.tensor_tensor(out=Pm[i2][:nk, :, 0:N], in0=Pexp[i2][:nk, :, 0:N],
                                                      in1=it["m_ap"].unsqueeze(1).to_broadcast([nk, 2, N]), op=ALU.mult),
                     reads=[Pexp_b[i2], const_b], writes=[Pm_b[i2]])
            return i2

        def pv(it, i2):
            nk, N = it["nk"], it["N"]
            ob, lb_ = it["obank"], it["lbank"]
            oc = it["ocols"]
            f_, l_ = it["first"], it["last"]
            P.pe_group([mm(banks[ob][0:64, oc], it["v_aps"][0], Pm[i2][:nk, 0, 0:N], f_, l_),
                        mm(banks[ob][64:128, oc], it["v_aps"][1], Pm[i2][:nk, 1, 0:N], f_, l_),
                        mm(banks[lb_][0:64, oc], ones[:nk, 0:64], Pm[i2][:nk, 0, 0:N], f_, l_),
                        mm(banks[lb_][64:128, oc], ones[:nk, 0:64], Pm[i2][:nk, 1, 0:N], f_, l_)],
                       reads=it["rd"] + [Pm_b[i2], const_b], writes=[bank_b[ob], bank_b[lb_]])
            if it["fin"] is not None:
                it["fin"]()

        side = list(side or [])
        for _ in range(min(pre, len(side))):
            side.pop(0)(sbase + 2 * (nslot - 1))
        every = max(1, len(items) // (len(side) + 1)) if side else 0
        npv = [0]

        def pv_side(a, b):
            pv(a, b)
            npv[0] += 1
            if side and npv[0] % every == 0:
                side.pop(0)(sbase + 2 * b)

        for idx, it in enumerate(items):
            st_.append((it, qk(it)))
            if len(st_) > nslot - 1:
                a, b = st_.pop(0)
                pv_side(a, b)
        while st_:
            a, b = st_.pop(0)
            pv_side(a, b)
        while side:
            side.pop(0)(sbase)

    def attn_finish(pr, cols, ocols_mix, obank, lbank):
        n_ = cols.stop - cols.start
        P.op("act", lambda e: e.activation(out=lnL[:, 0:n_], in_=banks[lbank][:, cols], func=AF.Ln), reads=[bank_b[lbank]], writes=[lnL_b])
        P.op("act", lambda e: e.activation(out=rL[:, 0:n_], in_=lnL[:, 0:n_], func=AF.Exp, scale=-1.0), reads=[lnL_b], writes=[rL_b])
        P.op("dve", lambda e: e.tensor_tensor(out=mixT[:, 4 + pr, ocols_mix], in0=banks[obank][:, cols], in1=rL[:, 0:n_], op=ALU.mult),
             reads=[bank_b[obank], rL_b], writes=[mixT_b])

    def phase_attn_prompt(G, st, side):
        gt0 = st * NT
        items = []
        for pr in range(4):
            obank = 0; lbank = 1
            its = []
            kts = [gt0] + [kt for kt in range(max(0, gt0 - 4), gt0 + NT) if kt != gt0]
            for kt in kts:
                j_lo = max(0, kt - gt0); j_hi = NT - 1
                nj = j_hi - j_lo + 1
                off_lo = gt0 + j_lo - kt
                cols = slice(j_lo * 128, (j_hi + 1) * 128)
                its.append(dict(kt_ap=KT[:, pr, kt * 128:(kt + 1) * 128], q_ap=QT[:, pr, cols],
                                v_aps=[VH[:, kt % 8, (2 * pr) * 64:(2 * pr + 1) * 64], VH[:, kt % 8, (2 * pr + 1) * 64:(2 * pr + 2) * 64]],
                                m_ap=maskr[:, off_lo:off_lo + nj, :], m3=True, nq=128, nk=128, N=nj * 128,
                                obank=obank, lbank=lbank, ocols=cols, rd=[KT_b, VH_b], fin=None))
            for G_ in (st - 2, st - 3, st - 4):
                if G_ < 0:
                    continue
                for rho in range(4):
                    its.append(dict(kt_ap=KT[:, pr, 512 * G_:512 * (G_ + 1)].rearrange("p (i f) -> p f i", f=4)[:, rho, :],
                                    q_ap=QT[:, pr, 0:T].rearrange("p (i f) -> p f i", f=4)[:, rho, :],
                                    v_aps=[Vp[:, G_ % 4, rho, (2 * pr) * 64:(2 * pr + 1) * 64],
                                           Vp[:, G_ % 4, rho, (2 * pr + 1) * 64:(2 * pr + 2) * 64]],
                                    m_ap=maskr[:, 19 if G_ == st - 4 else 18, :], m3=False, nq=128, nk=128, N=128,
                                    obank=obank, lbank=lbank, ocols=slice(rho, T, 4), rd=[KT_b, Vp_b[G_ % 4]], fin=None))
            for ii, it in enumerate(its):
                it["first"] = (ii == 0)
                it["last"] = (ii == len(its) - 1)
            its[-1]["fin"] = (lambda pr=pr, obank=obank, lbank=lbank: attn_finish(pr, slice(0, T), slice(0, T), obank, lbank))
            items += its
        attn_items(items, sbase=2, side=side, nslot=3, pre=2)

    def hgrn_g1(G, hh):
        n = G["n"]
        rm = rmask_p
        C = 64
        nch = n // C
        P.op("act", lambda e: e.activation(out=lf[:, 0:n], in_=th[hh][:, 0:n], func=AF.Ln,
                                           scale=c1[:, hh:hh + 1], bias=c0[:, hh:hh + 1]),
             reads=[th_b[hh], const_b], writes=[lf_b])
        P.op("dve", lambda e: e.tensor_tensor_scan(out=bb[:, 0:n], data0=rm[:, 0:n], data1=lf[:, 0:n], initial=0.0,
                                                   op0=ALU.mult, op1=ALU.add),
             reads=[lf_b, const_b], writes=[bb_b])
        P.op("act", lambda e: e.activation(out=eb[:, 0:n], in_=bb[:, 0:n], func=AF.Exp), reads=[bb_b], writes=[eb_b])
        P.op("act", lambda e: e.activation(out=enb[:, 0:n], in_=bb[:, 0:n], func=AF.Exp, scale=-1.0),
             reads=[bb_b], writes=[enb_b])
        P.op("dve", lambda e: e.scalar_tensor_tensor(out=qdT[hh][:, 0:n], in0=sq[hh][:, 0:n], scalar=128.0 ** -0.5,
                                                     in1=eb[:, 0:n], op0=ALU.mult, op1=ALU.mult),
             reads=[sq_b[hh], eb_b], writes=[qdT_b[hh]])
        P.op("dve", lambda e: e.tensor_copy(out=ebl[:, hh, 0:nch],
                                            in_=eb[:, 0:n].rearrange("p (c t) -> p c t", t=C)[:, :, C - 1]),
             reads=[eb_b], writes=[ebl_b[hh]])
        P.op("dve", lambda e: e.tensor_scalar(out=th[hh][:, 0:n], in0=th[hh][:, 0:n], scalar1=nc1[:, hh:hh + 1],
                                              scalar2=c1[:, hh:hh + 1], op0=ALU.mult, op1=ALU.add),
             reads=[th_b[hh], const_b], writes=[th_b[hh]])
        P.op("dve", lambda e: e.tensor_tensor(out=kdT[hh][:, 0:n], in0=th[hh][:, 0:n], in1=enb[:, 0:n], op=ALU.mult),
             reads=[th_b[hh], enb_b], writes=[kdT_b[hh]])

    def hgrn_g2(G):
        for hh in range(4):
            tb_ = 6 + (hh % 2)
            bv = bank_bf(tb_)
            P.pe_group([(lambda e, k=k, col0=col0: e.transpose(out=bv[:, k * 128:(k + 1) * 128], in_=kdT[hh][:, col0:col0 + 128],
                                                               identity=ident[:, :])) for k, (col0, ntok) in enumerate(G["stl"])],
                       reads=[kdT_b[hh], const_b], writes=[bank_b[tb_]])
            P.op("act", lambda e: e.activation(out=kdtok[hh][:, 0:NT, :],
                                               in_=bv[:, 0:NT * 128].rearrange("p (i k) -> p i k", k=128), func=AF.Copy),
                 reads=[bank_b[tb_]], writes=[kdtok_b[hh]])

    def hgrn_side_steps(G):
        n = G["n"]
        C = 64
        nch = n // C
        S_all = S[:, :, :].rearrange("p h v -> p (h v)")
        Sbf_all = S_bf[:, :, :].rearrange("p h v -> p (h v)")

        def emit_A(j, ba):
            cols = slice(j * C, (j + 1) * C)
            base = (j % 2) * 64
            P.pe_group([mm(banks[ba][base:base + C, 256 + hh * 64:256 + hh * 64 + C], kdT[hh][:, cols], qdT[hh][:, cols], True, True)
                        for hh in range(4)], reads=kdT_b + qdT_b, writes=[bank_b[ba]])
            P.op("dve", lambda e: e.tensor_tensor(
                out=AmA[base:base + C, :].rearrange("p (h t) -> p h t", t=64),
                in0=banks[ba][base:base + C, 256:512].rearrange("p (h t) -> p h t", t=64),
                in1=causal[base:base + C, 0:C].unsqueeze(1).to_broadcast([C, 4, C]), op=ALU.mult),
                reads=[bank_b[ba], const_b], writes=[AmA_b])

        def chunk(j, ba):
            if j == 0:
                emit_A(0, ba)
            cols = slice(j * C, (j + 1) * C)
            k = j // 2
            base = (j % 2) * 64
            oc = 0
            fns = []
            for hh in range(4):
                vtok = ia_tok[base:base + C, k, hh * 128:(hh + 1) * 128]
                fns.append(mm(banks[ba][:, oc + hh * 64:oc + hh * 64 + C], S_bf[:, hh, :], qdT[hh][:, cols], True, False))
                fns.append(mm(banks[ba][:, oc + hh * 64:oc + hh * 64 + C], vtok, AmA[base:base + C, hh * 64:hh * 64 + C], False, True))
                fns.append(mm(banks[ba + 1][:, hh * 128:(hh + 1) * 128], kdtok[hh][base:base + C, k, :], vtok, True, True))
            P.pe_group(fns, reads=Sbf_b + qdT_b + kdtok_b + [ia_b, AmA_b], writes=[bank_b[ba], bank_b[ba + 1]])
            if j + 1 < nch:
                emit_A(j + 1, ba)
            P.op("act", lambda e: e.activation(out=oa_sb[:, :, cols],
                                               in_=banks[ba][:, oc:oc + 256].rearrange("p (h t) -> p h t", t=64), func=AF.Copy),
                 reads=[bank_b[ba]], writes=[oa_b])
            P.op("dve", lambda e: e.tensor_tensor(out=StA[:, :], in0=banks[ba + 1][:, :], in1=S_all, op=ALU.add),
                 reads=[bank_b[ba + 1]] + S_b, writes=[StA_b])
            P.op("dve", lambda e: e.tensor_tensor(out=S[:, :, :], in0=StA[:, :].rearrange("p (h v) -> p h v", v=128),
                                                  in1=ebl[:, :, j:j + 1].to_broadcast([128, 4, 128]), op=ALU.mult),
                 reads=[StA_b] + ebl_b, writes=S_b)
            P.op("dve", lambda e: e.tensor_tensor(out=S_bf[:, :, :], in0=StA[:, :].rearrange("p (h v) -> p h v", v=128),
                                                  in1=ebl[:, :, j:j + 1].to_broadcast([128, 4, 128]), op=ALU.mult),
                 reads=[StA_b] + ebl_b, writes=Sbf_b)

        def hsq(hh, ba):
            P.op("dve", lambda e: e.tensor_tensor(out=qdT[hh][:, 0:n], in0=oa_sb[:, hh, 0:n], in1=oa_sb[:, hh, 0:n], op=ALU.mult),
                 reads=[oa_b], writes=[qdT_b[hh]])

        def hout(hh, ba):
            osq = qdT[hh]
            osq_b = qdT_b[hh]
            P.pe_group([mm(banks[ba][:, 0:n], ones[:, :], osq[:, 0:n], True, True)], reads=[osq_b, const_b], writes=[bank_b[ba]])
            P.op("act", lambda e: e.activation(out=lnv[:, 0:n], in_=banks[ba][:, 0:n], func=AF.Ln, scale=1.0 / 128, bias=EPS),
                 reads=[bank_b[ba]], writes=[lnv_b])
            P.op("act", lambda e: e.activation(out=rsv[:, 0:n], in_=lnv[:, 0:n], func=AF.Exp, scale=-0.5),
                 reads=[lnv_b], writes=[rsv_b])
            P.op("dve", lambda e: e.tensor_tensor(out=t1[:, 0:n], in0=oa_sb[:, hh, 0:n], in1=rsv[:, 0:n], op=ALU.mult),
                 reads=[oa_b, rsv_b], writes=[t1_b])
            P.op("dve", lambda e: e.scalar_tensor_tensor(out=mixT[:, hh, 0:n], in0=t1[:, 0:n], scalar=gon[:, hh:hh + 1],
                                                         in1=gs[hh][:, 0:n], op0=ALU.mult, op1=ALU.mult),
                 reads=[t1_b, gs_b[hh], const_b], writes=[mixT_b])

        steps = [(lambda ba, j=j: chunk(j, ba)) for j in range(nch)] + [(lambda ba, hh=hh: hsq(hh, ba)) for hh in range(4)] + \
            [(lambda ba, hh=hh: hout(hh, ba)) for hh in range(4)]
        return steps

    def phase_wo(G):
        for half in range(2):
            sl, slb = get_w(("wo", half))
            for k, (col0, ntok) in enumerate(G["dt"]):
                bk = (half * len(G["dt"]) + k) % 4
                P.pe_group([mm(banks[bk][:ntok, :], mixT[:, kc, col0:col0 + ntok], sl[:, kc, :], kc == 0, kc == 7)
                            for kc in range(8)], reads=[slb, mixT_b], writes=[bank_b[bk]])
                hv = h[:ntok, k, half * 512:(half + 1) * 512]
                P.op("dve", lambda e: e.tensor_tensor(out=hv, in0=banks[bk][:ntok, :], in1=hv, op=ALU.add),
                     reads=[bank_b[bk], h_b[k]], writes=[h_b[k]])

    def phase_ffn(G, sample, seq, last_st, ptok0=None):
        n = G["n"]
        nsq = NSEQ_S if sample else 1
        L = n // nsq
        ui = 0
        for g0 in range(0, NFF, 4):
            ncc = min(4, NFF - g0)
            if ptok0 is not None and g0 == 0:
                p_load(G, False, ptok0)
            if ptok0 is not None and g0 == 12:
                p_transposes(G, False, ptok0)
            slg, slgb = get_w(("gate", g0))
            slu, slub = get_w(("up", g0), oldest=ws["last"])
            for cc in range(ncc):
                c = g0 + cc
                bg = (2 * c) % 8; bu = bg + 1
                P.pe_group([mm(banks[bg][:, 0:n], slg[:, kc, cc * 128:(cc + 1) * 128], aT[:, kc, 0:n], kc == 0, kc == 7)
                            for kc in range(8)], reads=[slgb, aT_b], writes=[bank_b[bg]])
                P.pe_group([mm(banks[bu][:, 0:n], slu[:, kc, cc * 128:(cc + 1) * 128], aT[:, kc, 0:n], kc == 0, kc == 7)
                            for kc in range(8)], reads=[slub, aT_b], writes=[bank_b[bu]])
                u = uext[ui % 2]; ub = uext_b[ui % 2]
                cv_ = ccv[ui % 2]; cb_ = ccv_b[ui % 2]
                sv = scv[ui % 2]; svb = scv_b[ui % 2]
                ui += 1
                u3 = u[:, 0:nsq * (L + 2)].rearrange("p (s l) -> p s l", l=L + 2)
                c3 = cv_[:, 0:n].rearrange("p (s l) -> p s l", l=L)
                ps3 = banks[bg][:, 0:n].rearrange("p (s l) -> p s l", l=L)
                P.op("act", lambda e: e.activation(out=u3[:, :, 2:L + 2], in_=ps3, func=AF.Copy), reads=[bank_b[bg]], writes=[ub])
                P.op("act", lambda e: e.activation(out=cv_[:, 0:n], in_=banks[bg][:, 0:n], func=AF.Identity,
                                                   scale=convw[:, c, 2:3], bias=convb[:, c:c + 1]),
                     reads=[bank_b[bg], const_b], writes=[cb_])
                P.op("dve", lambda e: e.tensor_copy(out=u3[:, :, 0:2], in_=convhist[:, c, 0:nsq, :]),
                     reads=[convhist_b[c]], writes=[ub])
                P.op("dve", lambda e: e.tensor_copy(out=convhist[:, c, 0:nsq, :], in_=u3[:, :, L:L + 2]),
                     reads=[ub], writes=[convhist_b[c]])
                P.op("dve", lambda e: e.scalar_tensor_tensor(out=c3, in0=u3[:, :, 1:L + 1], scalar=convw[:, c, 1:2], in1=c3,
                                                             op0=ALU.mult, op1=ALU.add),
                     reads=[ub, cb_, const_b], writes=[cb_])
                P.op("dve", lambda e: e.scalar_tensor_tensor(out=c3, in0=u3[:, :, 0:L], scalar=convw[:, c, 0:1], in1=c3,
                                                             op0=ALU.mult, op1=ALU.add),
                     reads=[ub, cb_, const_b], writes=[cb_])
                P.op("act", lambda e: e.activation(out=sv[:, 0:n], in_=cv_[:, 0:n], func=AF.Silu), reads=[cb_], writes=[svb])
                P.op("dve", lambda e: e.tensor_tensor(out=gT[:, c, 0:n], in0=sv[:, 0:n], in1=banks[bu][:, 0:n], op=ALU.mult),
                     reads=[svb, bank_b[bu]], writes=[gT_b])
                if sample:
                    for s_ in range(NSEQ_S):
                        P.dma("sp", conv_s[s_, :, c, :], convhist[:, c, s_, :], reads=[convhist_b[c]])
                elif last_st:
                    P.dma("sp", conv_p[seq, :, c, :], convhist[:, c, 0, :], reads=[convhist_b[c]])
        ndt = len(G["dt"])
        for half in range(2):
            accb = [half * 4 + k for k in range(ndt)]
            for g0 in range(0, NFF, 8):
                ncc = min(8, NFF - g0)
                sld, sldb = get_w(("down", half, g0))
                fns = []
                for cc in range(ncc):
                    c = g0 + cc
                    for k, (col0, ntok) in enumerate(G["dt"]):
                        fns.append(mm(banks[accb[k]][:ntok, :], gT[:, c, col0:col0 + ntok], sld[:, cc, :], c == 0, c == NFF - 1))
                P.pe_group(fns, reads=[sldb, gT_b], writes=[bank_b[b] for b in accb])
            for k, (col0, ntok) in enumerate(G["dt"]):
                hv = h[:ntok, k, half * 512:(half + 1) * 512]
                P.op("dve", lambda e: e.tensor_tensor(out=hv, in0=banks[accb[k]][:ntok, :], in1=hv, op=ALU.add),
                     reads=[bank_b[accb[k]], h_b[k]], writes=[h_b[k]])

    def p_load(G, sample, tok0):
        P.dma("pool", wpp_sb[:, 0:2, :], w_pp[:, 0:512].rearrange("(kc p) n -> p kc n", p=128), writes=[wpp_b])
        P.dma("pool", wpp_sb[:, 2:4, :], w_pp[:, 512:1024].rearrange("(kc p) n -> p kc n", p=128), writes=[wpp_b])
        for k, (col0, ntok) in enumerate(G["dt"]):
            src = (p_s if sample else p_p)[tok0 + col0: tok0 + col0 + ntok, :]
            P.dma("pool", pb[:ntok, k, :], src, writes=[pb_b])

    def p_transposes(G, sample, tok0):
        for k, (col0, ntok) in enumerate(G["dt"]):
            tb_ = 6 + (k % 2)
            bv = bank_bf(tb_)
            P.pe_group([(lambda e, c=c: e.transpose(out=bv[:, c * 128:c * 128 + ntok], in_=pb[:ntok, k, c * 128:(c + 1) * 128],
                                                    identity=ident[:ntok, :ntok])) for c in range(2)],
                       reads=[pb_b, const_b], writes=[bank_b[tb_]])
            P.op("act", lambda e: e.activation(out=pT[:, :, col0:col0 + ntok],
                                               in_=bv[:, 0:256].rearrange("p (c t) -> p c t", t=128)[:, :, 0:ntok], func=AF.Copy),
                 reads=[bank_b[tb_]], writes=[pT_b])

    def phase_ple_final_prompt(G, tok0, has_next):
        slp, slpb = wpp_sb, wpp_b
        ti = 0
        u_ = 0
        dt_ = G["dt"]
        nsis = {}
        P.handoff([gT_b], yst_b)
        for half in range(2):
            sl, slb = get_w(("pg", half))
            for k, (col0, ntok) in enumerate(dt_):
                bg = (2 * u_) % 6; bp = bg + 1
                u_ += 1
                P.pe_group([mm(banks[bg][:ntok, :], aT[:, kc, col0:col0 + ntok], sl[:, kc, :], kc == 0, kc == 7)
                            for kc in range(8)], reads=[slb, aT_b], writes=[bank_b[bg]])
                P.pe_group([mm(banks[bp][:ntok, :], pT[:, kc, col0:col0 + ntok], slp[:, 2 * half + kc, :], kc == 0, kc == 1)
                            for kc in range(2)], reads=[slpb, pT_b], writes=[bank_b[bp]])
                tgv = tg[ti % 2]; tgb = tg_b[ti % 2]; t2v = t2[ti % 2]; t2b = t2_b[ti % 2]
                ti += 1
                P.op("act", lambda e: e.activation(out=tgv[:ntok, :], in_=banks[bg][:ntok, :], func=AF.Tanh, scale=0.5),
                     reads=[bank_b[bg]], writes=[tgb])
                P.op("dve", lambda e: e.scalar_tensor_tensor(out=t2v[:ntok, :], in0=tgv[:ntok, :], scalar=1.0, in1=banks[bp][:ntok, :],
                                                             op0=ALU.add, op1=ALU.mult),
                     reads=[tgb, bank_b[bp]], writes=[t2b])
                cs_ = slice(half * 512, (half + 1) * 512)
                P.op("dve", lambda e: e.scalar_tensor_tensor(out=yst[:ntok, k, cs_], in0=t2v[:ntok, :], scalar=0.5, in1=h[:ntok, k, cs_],
                                                             op0=ALU.mult, op1=ALU.add),
                     reads=[t2b, h_b[k]], writes=[yst_b[k]])
                if half == 1 and has_next:
                    nt0 = tok0 + T
                    P.dma("sp", h[:, k, :], x_p[nt0 + k * 128: nt0 + (k + 1) * 128, :], writes=[h_b[k]])
                    if k >= 2:
                        kk = k - 2
                        nsis[kk] = norm_A(kk, 128)
                        norm_B(kk, 128, nsis[kk])
        if has_next:
            norm_C(0, 128)
            norm_C(1, 128)
            for kk in (2, 3):
                nsis[kk] = norm_A(kk, 128)
            norm_D(0, 0, 128, 0)
            norm_B(2, 128, nsis[2])
            norm_C(2, 128)
            norm_D(1, 128, 128, 0)
            norm_B(3, 128, nsis[3])
            norm_C(3, 128)
            norm_D(2, 256, 128, 0)
            norm_D(3, 384, 128, 0)
        for k, (col0, ntok) in enumerate(dt_):
            si = ss_n[0] % 16
            ss_n[0] += 1
            P.op("act", lambda e: e.activation(out=junk[:ntok, :], in_=yst[:ntok, k, :], func=AF.Square,
                                               accum_out=ss[:ntok, si:si + 1]),
                 reads=[yst_b[k]], writes=[junk_b, ss_b[si]])
            P.op("dve", lambda e: e.tensor_scalar(out=ms[:ntok, si:si + 1], in0=ss[:ntok, si:si + 1], scalar1=1.0 / D,
                                                  scalar2=EPS, op0=ALU.mult, op1=ALU.add),
                 reads=[ss_b[si]], writes=[ss_b[si]])
            P.op("pool", lambda e: e.tensor_tensor(out=rstd[:ntok, si:si + 1], in0=ms[:ntok, si:si + 1],
                                                   in1=neghalf[:ntok, 0:1], op=ALU.pow),
                 reads=[ss_b[si], const_b], writes=[ss_b[si]])
            P.op("dve", lambda e: e.scalar_tensor_tensor(out=yst[:ntok, k, :], in0=yst[:ntok, k, :], scalar=rstd[:ntok, si:si + 1],
                                                         in1=gfin[:ntok, :], op0=ALU.mult, op1=ALU.mult),
                 reads=[yst_b[k], ss_b[si], const_b], writes=[yst_b[k]])
            P.dma("sp", y_p[tok0 + col0: tok0 + col0 + ntok, :], yst[:ntok, k, :], reads=[yst_b[k]])

    def phase_ple(G, sample, tok0):
        p_load(G, sample, tok0)
        p_transposes(G, sample, tok0)
        slp, slpb = wpp_sb, wpp_b
        ti = 0
        for half in range(2):
            sl, slb = get_w(("pg", half))
            for k, (col0, ntok) in enumerate(G["dt"]):
                bg = (2 * (half * len(G["dt"]) + k)) % 8; bp = bg + 1
                P.pe_group([mm(banks[bg][:ntok, :], aT[:, kc, col0:col0 + ntok], sl[:, kc, :], kc == 0, kc == 7)
                            for kc in range(8)], reads=[slb, aT_b], writes=[bank_b[bg]])
                P.pe_group([mm(banks[bp][:ntok, :], pT[:, kc, col0:col0 + ntok], slp[:, 2 * half + kc, :], kc == 0, kc == 1)
                            for kc in range(2)], reads=[slpb, pT_b], writes=[bank_b[bp]])
                tgv = tg[ti % 2]; tgb = tg_b[ti % 2]; t2v = t2[ti % 2]; t2b = t2_b[ti % 2]
                ti += 1
                P.op("act", lambda e: e.activation(out=tgv[:ntok, :], in_=banks[bg][:ntok, :], func=AF.Tanh, scale=0.5),
                     reads=[bank_b[bg]], writes=[tgb])
                P.op("dve", lambda e: e.scalar_tensor_tensor(out=t2v[:ntok, :], in0=tgv[:ntok, :], scalar=1.0, in1=banks[bp][:ntok, :],
                                                             op0=ALU.add, op1=ALU.mult),
                     reads=[tgb, bank_b[bp]], writes=[t2b])
                hv = h[:ntok, k, half * 512:(half + 1) * 512]
                P.op("dve", lambda e: e.scalar_tensor_tensor(out=hv, in0=t2v[:ntok, :], scalar=0.5, in1=hv, op0=ALU.mult, op1=ALU.add),
                     reads=[t2b, h_b[k]], writes=[h_b[k]])

    def phase_final(G, sample, tok0):
        for k, (col0, ntok) in enumerate(G["dt"]):
            si = ss_n[0] % 16
            ss_n[0] += 1
            P.op("act", lambda e: e.activation(out=junk[:ntok, :], in_=h[:ntok, k, :], func=AF.Square, accum_out=ss[:ntok, si:si + 1]),
                 reads=[h_b[k]], writes=[junk_b, ss_b[si]])
            P.op("dve", lambda e: e.tensor_scalar(out=ms[:ntok, si:si + 1], in0=ss[:ntok, si:si + 1], scalar1=1.0 / D, scalar2=EPS,
                                                  op0=ALU.mult, op1=ALU.add), reads=[ss_b[si]], writes=[ss_b[si]])
            P.op("pool", lambda e: e.tensor_tensor(out=rstd[:ntok, si:si + 1], in0=ms[:ntok, si:si + 1], in1=neghalf[:ntok, 0:1],
                                                   op=ALU.pow), reads=[ss_b[si], const_b], writes=[ss_b[si]])
            P.op("dve", lambda e: e.scalar_tensor_tensor(out=h[:ntok, k, :], in0=h[:ntok, k, :], scalar=rstd[:ntok, si:si + 1],
                                                         in1=gfin[:ntok, :], op0=ALU.mult, op1=ALU.mult),
                 reads=[h_b[k], ss_b[si], const_b], writes=[h_b[k]])
            dst = (y_s if sample else y_p)[tok0 + col0: tok0 + col0 + ntok, :]
            P.dma("sp", dst, h[:ntok, k, :], reads=[h_b[k]])

    Gp = {"n": T, "dt": [(i * 128, 128) for i in range(NT)], "stl": [(i * 128, 128) for i in range(NT)]}
    for seq in range(NSEQ_P):
        for hh in range(4):
            P.op("dve", lambda e, hh=hh: e.memset(S[:, hh, :], 0.0), writes=[S_b[hh]])
            P.op("dve", lambda e, hh=hh: e.memset(S_bf[:, hh, :], 0.0), writes=[Sbf_b[hh]])
        for c in range(NFF):
            P.op("pool", lambda e, c=c: e.memset(convhist[:, c, :, :], 0.0), writes=[convhist_b[c]])
        for st in range(NST):
            tok0 = seq * SEQ + st * T
            first = (seq == 0 and st == 0)
            last = (seq == NSEQ_P - 1 and st == NST - 1)
            P.mark('norm0')
            if first:
                for k in range(NT):
                    P.dma("sp", h[:, k, :], x_p[tok0 + k * 128: tok0 + (k + 1) * 128, :], writes=[h_b[k]])
                norm_T(Gp, 0)
            P.handoff(regH_B, regH_A)
            P.handoff(regM_B, regM_A)
            P.mark('win')
            phase_win(Gp, False, seq, st)
            P.mark('gates')
            hgrn_g2(Gp)
            P.handoff([lf_b, bb_b, eb_b, enb_b], [oa_b])
            P.mark('attn')
            ci_ = seq * NST + st
            if ci_ < 2 * NSEQ_S:
                src_, dst_ = (ck_in, ck_s) if ci_ % 2 == 0 else (cv_in, cv_s)
                P.dma("sp", dst_[ci_ // 2, 0:WB - TS, :], src_[ci_ // 2, TS:WB, :], ring="spc")
            phase_attn_prompt(Gp, st, hgrn_side_steps(Gp))
            P.handoff([oa_b], [lf_b, bb_b, eb_b, enb_b])
            P.mark('wo')
            phase_wo_norm1(Gp)
            P.mark('norm1')
            P.handoff(regH_A, regH_B)
            P.handoff(regM_A, regM_B)
            P.mark('ffn')
            phase_ffn(Gp, False, seq, st == NST - 1, ptok0=tok0)
            P.mark('norm2')
            norm_T(Gp, 2)
            P.mark('ple')
            phase_ple_final_prompt(Gp, tok0, not last)
            P.mark('final')
        for hh in range(4):
            P.dma("sp", st_p[seq, hh, :, :], S[:, hh, :], reads=[S_b[hh]])

    P.mark('sample')
    Gs = {"n": NSEQ_S * TS, "dt": [(0, NSEQ_S * TS)], "stl": [(s_ * TS, TS) for s_ in range(NSEQ_S)]}
    for c in range(NFF):
        P.dma("sp", convhist[:, c, :, :], conv_in[:, c, :, :], writes=[convhist_b[c]])
    P.dma("sp", h[0:NSEQ_S * TS, 0, :], x_s[:, :], writes=[h_b[0]])
    norm_T(Gs, 0)
    P.handoff(regH_B, regH_A)
    P.handoff(regM_B, regM_A)
    n = Gs["n"]
    for blk, kind in ((1, "f"), (0, "q"), (3, "g")):
        sl, slb = get_w(("win", blk))
        for hh in range(4):
            bk = hh
            P.pe_group([mm(banks[bk][:, 0:n], sl[:, kc, hh * 128:(hh + 1) * 128], aT[:, kc, 0:n], kc == 0, kc == 7)
                        for kc in range(8)], reads=[slb, aT_b], writes=[bank_b[bk]])
            if kind == "f":
                P.op("act", lambda e: e.activation(out=th[hh][:, 0:n], in_=banks[bk][:, 0:n], func=AF.Tanh, scale=0.5),
                     reads=[bank_b[bk]], writes=[th_b[hh]])
            elif kind == "q":
                P.op("act", lambda e: e.activation(out=sq[hh][:, 0:n], in_=banks[bk][:, 0:n], func=AF.Silu),
                     reads=[bank_b[bk]], writes=[sq_b[hh]])
            else:
                P.op("act", lambda e: e.activation(out=gs[hh][:, 0:n], in_=banks[bk][:, 0:n], func=AF.Silu),
                     reads=[bank_b[bk]], writes=[gs_b[hh]])
    sl, slb = get_w(("win", 2))
    for k, (col0, ntok) in enumerate(Gs["stl"]):
        bk = k % 4
        P.pe_group([mm(banks[bk][:ntok, :], aT[:, kc, col0:col0 + ntok], sl[:, kc, :], kc == 0, kc == 7) for kc in range(8)],
                   reads=[slb, aT_b], writes=[bank_b[bk]])
        P.op("act", lambda e: e.activation(out=ia_tok[:ntok, k, :], in_=banks[bk][:ntok, :], func=AF.Copy),
             reads=[bank_b[bk]], writes=[ia_b])
    hgrn_gates(Gs, True)
    hgrn_chunks(Gs, True)
    hgrn_out(Gs)
    slq, slqb = get_w(("win", 4))
    for k, (col0, ntok) in enumerate(Gs["stl"]):
        bk = k % 4
        P.pe_group([mm(banks[bk][:ntok, :], aT[:, kc, col0:col0 + ntok], slq[:, kc, :], kc == 0, kc == 7) for kc in range(8)],
                   reads=[slqb, aT_b], writes=[bank_b[bk]])
        P.op("act", lambda e: e.activation(out=stage[0][:ntok, :], in_=banks[bk][:ntok, :], func=AF.Copy),
             reads=[bank_b[bk]], writes=[stage_b[0]])
        rotary(Gs, stage[0], css, ntok, [stage_b[0]])
        P.op("dve", lambda e: e.tensor_copy(out=qkb[0][:ntok, :], in_=stage[0][:ntok, :]), reads=[stage_b[0]], writes=[qkb_b[0]])
        bv = bank_bf(6)
        P.pe_group([(lambda e, c=c: e.transpose(out=bv[:, c * 128:c * 128 + ntok], in_=qkb[0][:ntok, c * 128:(c + 1) * 128],
                                                identity=ident[:ntok, :ntok])) for c in range(4)],
                   reads=[qkb_b[0], const_b], writes=[bank_b[6]])
        P.op("act", lambda e: e.activation(out=QT[:, :, col0:col0 + ntok],
                                           in_=bv[:, 0:512].rearrange("p (c t) -> p c t", t=128)[:, :, 0:ntok], func=AF.Copy, scale=0.125),
             reads=[bank_b[6]], writes=[QT_b])
    slk, slkb = get_w(("win", 5))
    slv, slvb = get_w(("win", 6), oldest=ws["last"])
    for s_, (col0, ntok) in enumerate(Gs["stl"]):
        P.pe_group([mm(banks[0][:ntok, :], aT[:, kc, col0:col0 + ntok], slk[:, kc, :], kc == 0, kc == 7) for kc in range(8)],
                   reads=[slkb, aT_b], writes=[bank_b[0]])
        P.op("act", lambda e: e.activation(out=stage[1][:ntok, :], in_=banks[0][:ntok, :], func=AF.Copy),
             reads=[bank_b[0]], writes=[stage_b[1]])
        rotary(Gs, stage[1], css, ntok, [stage_b[1]])
        P.dma("sp", ck_s[s_, WB - TS:WB, :], stage[1][:ntok, :], reads=[stage_b[1]])
        P.op("dve", lambda e: e.tensor_copy(out=qkb[1][:ntok, :], in_=stage[1][:ntok, :]), reads=[stage_b[1]], writes=[qkb_b[1]])
        bv = bank_bf(7)
        P.pe_group([(lambda e, c=c: e.transpose(out=bv[:, c * 128:c * 128 + ntok], in_=qkb[1][:ntok, c * 128:(c + 1) * 128],
                                                identity=ident[:ntok, :ntok])) for c in range(4)],
                   reads=[qkb_b[1], const_b], writes=[bank_b[7]])
        P.op("act", lambda e: e.activation(out=KTn[:, :, 0:ntok],
                                           in_=bv[:, 0:512].rearrange("p (c t) -> p c t", t=128)[:, :, 0:ntok], func=AF.Copy),
             reads=[bank_b[7]], writes=[KTn_b])
        P.pe_group([mm(banks[1][:ntok, :], aT[:, kc, col0:col0 + ntok], slv[:, kc, :], kc == 0, kc == 7) for kc in range(8)],
                   reads=[slvb, aT_b], writes=[bank_b[1]])
        P.op("act", lambda e: e.activation(out=stage[2][:ntok, :], in_=banks[1][:ntok, :], func=AF.Copy),
             reads=[bank_b[1]], writes=[stage_b[2]])
        P.dma("sp", cv_s[s_, WB - TS:WB, :], stage[2][:ntok, :], reads=[stage_b[2]])
        P.op("dve", lambda e: e.tensor_copy(out=Vn[:ntok, :], in_=stage[2][:ntok, :]), reads=[stage_b[2]], writes=[Vn_b])
        if s_ == 0:
            sKT_b = [Buf("sKT%d" % i) for i in range(7)]; sVH_b = [Buf("sVH%d" % i) for i in range(7)]
            P.handoff([KT_b], sKT_b)
            P.handoff([VH_b], sVH_b)
        for sl_ in range(7):
            ci = sl_ % 2
            if sl_ < 3:
                for r_ in range(4):
                    srck = ck_in[s_, 512 * sl_:512 * (sl_ + 1), :].rearrange("(m r) c -> r m c", r=16)[r_]
                    srcv = cv_in[s_, 512 * sl_:512 * (sl_ + 1), :].rearrange("(m r) c -> r m c", r=16)[r_]
                    P.dma("pool", cbf[ci][32 * r_:32 * (r_ + 1), :], srck, writes=[cbf_b[ci]])
                    P.dma("pool", VH[32 * r_:32 * (r_ + 1), sl_, :], srcv, writes=[sVH_b[sl_]])
            else:
                kt = 9 + sl_
                P.dma("pool", cbf[ci][:, :], ck_in[s_, kt * 128:(kt + 1) * 128, :], writes=[cbf_b[ci]])
                P.dma("pool", VH[:, sl_, :], cv_in[s_, kt * 128:(kt + 1) * 128, :], writes=[sVH_b[sl_]])
            tb_ = 6 + (sl_ % 2)
            bv = bank_bf(tb_)
            P.pe_group([(lambda e, c=c: e.transpose(out=bv[:, c * 128:(c + 1) * 128], in_=cbf[ci][:, c * 128:(c + 1) * 128],
                                                    identity=ident[:, :])) for c in range(4)],
                       reads=[cbf_b[ci], const_b], writes=[bank_b[tb_]])
            P.op("act", lambda e: e.activation(out=KT[:, :, sl_ * 128:(sl_ + 1) * 128],
                                               in_=bv[:, 0:512].rearrange("p (c t) -> p c t", t=128), func=AF.Copy),
                 reads=[bank_b[tb_]], writes=[sKT_b[sl_]])
        qc = slice(col0, col0 + TS)
        sitems = []
        for sl_ in range(7):
            mi = 17 if sl_ < 3 else 16 - (9 + sl_)
            sitems.append((lambda pr, e2, sl_=sl_: KT[64 * e2:64 * e2 + 64, pr, sl_ * 128:(sl_ + 1) * 128],
                           lambda hd, sl_=sl_: VH[:, sl_, hd * 64:(hd + 1) * 64],
                           maskr[:, mi, 0:TS], 128, [sKT_b[sl_], sVH_b[sl_]]))
        sitems.append((lambda pr, e2: KTn[64 * e2:64 * e2 + 64, pr, 0:TS],
                       lambda hd: Vn[0:TS, hd * 64:(hd + 1) * 64],
                       maskr[0:TS, 0, 0:TS], TS, [KTn_b, Vn_b]))
        pend = []

        def s_qk(idx, it):
            ktf, vf, m_ap, nk, rd = it
            i2 = idx % 2
            sbk = 2 + 3 * i2
            sc2 = psum_all[:, sbk * 512:(sbk + 2) * 512].rearrange("p (e n) -> p e n", n=512)
            P.pe_group([mm(sc2[:nk, e2, pr * TS:(pr + 1) * TS], ktf(pr, e2), QT[64 * e2:64 * e2 + 64, pr, qc], True, True)
                        for pr in range(4) for e2 in range(2)], reads=rd + [QT_b], writes=[bank_b[sbk], bank_b[sbk + 1]])
            P.op("act", lambda e: e.activation(out=Pexp[i2][:nk, :, 0:16], in_=sc2[:nk, :, 0:16], func=AF.Exp),
                 reads=[bank_b[sbk], bank_b[sbk + 1]], writes=[Pexp_b[i2]])
            P.op("dve", lambda e: e.tensor_tensor(out=Pm[i2][:nk, :, 0:16].rearrange("p e (a t) -> p e a t", t=TS),
                                                  in0=Pexp[i2][:nk, :, 0:16].rearrange("p e (a t) -> p e a t", t=TS),
                                                  in1=m_ap.unsqueeze(1).unsqueeze(1).to_broadcast([nk, 2, 4, TS]), op=ALU.mult),
                 reads=[Pexp_b[i2], const_b], writes=[Pm_b[i2]])

        def s_pv(idx, it):
            ktf, vf, m_ap, nk, rd = it
            i2 = idx % 2
            obk = 4 + 3 * i2
            fns = []
            for pr in range(4):
                for e2 in range(2):
                    hd = 2 * pr + e2
                    rhs = Pm[i2][:nk, e2, pr * TS:(pr + 1) * TS]
                    fns.append(mm(banks[obk][64 * e2:64 * e2 + 64, pr * TS:(pr + 1) * TS], vf(hd), rhs, True, True))
                    fns.append(mm(banks[obk][64 * e2:64 * e2 + 64, 16 + pr * TS:16 + (pr + 1) * TS], ones[:nk, 0:64], rhs, True, True))
            P.pe_group(fns, reads=rd + [Pm_b[i2], const_b], writes=[bank_b[obk]])
            if idx == 0:
                P.op("dve", lambda e: e.tensor_copy(out=sacc[:, :], in_=banks[obk][:, 0:32]), reads=[bank_b[obk]], writes=[sacc_b])
            else:
                P.op("dve", lambda e: e.tensor_tensor(out=sacc[:, :], in0=banks[obk][:, 0:32], in1=sacc[:, :], op=ALU.add),
                     reads=[bank_b[obk], sacc_b], writes=[sacc_b])

        for idx, it in enumerate(sitems):
            s_qk(idx, it)
            pend.append((idx, it))
            if len(pend) > 1:
                s_pv(*pend.pop(0))
        while pend:
            s_pv(*pend.pop(0))
        P.op("dve", lambda e: e.reciprocal(out=sacc[:, 16:32], in_=sacc[:, 16:32]), reads=[sacc_b], writes=[sacc_b])
        P.op("dve", lambda e: e.tensor_tensor(out=mixT[:, 4:8, qc], in0=sacc[:, 0:16].rearrange("p (a t) -> p a t", t=TS),
                                              in1=sacc[:, 16:32].rearrange("p (a t) -> p a t", t=TS), op=ALU.mult),
             reads=[sacc_b], writes=[mixT_b])
    phase_wo(Gs)
    norm_T(Gs, 1)
    P.handoff(regH_A, regH_B)
    P.handoff(regM_A, regM_B)
    phase_ffn(Gs, True, 0, True)
    norm_T(Gs, 2)
    phase_ple(Gs, True, 0)
    phase_final(Gs, True, 0)

    P.mark('end')
    P.final_wait()
    return nc, P


_CACHE = {}


def _consts():
    maskr = np.zeros((20, 128, 128), np.float32)
    kk = np.arange(128)[:, None]
    qq = np.arange(128)[None, :]
    for off in range(17):
        diff = 128 * off + qq - kk
        m = np.zeros((128, 128), np.float32)
        for dil in (1, 4, 16):
            m += ((diff >= 0) & (diff % dil == 0) & (diff <= 128 * dil)).astype(np.float32)
        maskr[off] = m
    for t_ in range(TS):
        maskr[17, 32 * t_:32 * (t_ + 1), t_] = 1.0
    ii = np.arange(128)[:, None]; jj = np.arange(128)[None, :]
    maskr[18] = ((ii - jj) % 4 == 0).astype(np.float32)
    maskr[19] = (((ii - jj) % 4 == 0) & (jj <= ii)).astype(np.float32)
    maskr = np.ascontiguousarray(maskr.transpose(1, 0, 2))
    s = np.arange(64)[:, None]
    t = np.arange(64)[None, :]
    cz = (s <= t).astype(np.float32)
    causal = np.concatenate([cz, cz], axis=0)
    half = 8
    inv_freq = np.power(np.float32(500000.0), -np.arange(half, dtype=np.float32) * np.float32(2.0 / 16)).astype(np.float32)

    def cs(pos):
        ang = (pos.astype(np.float32)[:, None] * inv_freq[None, :]).astype(np.float32)
        return np.concatenate([np.cos(ang), np.sin(ang)], axis=1).astype(np.float32)

    csp = cs(np.arange(SEQ)).reshape(SEQ // 128, 128, 16).transpose(1, 0, 2)
    css = cs(PAST + np.arange(TS))
    return maskr, causal, np.ascontiguousarray(csp), np.ascontiguousarray(css)


def kernel(x_prompt, x_sample, state_hgrn, cache_win_k, cache_win_v, state_ffn_conv, p_prompt, p_sample,
           norm_attn_g, w_in, hgrn_lb_logits, hgrn_onorm_g, w_o, norm_ffn_g, w_gate, w_up, conv_w, conv_b,
           w_down, norm_ple_g, w_ple_gate, w_ple_proj, norm_final_g):
    f = lambda a: np.ascontiguousarray(np.asarray(a, dtype=np.float32))
    if "nc" not in _CACHE:
        _CACHE["nc"] = build_program()[0]
        _CACHE["consts"] = _consts()
    nc = _CACHE["nc"]
    maskr, causal, csp, css = _CACHE["consts"]
    x_prompt = f(x_prompt); x_sample = f(x_sample); p_prompt = f(p_prompt); p_sample = f(p_sample)
    state_hgrn = f(state_hgrn); cache_win_k = f(cache_win_k); cache_win_v = f(cache_win_v)
    state_ffn_conv = f(state_ffn_conv)

    def cols(g, nchunk):
        return np.ascontiguousarray(f(g).reshape(nchunk, 128).T)

    gcols = np.ascontiguousarray(np.stack([cols(norm_attn_g[0], 8), cols(norm_ffn_g[0], 8), cols(norm_ple_g[0], 8)], axis=1))
    gfin = np.ascontiguousarray(np.broadcast_to(f(norm_final_g)[None, :], (128, D)))
    lbl = np.ascontiguousarray(f(hgrn_lb_logits).reshape(2, 4, 128).transpose(2, 0, 1))
    gon = cols(hgrn_onorm_g[0], 4)
    convw = np.ascontiguousarray(f(conv_w)[0].reshape(3, NFF, 128).transpose(2, 1, 0))
    convb = cols(conv_b[0], NFF)
    shared = {
        "w_in": f(w_in)[0], "w_o": f(w_o)[0], "w_gate": f(w_gate)[0], "w_up": f(w_up)[0], "w_down": f(w_down)[0],
        "w_pg": f(w_ple_gate)[0], "w_pp": f(w_ple_proj)[0], "gcols": gcols, "gfin": gfin, "lbl": lbl, "gon": gon,
        "convw": convw, "convb": convb, "maskr": maskr, "causal": causal, "csp": csp, "css": css,
    }
    in_maps = []
    for c in range(NCORES):
        ps = slice(NSEQ_P * c, NSEQ_P * (c + 1))
        ss_ = slice(NSEQ_S * c, NSEQ_S * (c + 1))
        m = dict(shared)
        m["x_p"] = x_prompt[ps].reshape(NSEQ_P * SEQ, D)
        m["p_p"] = p_prompt[0, ps].reshape(NSEQ_P * SEQ, PLE)
        m["x_s"] = x_sample[ss_].reshape(NSEQ_S * TS, D)
        m["p_s"] = p_sample[0, ss_].reshape(NSEQ_S * TS, PLE)
        m["st_in"] = state_hgrn[0, ss_]
        m["ck_in"] = cache_win_k[0, ss_].reshape(NSEQ_S, WB, 512)
        m["cv_in"] = cache_win_v[0, ss_].reshape(NSEQ_S, WB, 512)
        m["conv_in"] = np.ascontiguousarray(state_ffn_conv[0, ss_].reshape(NSEQ_S, 2, NFF, 128).transpose(3, 2, 0, 1))
        in_maps.append(m)
    res = run_bass_kernel_spmd(nc, in_maps, core_ids=list(range(NCORES)))
    R = res.results
    cat = lambda k: np.concatenate([np.asarray(r[k]) for r in R], axis=0)
    y_p = cat("y_p").reshape(16, SEQ, D)
    y_s = cat("y_s").reshape(32, TS, D)
    st_p = cat("st_p")[None]
    ck_p = cat("ck_p").reshape(1, 16, SEQ, 8, 64)
    cv_p = cat("cv_p").reshape(1, 16, SEQ, 8, 64)
    conv_p = cat("conv_p").transpose(0, 3, 2, 1).reshape(1, 16, 2, DFF)
    st_s = cat("st_s")[None]
    ck_s = cat("ck_s").reshape(1, 32, WB, 8, 64)
    cv_s = cat("cv_s").reshape(1, 32, WB, 8, 64)
    conv_s = cat("conv_s").transpose(0, 3, 2, 1).reshape(1, 32, 2, DFF)
    outs = (y_p, y_s, st_p, ck_p, cv_p, conv_p, st_s, ck_s, cv_s, conv_s)
    return tuple(np.ascontiguousarray(o, dtype=np.float32) for o in outs)
```
